# Optimizing a Trainium2 kernel written in Bass

```python
import math
import jax, jax.numpy as jnp
from jax import lax
import numpy as np

D_MODEL = 1024
BATCH = 16
SEQ = 4096
DEPTH = 4

HEAD_DIM = 64
A_HEADS = 8
A_QK = A_HEADS * 2 * HEAD_DIM
A_V = A_HEADS * 2 * HEAD_DIM
B_HEADS = 12
B_W = B_HEADS * HEAD_DIM
C_CH = 768
CONV_W = 31
GATE_W = 3 * D_MODEL
D_FF = 2816
IN_SIZES = (A_QK, A_QK, A_V, B_W, B_W, B_W, 2 * C_CH, GATE_W)
IN_SPLITS = tuple(int(v) for v in np.cumsum(IN_SIZES)[:-1])
IN_W = int(sum(IN_SIZES))
Q_BLK = 128
DIL_BLK = 64
DILATED_PATTERNS = ((128, 1), (512, 4), (2048, 16))
ATTN_SCALE = HEAD_DIM ** -0.5
EPS = 1e-6

kernel_name = 'hybrid_diffattn_dilated_conformer_encoder'


def rms_norm(x, g):
    xf = x.astype(jnp.float32)
    y = xf * lax.rsqrt(jnp.mean(xf * xf, axis=-1, keepdims=True) + EPS)
    return (y * g.astype(jnp.float32)).astype(x.dtype)


def alibi_slopes(n):
    return jnp.asarray(2.0 ** (-8.0 * np.arange(1, n + 1) / n), dtype=jnp.float32)


def lambda_init(layer):
    return 0.8 - 0.6 * math.exp(-0.3 * layer)


def swiglu(h, w_up, w_down):
    a, b = jnp.split(h @ w_up, 2, axis=-1)
    return (jax.nn.silu(a) * b) @ w_down


def diff_attention(q, k, v, lam, slopes):
    bn, s_len, h, _, dh = q.shape
    nblk = s_len // Q_BLK
    qb = q.reshape(bn, nblk, Q_BLK, h, 2, dh).transpose(1, 0, 3, 4, 2, 5)
    kt = k.transpose(0, 2, 3, 1, 4)
    vt = v.transpose(0, 2, 1, 3)
    pos = jnp.arange(s_len)

    def one_block(args):
        qblk, start = args
        sc = jnp.einsum('bhiqd,bhikd->bhiqk', qblk, kt).astype(jnp.float32) * ATTN_SCALE
        tq = start + jnp.arange(Q_BLK)
        dist = jnp.abs(tq[:, None] - pos[None, :]).astype(jnp.float32)
        sc = sc - slopes[None, :, None, None, None] * dist
        p = jax.nn.softmax(sc, axis=-1)
        w = p[:, :, 0] - lam * p[:, :, 1]
        return jnp.einsum('bhqk,bhkd->bhqd', w.astype(v.dtype), vt)

    o = lax.map(one_block, (qb, jnp.arange(nblk) * Q_BLK))
    return o.transpose(1, 0, 3, 2, 4).reshape(bn, s_len, h, 2 * dh)


def dilated_pattern(q, k, v, slopes, dil, radius):
    bn, s_len, h, dh = q.shape
    u_len = s_len // dil

    def to_res(t):
        return t.reshape(bn, u_len, dil, h, dh).transpose(0, 2, 3, 1, 4)

    qr, kr, vr = to_res(q), to_res(k), to_res(v)
    nb = -(-u_len // DIL_BLK)
    up = nb * DIL_BLK
    pad = up - u_len
    qb = jnp.pad(qr, ((0, 0), (0, 0), (0, 0), (0, pad), (0, 0))).reshape(bn, dil, h, nb, DIL_BLK, dh)

    def windows(t):
        tp = jnp.pad(t, ((0, 0), (0, 0), (0, 0), (DIL_BLK, pad + DIL_BLK), (0, 0)))
        tb = tp.reshape(bn, dil, h, nb + 2, DIL_BLK, dh)
        return jnp.concatenate([tb[:, :, :, :-2], tb[:, :, :, 1:-1], tb[:, :, :, 2:]], axis=4)

    kw, vw = windows(kr), windows(vr)
    sc = jnp.einsum('brhnqd,brhnkd->brhnqk', qb, kw).astype(jnp.float32) * ATTN_SCALE
    uq = jnp.arange(nb)[:, None] * DIL_BLK + jnp.arange(DIL_BLK)[None, :]
    uk = jnp.arange(nb)[:, None] * DIL_BLK - DIL_BLK + jnp.arange(3 * DIL_BLK)[None, :]
    du = jnp.abs(uq[:, :, None] - uk[:, None, :])
    valid = (du <= radius) & (uk[:, None, :] >= 0) & (uk[:, None, :] < u_len)
    bias = -slopes[:, None, None, None] * (du * dil).astype(jnp.float32)
    sc = jnp.where(valid, sc + bias, -jnp.inf)
    m = jnp.max(sc, axis=-1, keepdims=True)
    p = jnp.exp(sc - m)
    den = jnp.sum(p, axis=-1)
    lse = m[..., 0] + jnp.log(den)
    o = jnp.einsum('brhnqk,brhnkd->brhnqd', p.astype(v.dtype), vw).astype(jnp.float32) / den[..., None]
    o = o.reshape(bn, dil, h, up, dh)[:, :, :, :u_len].transpose(0, 3, 1, 2, 4).reshape(bn, s_len, h, dh)
    lse = lse.reshape(bn, dil, h, up)[:, :, :, :u_len].transpose(0, 3, 1, 2).reshape(bn, s_len, h)
    return o, lse


def dilated_mixture(q, k, v, slopes):
    outs, lses = [], []
    for window, dil in DILATED_PATTERNS:
        o, lse = dilated_pattern(q, k, v, slopes, dil, window // 2 // dil)
        outs.append(o)
        lses.append(lse)
    wts = jax.nn.softmax(jnp.stack(lses, 0), axis=0)
    o = jnp.einsum('gbsh,gbshd->bshd', wts, jnp.stack(outs, 0))
    return o.astype(q.dtype)


def conv_module(u, dw_w, dw_b, norm_g):
    a, gate = jnp.split(u, 2, axis=-1)
    z = a * jax.nn.sigmoid(gate)
    z = lax.conv_general_dilated(
        z, dw_w[:, None, :].astype(z.dtype), window_strides=(1,),
        padding=[(CONV_W // 2, CONV_W // 2)],
        dimension_numbers=('NWC', 'WIO', 'NWC'),
        feature_group_count=C_CH) + dw_b
    return jax.nn.silu(rms_norm(z, norm_g))


def setup_inputs(seed: int = 0) -> dict:
    key = jax.random.key(seed)
    ks = iter(jax.random.split(key, 32))
    L = DEPTH

    def nrm(shape, scale):
        return jax.random.normal(next(ks), shape, jnp.float32) * scale

    def gain(shape):
        return 1.0 + nrm(shape, 0.02)

    return {
        'x': nrm((BATCH, SEQ, D_MODEL), 1.0),
        'ffn1_norm': gain((L, D_MODEL)),
        'ffn1_w_up': nrm((L, D_MODEL, 2 * D_FF), D_MODEL ** -0.5),
        'ffn1_w_down': nrm((L, D_FF, D_MODEL), D_FF ** -0.5),
        'mix_norm': gain((L, D_MODEL)),
        'w_in': nrm((L, D_MODEL, IN_W), D_MODEL ** -0.5),
        'b_in': nrm((L, IN_W), 0.02),
        'a_q_norm': gain((L, HEAD_DIM)),
        'a_k_norm': gain((L, HEAD_DIM)),
        'a_lambda': nrm((L, 4, HEAD_DIM), 0.1),
        'a_sub_norm': gain((L, 2 * HEAD_DIM)),
        'w_out_a': nrm((L, A_V, D_MODEL), A_V ** -0.5),
        'b_q_norm': gain((L, HEAD_DIM)),
        'b_k_norm': gain((L, HEAD_DIM)),
        'w_out_b': nrm((L, B_W, D_MODEL), B_W ** -0.5),
        'c_dw_w': nrm((L, CONV_W, C_CH), CONV_W ** -0.5),
        'c_dw_b': nrm((L, C_CH), 0.02),
        'c_norm': gain((L, C_CH)),
        'w_out_c': nrm((L, C_CH, D_MODEL), C_CH ** -0.5),
        'w_out': nrm((L, D_MODEL, D_MODEL), D_MODEL ** -0.5),
        'ffn2_norm': gain((L, D_MODEL)),
        'ffn2_w_up': nrm((L, D_MODEL, 2 * D_FF), D_MODEL ** -0.5),
        'ffn2_w_down': nrm((L, D_FF, D_MODEL), D_FF ** -0.5),
    }


def reference(x, ffn1_norm, ffn1_w_up, ffn1_w_down, mix_norm, w_in, b_in,
              a_q_norm, a_k_norm, a_lambda, a_sub_norm, w_out_a,
              b_q_norm, b_k_norm, w_out_b,
              c_dw_w, c_dw_b, c_norm, w_out_c, w_out,
              ffn2_norm, ffn2_w_up, ffn2_w_down):
    bn, s_len, _ = x.shape
    slopes_a = alibi_slopes(A_HEADS)
    slopes_b = alibi_slopes(B_HEADS)
    for l in range(DEPTH):
        x = x + 0.5 * swiglu(rms_norm(x, ffn1_norm[l]), ffn1_w_up[l], ffn1_w_down[l])

        h = rms_norm(x, mix_norm[l])
        proj = h @ w_in[l] + b_in[l]
        aq, ak, av, bq, bk, bv, cu, gl = jnp.split(proj, IN_SPLITS, axis=-1)

        aq = rms_norm(aq.reshape(bn, s_len, A_HEADS, 2, HEAD_DIM), a_q_norm[l])
        ak = rms_norm(ak.reshape(bn, s_len, A_HEADS, 2, HEAD_DIM), a_k_norm[l])
        av = av.reshape(bn, s_len, A_HEADS, 2 * HEAD_DIM)
        lam0 = lambda_init(l)
        lp = a_lambda[l].astype(jnp.float32)
        lam = jnp.exp(jnp.sum(lp[0] * lp[1])) - jnp.exp(jnp.sum(lp[2] * lp[3])) + lam0
        ya = diff_attention(aq, ak, av, lam, slopes_a)
        ya = rms_norm(ya, a_sub_norm[l]) * (1.0 - lam0)
        ya = ya.reshape(bn, s_len, A_V) @ w_out_a[l]

        bq = rms_norm(bq.reshape(bn, s_len, B_HEADS, HEAD_DIM), b_q_norm[l])
        bk = rms_norm(bk.reshape(bn, s_len, B_HEADS, HEAD_DIM), b_k_norm[l])
        bv = bv.reshape(bn, s_len, B_HEADS, HEAD_DIM)
        yb = dilated_mixture(bq, bk, bv, slopes_b).reshape(bn, s_len, B_W) @ w_out_b[l]

        yc = conv_module(cu, c_dw_w[l], c_dw_b[l], c_norm[l]) @ w_out_c[l]

        g = jax.nn.sigmoid(gl).reshape(bn, s_len, 3, D_MODEL)
        merged = g[:, :, 0] * ya + g[:, :, 1] * yb + g[:, :, 2] * yc
        x = x + merged @ w_out[l]

        x = x + 0.5 * swiglu(rms_norm(x, ffn2_norm[l]), ffn2_w_up[l], ffn2_w_down[l])
    return x
```

```python
import math
import os
from contextlib import ExitStack

import numpy as np
import concourse.bass as bass
import concourse.mybir as mybir
from concourse.bass_utils import run_bass_kernel_spmd

F32 = mybir.dt.float32
BF16 = mybir.dt.bfloat16
I32 = mybir.dt.int32
AF = mybir.ActivationFunctionType
ALU = mybir.AluOpType
AX = mybir.AxisListType

D = 1024
DEPTH = 4
HD = 64
A_HEADS = 8
B_HEADS = 12
C_CH = 768
CONV_W = 31
D_FF = 2816
OFF_AQ, OFF_AK, OFF_AV = 0, 1024, 2048
OFF_BQ, OFF_BK, OFF_BV = 3072, 3840, 4608
OFF_CU = 5376
OFF_G = 6912
IN_W = 9984
EPS = 1e-6
ATTN_SCALE = HD ** -0.5
PATTERNS = ((128, 1), (512, 4), (2048, 16))
SLOPES_A = [2.0 ** (-8.0 * i / A_HEADS) for i in range(1, A_HEADS + 1)]
SLOPES_B = [2.0 ** (-8.0 * i / B_HEADS) for i in range(1, B_HEADS + 1)]
NEG_BIG = -1.0e6


class Buf:
    __slots__ = ("name", "w", "r", "excl")

    def __init__(self, name="", excl=False):
        self.name = name
        self.w = {}
        self.r = {}
        self.excl = excl


class Eng:
    def __init__(self, fw, key, e, sem):
        self.fw, self.key, self.e, self.sem = fw, key, e, sem
        self.count = 0
        self.seen = {}

    def wait_ev(self, key, cnt):
        if self.seen.get(key, 0) >= cnt:
            return
        self.e.wait_ge(self.fw.sems[key], cnt)
        self.seen[key] = cnt


class FW:
    NDMA = 8

    def __init__(self, nc, stack):
        self.nc = nc
        self.sems = {}
        self.engs = {}
        for key, e in (("pe", nc.tensor), ("act", nc.scalar), ("dve", nc.vector),
                       ("pool", nc.gpsimd), ("sp", nc.sync)):
            sem = stack.enter_context(nc.semaphore("s_" + key))
            self.sems[key] = sem
            self.engs[key] = Eng(self, key, e, sem)
        self.dma_keys = []
        for i in range(self.NDMA):
            k = "dma%d" % i
            self.sems[k] = stack.enter_context(nc.semaphore("s_" + k))
            self.dma_keys.append(k)
        self.dma_cnt = {k: 0 for k in self.dma_keys}
        self.dma_rr = 0
        self.n_instr = 0
        self.bufs = []

    def buf(self, name="", excl=False):
        b = Buf(name, excl)
        self.bufs.append(b)
        return b

    def reset_state(self):
        for e in self.engs.values():
            e.count = 0
            e.seen = {}
        self.dma_cnt = {k: 0 for k in self.dma_keys}
        self.dma_rr = 0
        for b in self.bufs:
            b.w = {}
            b.r = {}

    def _deps(self, eng, reads, writes):
        for b in reads:
            for k, c in b.w.items():
                eng.wait_ev(k, c)
            if b.excl:
                for k, c in b.r.items():
                    if k != eng.key:
                        eng.wait_ev(k, c)
        for b in writes:
            for k, c in b.w.items():
                if k != eng.key:
                    eng.wait_ev(k, c)
            for k, c in b.r.items():
                if k != eng.key:
                    eng.wait_ev(k, c)

    def _record(self, key, cnt, reads, writes, partial):
        for b in reads:
            if b.r.get(key, 0) < cnt:
                b.r[key] = cnt
        for b in writes:
            if not partial:
                b.r = {}
                b.w = {}
            b.w[key] = cnt

    def op(self, ek, fn, reads=(), writes=(), partial=False):
        eng = self.engs[ek]
        self._deps(eng, reads, writes)
        ins = fn()
        eng.count += 1
        ins.then_inc(eng.sem, 1)
        self._record(ek, eng.count, reads, writes, partial)
        self.n_instr += 1

    def dma(self, out, in_, reads=(), writes=(), partial=False, **kw):
        sp = self.engs["sp"]
        self._deps(sp, reads, writes)
        k = self.dma_keys[self.dma_rr % self.NDMA]
        self.dma_rr += 1
        if self.dma_cnt[k] > 0:
            sp.wait_ev(k, self.dma_cnt[k])
        self.nc.sync.dma_start(out=out, in_=in_, **kw).then_inc(self.sems[k], 16)
        self.dma_cnt[k] += 16
        self._record(k, self.dma_cnt[k], reads, writes, partial)
        self.n_instr += 1

    def drain_dmas(self):
        sp = self.engs["sp"]
        for k in self.dma_keys:
            if self.dma_cnt[k] > 0:
                sp.wait_ev(k, self.dma_cnt[k])

    def sync_all(self):
        for e in self.engs.values():
            for f in self.engs.values():
                if f is not e and f.count > 0:
                    e.wait_ev(f.key, f.count)
            for k in self.dma_keys:
                if self.dma_cnt[k] > 0:
                    e.wait_ev(k, self.dma_cnt[k])

    def hard_barrier(self):
        self.drain_dmas()
        self.nc.all_engine_barrier()
        for s in list(self.sems.values()) + list(getattr(self, "extra_sems", [])):
            self.nc.gpsimd.sem_clear(s)
        self.nc.all_engine_barrier()
        self.reset_state()


class Ring:
    def __init__(self, items):
        self.items = list(items)
        self.i = 0

    def next(self):
        it = self.items[self.i % len(self.items)]
        self.i += 1
        return it


BIG_W = ("ffn1_w_up", "ffn1_w_down", "w_in", "w_out_a", "w_out_b", "w_out_c", "w_out", "ffn2_w_up", "ffn2_w_down")
W_SHAPES = (("ffn1_norm", (D,)), ("mix_norm", (D,)), ("b_in", (IN_W,)),
            ("a_q_norm", (HD,)), ("a_k_norm", (HD,)), ("a_lambda", (4, HD)),
            ("a_sub_norm", (2 * HD,)), ("b_q_norm", (HD,)),
            ("b_k_norm", (HD,)), ("c_dw_w", (CONV_W, C_CH)),
            ("c_dw_b", (C_CH,)), ("c_norm", (C_CH,)), ("ffn2_norm", (D,)),
            ("lam0", (2,)), ("aq_aug", (4, A_HEADS * 512)), ("ak_aug", (4, A_HEADS * 2 * 128)),
            ("ffn1_w_up", (D, 2 * D_FF)), ("ffn1_w_down", (D_FF, D)), ("w_in", (D, IN_W)),
            ("w_out_a", (D, D)), ("w_out_b", (768, D)), ("w_out_c", (C_CH, D)), ("w_out", (D, D)),
            ("ffn2_w_up", (D, 2 * D_FF)), ("ffn2_w_down", (D_FF, D)))
BLOB_COLS = 2048
PIECE_ROWS = 1024


def blob_layout():
    off = 0
    lay = {}
    for name, shp in W_SHAPES:
        n = int(np.prod(shp))
        lay[name] = (off, shp)
        off += (n + 63) // 64 * 64
    rows = (off + BLOB_COLS - 1) // BLOB_COLS
    rows = (rows + PIECE_ROWS - 1) // PIECE_ROWS * PIECE_ROWS
    return lay, rows


class Cfg:
    def __init__(self, S=4096, NSEQ=2, L=DEPTH, phases=None, a_heads=None, b_pairs=None, debug=False):
        self.S, self.NSEQ, self.L = S, NSEQ, L
        self.phases = phases or ("ffn1", "hT", "A", "B", "C", "merge", "ffn2")
        self.a_heads = list(range(A_HEADS)) if a_heads is None else a_heads
        self.b_pairs = list(range(B_HEADS // 2)) if b_pairs is None else b_pairs
        self.debug = debug


def build_program(cfg):
    S, NSEQ, L = cfg.S, cfg.NSEQ, cfg.L
    NT = S // 128
    NCH = S // 512
    nc = bass.Bass("TRN2", target_bir_lowering=False)

    def din(name, shape):
        return nc.dram_tensor(name, list(shape), F32, kind="ExternalInput").ap()

    x_in = din("x", [NSEQ, S, D])
    LAY, BROWS = blob_layout()
    NPIECE = BROWS // PIECE_ROWS
    wblob = din("wblob", [L, BROWS, BLOB_COLS])
    wcur = nc.dram_tensor("wcur", [PIECE_ROWS, BLOB_COLS], F32).ap()
    wcur16 = nc.dram_tensor("wcur16", [BROWS, BLOB_COLS], BF16).ap()
    xcur = nc.dram_tensor("xcur", [S, D], F32).ap()
    out = nc.dram_tensor("out", [NSEQ, S, D], F32, kind="ExternalOutput").ap()
    ysc_kind = "ExternalOutput" if cfg.debug else "Internal"
    ysc = nc.dram_tensor("ysc", [20, 128, S], BF16, kind=ysc_kind).ap()
    hdbg = nc.dram_tensor("hdbg", [8, 128, S], BF16, kind="ExternalOutput").ap() if cfg.debug else None

    with ExitStack() as st:
        fw = FW(nc, st)
        op, dma = fw.op, fw.dma
        s_wc = st.enter_context(nc.semaphore("s_wc"))
        s_xc = st.enter_context(nc.semaphore("s_xc"))
        fw.extra_sems = [s_wc, s_xc]

        uniq = [0]

        def sbt(stack, name, shape, dt):
            uniq[0] += 1
            return stack.enter_context(nc.sbuf_tensor("%s_%d" % (name, uniq[0]), list(shape), dt))

        identb = sbt(st, "identb", [128, 128], BF16)
        identf = sbt(st, "identf", [128, 128], F32)
        onesb = sbt(st, "onesb", [128, 128], BF16)
        vecT = sbt(st, "vecT", [128, 72], F32)
        small = sbt(st, "small", [128, 16], F32)
        B_identb, B_identf, B_onesb, B_vecT, B_small = (fw.buf(n) for n in ("identb", "identf", "onesb", "vecT", "small"))
        banks = []
        dbanks = []
        for i in range(4):
            t = st.enter_context(nc.psum_tensor("dbank%d" % i, [128, 1024], F32))
            dbanks.append(t)
            banks.append((t[:, 0:512], fw.buf("bank%d" % (2 * i), excl=True)))
            banks.append((t[:, 512:1024], fw.buf("bank%d" % (2 * i + 1), excl=True)))

        def bank_bf(i):
            return banks[i][0][:, :].bitcast(BF16)

        op("pool", lambda: nc.gpsimd.memset(identf[:], 0.0), writes=[B_identf])
        op("pool", lambda: nc.gpsimd.affine_select(out=identf[:], in_=identf[:], pattern=[[-1, 128]], compare_op=ALU.not_equal,
                                                   fill=1.0, base=0, channel_multiplier=1), reads=[B_identf], writes=[B_identf])
        op("pool", lambda: nc.gpsimd.memset(onesb[:], 1.0), writes=[B_onesb])
        op("dve", lambda: nc.vector.tensor_copy(out=identb[:], in_=identf[:]), reads=[B_identf], writes=[B_identb])

        for s_ in range(NSEQ):
            for c in range(NCH):
                dma(out[s_, c * 512:(c + 1) * 512, :], x_in[s_, c * 512:(c + 1) * 512, :])
        fw.hard_barrier()

        def body(l, seqs):
            cur = [seqs[0]]
            def wl(name, *idx):
                off, shp = LAY[name]
                n = int(np.prod(shp))
                src_ = wcur16 if name in BIG_W else wcur
                flat = src_.rearrange("r c -> (r c)")[off:off + n]
                if len(shp) == 2:
                    flat = flat.rearrange("(a b) -> a b", b=shp[1])
                return flat[idx] if idx else flat

            def xchunk_s(sq, c, n=512):
                return out[sq, c * n:(c + 1) * n, :].rearrange("(t p) f -> p t f", p=128)

            def xchunk(c, n=512):
                return xchunk_s(cur[0], c, n)

            B_xall = [fw.buf("xdram%d" % c) for c in range(NSEQ * NCH)]

            class _BX:
                def __getitem__(self, c):
                    return B_xall[cur[0] * NCH + c]
            B_x = _BX()
            B_ysc = fw.buf("ysc")

            with ExitStack() as ph:
                stgv = sbt(ph, "stgv", [72, 128], F32)
                B_stgv = fw.buf("stgv")
                rows = 0
                for name, n in (("ffn1_norm", 8), ("mix_norm", 8), ("ffn2_norm", 8)):
                    dma(stgv[rows:rows + n, :], wl(name).rearrange("(k p) -> k p", p=128), writes=[B_stgv], partial=True)
                    rows += n
                dma(stgv[24:36, :], wl("b_in", slice(OFF_CU, OFF_CU + 1536)).rearrange("(k p) -> k p", p=128), writes=[B_stgv], partial=True)
                dma(stgv[36:60, :], wl("b_in", slice(OFF_G, OFF_G + 3072)).rearrange("(k p) -> k p", p=128), writes=[B_stgv], partial=True)
                dma(stgv[60:66, :], wl("c_dw_b").rearrange("(k p) -> k p", p=128), writes=[B_stgv], partial=True)
                dma(stgv[66:72, :], wl("c_norm").rearrange("(k p) -> k p", p=128), writes=[B_stgv], partial=True)
                pt, B_pt = banks[0]
                op("pe", lambda: nc.tensor.transpose(out=pt[:, 0:72], in_=stgv[0:72, :], identity=identf[0:72, 0:72]),
                   reads=[B_stgv, B_identf], writes=[B_pt])
                op("act", lambda: nc.scalar.copy(out=vecT[:, :], in_=pt[:, 0:72]), reads=[B_pt], writes=[B_vecT])
                B_sm_in = fw.buf("sm_in")
                smi = sbt(ph, "smi", [128, 8], F32)
                for col, name in ((0, "a_q_norm"), (1, "a_k_norm"), (2, "b_q_norm"), (3, "b_k_norm")):
                    src = wl(name).rearrange("(d i) -> d i", i=1)
                    dma(smi[0:64, col:col + 1], src, writes=[B_sm_in], partial=True)
                    dma(smi[64:128, col:col + 1], src, writes=[B_sm_in], partial=True)
                dma(smi[:, 4:5], wl("a_sub_norm").rearrange("(d i) -> d i", i=1), writes=[B_sm_in], partial=True)
                dma(smi[:, 5:7], wl("lam0").rearrange("(o c) -> o c", o=1).partition_broadcast(128), writes=[B_sm_in], partial=True)
                lamt = sbt(ph, "lamt", [128, 4, HD], F32)
                B_lamt = fw.buf("lamt")
                dma(lamt[:, :, :].rearrange("p a d -> p (a d)"),
                    wl("a_lambda").rearrange("a d -> (a d)").rearrange("(o n) -> o n", o=1).partition_broadcast(128), writes=[B_lamt])
                lprod = sbt(ph, "lprod", [128, 2, HD], F32)
                lsum = sbt(ph, "lsum", [128, 4], F32)
                B_lp, B_ls = fw.buf("lprod"), fw.buf("lsum")
                op("dve", lambda: nc.vector.tensor_tensor(out=lprod[:, :, :], in0=lamt[:, 0:4:2, :], in1=lamt[:, 1:4:2, :], op=ALU.mult),
                   reads=[B_lamt], writes=[B_lp])
                op("dve", lambda: nc.vector.tensor_reduce(out=lsum[:, 0:2], in_=lprod[:, :, :], axis=AX.X, op=ALU.add),
                   reads=[B_lp], writes=[B_ls])
                op("act", lambda: nc.scalar.activation(out=lsum[:, 2:4], in_=lsum[:, 0:2], func=AF.Exp), reads=[B_ls], writes=[B_ls], partial=True)
                op("dve", lambda: nc.vector.tensor_tensor(out=small[:, 5:6], in0=lsum[:, 2:3], in1=lsum[:, 3:4], op=ALU.subtract),
                   reads=[B_ls], writes=[B_small], partial=True)
                op("dve", lambda: nc.vector.tensor_tensor(out=small[:, 5:6], in0=small[:, 5:6], in1=smi[:, 5:6], op=ALU.add),
                   reads=[B_small, B_sm_in], writes=[B_small], partial=True)
                op("dve", lambda: nc.vector.tensor_scalar(out=small[:, 6:7], in0=small[:, 5:6], scalar1=-1.0, scalar2=None, op0=ALU.mult),
                   reads=[B_small], writes=[B_small], partial=True)
                op("dve", lambda: nc.vector.tensor_scalar(out=small[:, 0:1], in0=smi[:, 0:1], scalar1=ATTN_SCALE, scalar2=None, op0=ALU.mult),
                   reads=[B_sm_in], writes=[B_small], partial=True)
                op("dve", lambda: nc.vector.tensor_copy(out=small[:, 1:2], in_=smi[:, 1:2]), reads=[B_sm_in], writes=[B_small], partial=True)
                op("dve", lambda: nc.vector.tensor_scalar(out=small[:, 2:3], in0=smi[:, 2:3], scalar1=ATTN_SCALE, scalar2=None, op0=ALU.mult),
                   reads=[B_sm_in], writes=[B_small], partial=True)
                op("dve", lambda: nc.vector.tensor_copy(out=small[:, 3:4], in_=smi[:, 3:4]), reads=[B_sm_in], writes=[B_small], partial=True)
                op("dve", lambda: nc.vector.tensor_tensor(out=small[:, 4:5], in0=smi[:, 4:5], in1=smi[:, 6:7], op=ALU.mult),
                   reads=[B_sm_in], writes=[B_small], partial=True)
                fw.sync_all()

            G_FFN1, G_MIX, G_FFN2, V_BC, V_BG, V_DWB, V_CN = 0, 8, 16, 24, 36, 60, 66

            def norm_transpose(xc, B_xc, ntile, xn, B_xn, stat, B_stat, gcol, dst_fn, B_dst, tbanks, tcols, part=0):
                for t in range(ntile if part in (0, 1) else 0):
                    op("act", lambda t=t: nc.scalar.activation(out=xn[:, t, :], in_=xc[:, t, :], func=AF.Square,
                                                               accum_out=stat[:, t:t + 1]),
                       reads=[B_xc], writes=[B_xn, B_stat], partial=True)
                if part in (0, 1):
                    op("dve", lambda: nc.vector.tensor_scalar(out=stat[:, 8:8 + ntile], in0=stat[:, 0:ntile], scalar1=1.0 / D, scalar2=EPS,
                                                              op0=ALU.mult, op1=ALU.add), reads=[B_stat], writes=[B_stat], partial=True)
                    op("act", lambda: nc.scalar.activation(out=stat[:, 16:16 + ntile], in_=stat[:, 8:8 + ntile], func=AF.Sqrt),
                       reads=[B_stat], writes=[B_stat], partial=True)
                    op("dve", lambda: nc.vector.reciprocal(out=stat[:, 24:24 + ntile], in_=stat[:, 16:16 + ntile]),
                       reads=[B_stat], writes=[B_stat], partial=True)
                for t in range(ntile if part in (0, 1) else 0):
                    op("dve", lambda t=t: nc.vector.tensor_scalar(out=xn[:, t, :], in0=xc[:, t, :], scalar1=stat[:, 24 + t:25 + t],
                                                                  scalar2=None, op0=ALU.mult),
                       reads=[B_xc, B_stat], writes=[B_xn], partial=True)
                for fc in range(8 if part in (0, 2) else 0):
                    bi = tbanks[fc % len(tbanks)]
                    ptb, B_ptb = bank_bf(bi), banks[bi][1]

                    def tr(fc=fc, ptb=ptb):
                        ins = None
                        for t in range(ntile):
                            ins = nc.tensor.transpose(out=ptb[:, t * 128:(t + 1) * 128], in_=xn[:, t, fc * 128:(fc + 1) * 128],
                                                      identity=identb[:, :])
                        return ins
                    op("pe", tr, reads=[B_xn, B_identb], writes=[B_ptb])
                    eng = "act" if fc % 2 == 0 else "dve"
                    if eng == "act":
                        op("act", lambda fc=fc, ptb=ptb: nc.scalar.activation(out=dst_fn(fc), in_=ptb[:, 0:ntile * 128], func=AF.Copy,
                                                                              scale=vecT[:, gcol + fc:gcol + fc + 1]),
                           reads=[B_ptb, B_vecT], writes=[B_dst], partial=True)
                    else:
                        op("dve", lambda fc=fc, ptb=ptb: nc.vector.tensor_scalar(out=dst_fn(fc), in0=ptb[:, 0:ntile * 128],
                                                                                 scalar1=vecT[:, gcol + fc:gcol + fc + 1], scalar2=None, op0=ALU.mult),
                           reads=[B_ptb, B_vecT], writes=[B_dst], partial=True)

            def load_weight(dst, B_dst, src_fn, nk, ncols, stg_ring=None, max_cols=None):
                for k in range(nk):
                    dma(dst[:, k, 0:ncols], src_fn(k, 0, ncols), writes=[B_dst], partial=True)

            def phase_ffn(which):
                gcol = G_FFN1 if which == 0 else G_FFN2
                wun, wdn_n = ("ffn1_w_up", "ffn1_w_down") if which == 0 else ("ffn2_w_up", "ffn2_w_down")
                with ExitStack() as ph:
                    wup = sbt(ph, "wup", [128, 8, 2 * D_FF], BF16)
                    wdn = sbt(ph, "wdn", [128, 22, D], BF16)
                    B_wup, B_wdn = fw.buf("wup"), fw.buf("wdn")
                    stgs = None
                    xc = sbt(ph, "fxc", [128, 4, D], F32)
                    xc2 = sbt(ph, "fxc2", [128, 4, D], F32)
                    B_xc2 = fw.buf("fxc2")
                    xn = sbt(ph, "fxn", [128, 4, D], BF16)
                    hTc = sbt(ph, "fhT", [128, 8, 512], BF16)
                    uT = [sbt(ph, "fuT%d" % i, [128, 11, 512], BF16) for i in range(2)]
                    sa = [sbt(ph, "fsa%d" % i, [128, 512], F32) for i in range(2)]
                    stat = sbt(ph, "fstat", [128, 32], F32)
                    B_xc, B_xn, B_hTc, B_stat = fw.buf("fxc"), fw.buf("fxn"), fw.buf("fhT"), fw.buf("fstat")
                    B_uT = [fw.buf("fuT0"), fw.buf("fuT1")]
                    B_sa = [fw.buf("fsa0"), fw.buf("fsa1")]
                    load_weight(wup, B_wup, lambda k, c0, c1: wl(wun, slice(k * 128, (k + 1) * 128), slice(c0, c1)), 8, 2 * D_FF, stgs, 1408)
                    load_weight(wdn, B_wdn, lambda k, c0, c1: wl(wdn_n, slice(k * 128, (k + 1) * 128), slice(c0, c1)), 22, D, stgs, 1408)
                    abank = Ring([2, 3])
                    bbank = Ring([4, 5])
                    ybank = Ring([6, 7])
                    sai = 0
                    NG = len(seqs) * NCH

                    def xg(g):
                        return xchunk_s(seqs[g // NCH], g % NCH)

                    def Bx(g):
                        return B_xall[seqs[g // NCH] * NCH + g % NCH]
                    xcs_ = [xc, xc2]
                    B_xcs_ = [B_xc, B_xc2]

                    def load_x(c):
                        for t_ in range(4):
                            dma(xcs_[c % 2][:, t_, :], xg(c)[:, t_, :], reads=[Bx(c)], writes=[B_xcs_[c % 2]], partial=(t_ > 0))

                    def nt(c, part):
                        norm_transpose(xcs_[c % 2], B_xcs_[c % 2], 4, xn, B_xn, stat, B_stat, gcol, lambda fc: hTc[:, fc, :], B_hTc, [0, 1], 512, part=part)
                    load_x(0)
                    nt(0, 0)
                    for c in range(NG):
                        xc_, B_xc_ = xcs_[c % 2], B_xcs_[c % 2]
                        if c + 1 < NG:
                            load_x(c + 1)
                        for half in range(2):
                            if half == 1 and c + 1 < NG:
                                nt(c + 1, 1)
                            for jj in range(11):
                                j = half * 11 + jj
                                ai, bi = abank.next(), bbank.next()
                                pa, B_pa = banks[ai]
                                pb, B_pb = banks[bi]

                                def mm_up(pa=pa, pb=pb, j=j):
                                    ins = None
                                    for k in range(8):
                                        nc.tensor.matmul(pa[:, :], lhsT=wup[:, k, j * 128:(j + 1) * 128], rhs=hTc[:, k, :], start=(k == 0), stop=(k == 7))
                                    for k in range(8):
                                        ins = nc.tensor.matmul(pb[:, :], lhsT=wup[:, k, D_FF + j * 128:D_FF + (j + 1) * 128], rhs=hTc[:, k, :],
                                                               start=(k == 0), stop=(k == 7))
                                    return ins
                                op("pe", mm_up, reads=[B_wup, B_hTc], writes=[B_pa, B_pb])
                                sat, B_sat = sa[sai % 2], B_sa[sai % 2]
                                sai += 1
                                op("act", lambda pa=pa, sat=sat: nc.scalar.activation(out=sat[:, :], in_=pa[:, :], func=AF.Silu),
                                   reads=[B_pa], writes=[B_sat])
                                op("dve", lambda pb=pb, sat=sat, half=half, jj=jj: nc.vector.tensor_tensor(out=uT[half][:, jj, :], in0=sat[:, :], in1=pb[:, :], op=ALU.mult),
                                   reads=[B_sat, B_pb], writes=[B_uT[half]], partial=True)
                            if half == 1 and c + 1 < NG:
                                nt(c + 1, 2)
                            for t in range(4):
                                for oh in range(2):
                                    yi = ybank.next()
                                    py, B_py = banks[yi]

                                    def mm_dn(py=py, t=t, oh=oh, half=half):
                                        ins = None
                                        for jj in range(11):
                                            ins = nc.tensor.matmul(py[:, :], lhsT=uT[half][:, jj, t * 128:(t + 1) * 128],
                                                                   rhs=wdn[:, half * 11 + jj, oh * 512:(oh + 1) * 512], start=(jj == 0), stop=(jj == 10))
                                        return ins
                                    op("pe", mm_dn, reads=[B_uT[half], B_wdn], writes=[B_py])
                                    op("dve", lambda py=py, t=t, oh=oh: nc.vector.scalar_tensor_tensor(
                                        out=xc_[:, t, oh * 512:(oh + 1) * 512], in0=py[:, :], scalar=0.5, in1=xc_[:, t, oh * 512:(oh + 1) * 512],
                                        op0=ALU.mult, op1=ALU.add), reads=[B_py, B_xc_], writes=[B_xc_], partial=True)
                        for t_ in range(4):
                            dma(xg(c)[:, t_, :], xc_[:, t_, :], reads=[B_xc_], writes=[Bx(c)], partial=(t_ > 0))
                    fw.sync_all()


            def phase_hT():
                with ExitStack() as ph:
                    xcs = [sbt(ph, "hxc%d" % i, [128, 4, D], F32) for i in range(2)]
                    B_xcs = [fw.buf("hxc0"), fw.buf("hxc1")]
                    xn = sbt(ph, "hxn", [128, 4, D], BF16)
                    stat = sbt(ph, "hstat", [128, 32], F32)
                    B_xn, B_stat = fw.buf("hxn"), fw.buf("hstat")
                    for c in range(NCH):
                        xc, B_xc = xcs[c % 2], B_xcs[c % 2]
                        dma(xc[:, :, :], xchunk(c), reads=[B_x[c]], writes=[B_xc])
                        norm_transpose(xc, B_xc, 4, xn, B_xn, stat, B_stat, G_MIX,
                                       lambda fc, c=c: hTf[:, fc, c * 512:(c + 1) * 512], B_hTf, [0, 1], 512)
                    if cfg.debug:
                        for fc in range(8):
                            dma(hdbg[fc, :, :], hTf[:, fc, :], reads=[B_hTf])
                    fw.sync_all()

            def alloc_qkv(ph, need_v=True):
                o = {}
                o["wq"] = [sbt(ph, "wqkv%d" % i, [128, 8, 384], BF16) for i in range(2)]
                o["bt"] = [sbt(ph, "bt%d" % i, [128, 384], F32) for i in range(2)]
                o["B_wq"] = [fw.buf("wq0"), fw.buf("wq1")]
                o["B_bt"] = [fw.buf("bt0"), fw.buf("bt1")]
                o["QT"] = sbt(ph, "QT", [128, S], BF16)
                o["KT"] = sbt(ph, "KT", [128, S], BF16)
                o["V"] = sbt(ph, "V", [128, NT, 128], BF16) if need_v else None
                for k in ("QT", "KT", "V"):
                    o["B_" + k] = fw.buf(k)
                NR = 3
                o["NR"] = NR
                o["qkv"] = [sbt(ph, "qkv%d" % i, [128, 384], F32) for i in range(NR)]
                o["sq"] = [sbt(ph, "sqt%d" % i, [128, 256], F32) for i in range(2)]
                o["qn"] = [sbt(ph, "qn%d" % i, [128, 256], BF16) for i in range(NR)]
                o["st"] = [sbt(ph, "qst%d" % i, [128, 16], F32) for i in range(NR)]
                o["B_qkv"] = [fw.buf("qkv%d" % i) for i in range(NR)]
                o["B_sq"] = [fw.buf("sq0"), fw.buf("sq1")]
                o["B_qn"] = [fw.buf("qn%d" % i) for i in range(NR)]
                o["B_st"] = [fw.buf("qst%d" % i) for i in range(NR)]
                return o

            def load_qkv_w(o, slot, cq, ck, cv):
                wq, bt = o["wq"][slot], o["bt"][slot]
                for i, c0 in enumerate((cq, ck, cv)):
                    dma(wq[:, :, i * 128:(i + 1) * 128], wl("w_in", slice(None), slice(c0, c0 + 128)).rearrange("(k p) c -> p k c", p=128),
                        writes=[o["B_wq"][slot]], partial=True)
                    dma(bt[:, i * 128:(i + 1) * 128],
                        wl("b_in", slice(c0, c0 + 128)).rearrange("(o n) -> o n", o=1).partition_broadcast(128),
                        writes=[o["B_bt"][slot]], partial=True)

            def qkv_project(o, slot, gqc, gkc, vdst=None):
                wq, bt, QT, KT, V = o["wq"][slot], o["bt"][slot], o["QT"], o["KT"], o["V"]
                B_wq, B_bt = o["B_wq"][slot], o["B_bt"][slot]
                NR = o["NR"]
                ptr = Ring([2, 3])
                ptbs = {}

                def stM(t):
                    pp, B_pp = banks[t % 2]

                    def mm():
                        ins = None
                        for k in range(8):
                            ins = nc.tensor.matmul(pp[:, 0:384], lhsT=hTf[:, k, t * 128:(t + 1) * 128], rhs=wq[:, k, :], start=(k == 0), stop=(k == 7))
                        return ins
                    op("pe", mm, reads=[B_hTf, B_wq], writes=[B_pp])

                def stA(t):
                    pp, B_pp = banks[t % 2]
                    qkv, B_qkv = o["qkv"][t % NR], o["B_qkv"][t % NR]
                    stt_, B_st = o["st"][t % NR], o["B_st"][t % NR]
                    sq, B_sq = o["sq"][t % 2], o["B_sq"][t % 2]
                    op("dve", lambda: nc.vector.tensor_tensor(out=qkv[:, :], in0=pp[:, 0:384], in1=bt[:, :], op=ALU.add), reads=[B_pp, B_bt], writes=[B_qkv])
                    for g_ in range(4):
                        op("act", lambda g_=g_: nc.scalar.activation(out=sq[:, g_ * 64:(g_ + 1) * 64], in_=qkv[:, g_ * 64:(g_ + 1) * 64], func=AF.Square,
                                                                     accum_out=stt_[:, g_:g_ + 1]),
                           reads=[B_qkv], writes=[B_sq, B_st], partial=True)
                    op("dve", lambda: nc.vector.tensor_scalar(out=stt_[:, 4:8], in0=stt_[:, 0:4], scalar1=1.0 / HD, scalar2=EPS, op0=ALU.mult, op1=ALU.add),
                       reads=[B_st], writes=[B_st], partial=True)
                    op("act", lambda: nc.scalar.activation(out=stt_[:, 8:12], in_=stt_[:, 4:8], func=AF.Sqrt), reads=[B_st], writes=[B_st], partial=True)
                    if vdst is None:
                        op("act", lambda: nc.scalar.copy(out=V[:, t, :], in_=qkv[:, 256:384]), reads=[B_qkv], writes=[o["B_V"]], partial=True)
                    else:
                        vt_, B_vt_ = vdst
                        op("act", lambda: nc.scalar.copy(out=vt_[:, t, :, 0:64], in_=qkv[:, 256:384].rearrange("p (e d) -> p e d", e=2)),
                           reads=[B_qkv], writes=[B_vt_], partial=True)

                def stB(t):
                    qkv, B_qkv = o["qkv"][t % NR], o["B_qkv"][t % NR]
                    qn, B_qn = o["qn"][t % NR], o["B_qn"][t % NR]
                    stt_, B_st = o["st"][t % NR], o["B_st"][t % NR]
                    op("dve", lambda: nc.vector.reciprocal(out=stt_[:, 12:16], in_=stt_[:, 8:12]), reads=[B_st], writes=[B_st], partial=True)
                    op("dve", lambda: nc.vector.tensor_tensor(
                        out=qn[:, :].rearrange("p (g d) -> p g d", g=4), in0=qkv[:, 0:256].rearrange("p (g d) -> p g d", g=4),
                        in1=stt_[:, 12:16].rearrange("p (g o) -> p g o", o=1).to_broadcast([128, 4, HD]), op=ALU.mult),
                       reads=[B_qkv, B_st], writes=[B_qn])

                def stT(t):
                    qn, B_qn = o["qn"][t % NR], o["B_qn"][t % NR]
                    if t % 4 == 0:
                        ti = ptr.next()
                        ptbs[t // 4] = (bank_bf(ti), banks[ti][1])
                    ptb, B_ptb = ptbs[t // 4]

                    def tr():
                        nc.tensor.transpose(out=ptb[:, (t % 4) * 128:(t % 4 + 1) * 128], in_=qn[:, 0:128], identity=identb[:, :])
                        return nc.tensor.transpose(out=ptb[:, 512 + (t % 4) * 128:512 + (t % 4 + 1) * 128], in_=qn[:, 128:256], identity=identb[:, :])
                    op("pe", tr, reads=[B_qn, B_identb], writes=[B_ptb], partial=True)
                    if t % 4 == 3:
                        t0 = t - 3
                        op("act", lambda: nc.scalar.activation(out=QT[:, t0 * 128:(t0 + 4) * 128], in_=ptb[:, 0:512], func=AF.Copy, scale=small[:, gqc:gqc + 1]),
                           reads=[B_ptb, B_small], writes=[o["B_QT"]], partial=True)
                        op("dve", lambda: nc.vector.tensor_scalar(out=KT[:, t0 * 128:(t0 + 4) * 128], in0=ptb[:, 512:1024],
                                                                  scalar1=small[:, gkc:gkc + 1], scalar2=None, op0=ALU.mult),
                           reads=[B_ptb, B_small], writes=[o["B_KT"]], partial=True)
                for s_ in range(NT + 3):
                    if s_ < NT:
                        stM(s_)
                    if 0 <= s_ - 1 < NT:
                        stA(s_ - 1)
                    if 0 <= s_ - 2 < NT:
                        stB(s_ - 2)
                    if 0 <= s_ - 3 < NT:
                        stT(s_ - 3)

            def phase_A():
                with ExitStack() as ph:
                    o = alloc_qkv(ph)
                    QT, KT, V = o["QT"], o["KT"], o["V"]
                    ib = sbt(ph, "ibias", [128, 4, 512], I32)
                    Bneg = sbt(ph, "Bneg", [128, 512], F32)
                    Bdiag = sbt(ph, "Bdiag", [128, 4, 512], F32)
                    B_ib, B_Bneg, B_Bdiag = fw.buf("ib"), fw.buf("Bneg"), fw.buf("Bdiag")
                    op("pool", lambda: nc.gpsimd.iota(ib[:, 0, :], pattern=[[-1, 512]], base=0, channel_multiplier=1), writes=[B_ib])
                    op("dve", lambda: nc.vector.tensor_copy(out=Bneg[:, :], in_=ib[:, 0, :]), reads=[B_ib], writes=[B_Bneg])
                    op("pool", lambda: nc.gpsimd.iota(ib[:, :, :], pattern=[[-128, 4], [1, 512]], base=0, channel_multiplier=-1), reads=[], writes=[B_ib])
                    op("dve", lambda: nc.vector.tensor_copy(out=Bdiag[:, :, :], in_=ib[:, :, :]), reads=[B_ib], writes=[B_Bdiag])
                    for tt_ in range(4):
                        op("dve", lambda tt_=tt_: nc.vector.scalar_tensor_tensor(out=Bdiag[:, tt_, :], in0=Bdiag[:, tt_, :], scalar=-1.0, in1=Bdiag[:, tt_, :], op0=ALU.mult, op1=ALU.min),
                           reads=[B_Bdiag], writes=[B_Bdiag], partial=True)
                    QA = KA = None
                    B_QA, B_KA = fw.buf("QAaug"), fw.buf("KAaug")
                    tts = [sbt(ph, "att%d" % i, [128, 2, 512], F32) for i in range(2)]
                    B_tts = [fw.buf("att0"), fw.buf("att1")]
                    NPT = 4
                    PTs = [sbt(ph, "aPT%d" % i, [128, 2, 512], BF16) for i in range(NPT)]
                    B_PTs = [fw.buf("aPT%d" % i) for i in range(NPT)]
                    f = [sbt(ph, "af%d" % i, [128, 512], F32) for i in range(4)]
                    B_f = [fw.buf("af%d" % i) for i in range(4)]
                    sqb = sbt(ph, "asqb", [128, 512], BF16)
                    B_sqb = fw.buf("asqb")
                    yaTs = [sbt(ph, "ayaT%d" % i, [128, 512], BF16) for i in range(2)]
                    B_yaTs = [fw.buf("ayaT0"), fw.buf("ayaT1")]
                    ycount = 0
                    if cfg.a_heads:
                        h0_ = cfg.a_heads[0]
                        load_qkv_w(o, 0, OFF_AQ + h0_ * 128, OFF_AK + h0_ * 128, OFF_AV + h0_ * 128)
                    for hi_, h in enumerate(cfg.a_heads):
                        if hi_ + 1 < len(cfg.a_heads):
                            hn_ = cfg.a_heads[hi_ + 1]
                            load_qkv_w(o, (hi_ + 1) % 2, OFF_AQ + hn_ * 128, OFF_AK + hn_ * 128, OFF_AV + hn_ * 128)
                        qkv_project(o, hi_ % 2, 0, 1)
                        m = SLOPES_A[h]
                        SKIP_T = 200.0
                        kept = {}
                        for qc in range(NCH):
                            kl = []
                            for kt in range(NT):
                                q0_, k0_ = qc * 512, kt * 128
                                dmin = max(q0_ - (k0_ + 127), k0_ - (q0_ + 511), 0)
                                if m * dmin <= SKIP_T:
                                    kl.append(kt)
                            kept[qc] = kl
                        steps = [(qc, kt) for qc in range(NCH) for kt in kept[qc]]
                        spair = Ring([0, 1])
                        pend = {}

                        USE_AUG = False

                        def side_of(qc, kt):
                            q0, k0 = qc * 512, kt * 128
                            if k0 + 128 <= q0:
                                return 0
                            if k0 >= q0 + 512:
                                return 1
                            return -1

                        def emit_S(idx):
                            qc, kt = steps[idx]
                            di = spair.next()
                            pend[idx] = di
                            sd = side_of(qc, kt) if USE_AUG else -1

                            def mmS(di=di, qc=qc, kt=kt):
                                ins = None
                                for g in range(2):
                                    ins = nc.tensor.matmul(dbanks[di][:, g * 512:(g + 1) * 512], lhsT=KT[g * 64:(g + 1) * 64, kt * 128:(kt + 1) * 128],
                                                           rhs=QT[g * 64:(g + 1) * 64, qc * 512:(qc + 1) * 512], start=True, stop=(sd < 0))
                                    if sd >= 0:
                                        ins = nc.tensor.matmul(dbanks[di][:, g * 512:(g + 1) * 512], lhsT=KA[:, h, sd, :], rhs=QA[:, h, :], start=False, stop=True)
                                return ins
                            op("pe", mmS, reads=[o["B_QT"], o["B_KT"], B_QA, B_KA], writes=[banks[2 * di][1], banks[2 * di + 1][1]])
                        emit_S(0)
                        backs = []
                        LAG = 1
                        ycount_box = [ycount]
                        for idx, (qc, kt) in enumerate(steps):
                            if idx + 1 < len(steps):
                                emit_S(idx + 1)
                            di = pend.pop(idx)
                            q0, k0 = qc * 512, kt * 128
                            tt, B_tt = tts[idx % 2], B_tts[idx % 2]
                            PT, B_PT = PTs[idx % NPT], B_PTs[idx % NPT]
                            sd = side_of(qc, kt)
                            if sd < 0:
                                tab, B_tab = Bdiag[:, (k0 - q0) // 128, :], B_Bdiag
                                op("dve", lambda tab=tab, di=di, tt=tt: nc.vector.scalar_tensor_tensor(
                                    out=tt[:, :, :], in0=tab.rearrange("p (o n) -> p o n", o=1).to_broadcast([128, 2, 512]), scalar=float(m),
                                    in1=dbanks[di][:, :].rearrange("p (g n) -> p g n", g=2), op0=ALU.mult, op1=ALU.add),
                                   reads=[B_tab, banks[2 * di][1], banks[2 * di + 1][1]], writes=[B_tt])
                                op("act", lambda tt=tt, PT=PT: nc.scalar.activation(out=PT[:, :, :], in_=tt[:, :, :], func=AF.Exp),
                                   reads=[B_tt], writes=[B_PT])
                            elif not USE_AUG:
                                cst = -m * (q0 - k0) if sd == 0 else -m * (k0 - q0)
                                sc = m if sd == 0 else -m
                                op("dve", lambda di=di, tt=tt, sc=sc: nc.vector.scalar_tensor_tensor(
                                    out=tt[:, :, :], in0=Bneg[:, :].rearrange("p (o n) -> p o n", o=1).to_broadcast([128, 2, 512]), scalar=float(sc),
                                    in1=dbanks[di][:, :].rearrange("p (g n) -> p g n", g=2), op0=ALU.mult, op1=ALU.add),
                                   reads=[B_Bneg, banks[2 * di][1], banks[2 * di + 1][1]], writes=[B_tt])
                                op("act", lambda tt=tt, PT=PT, cst=cst: nc.scalar.activation(out=PT[:, :, :], in_=tt[:, :, :], func=AF.Exp, bias=float(cst)),
                                   reads=[B_tt], writes=[B_PT])
                            else:
                                cst = -m * (q0 - k0) if sd == 0 else -m * (k0 - q0)
                                op("act", lambda di=di, PT=PT, cst=cst: nc.scalar.activation(
                                    out=PT[:, :, :], in_=dbanks[di][:, :].rearrange("p (g n) -> p g n", g=2), func=AF.Exp, bias=float(cst)),
                                   reads=[banks[2 * di][1], banks[2 * di + 1][1]], writes=[B_PT])

                            def back(idx=idx, qc=qc, kt=kt, q0=q0, PT=PT, B_PT=B_PT):
                                def mmPV(kt=kt, PT=PT):
                                    ins = None
                                    for g in range(2):
                                        nc.tensor.matmul(banks[4 + 2 * g][0][:, :], lhsT=V[:, kt, :], rhs=PT[:, g, :], start=(kt == kept[qc][0]), stop=(kt == kept[qc][-1]))
                                        ins = nc.tensor.matmul(banks[5 + 2 * g][0][:, :], lhsT=onesb[:, :], rhs=PT[:, g, :], start=(kt == kept[qc][0]), stop=(kt == kept[qc][-1]))
                                    return ins
                                op("pe", mmPV, reads=[o["B_V"], B_PT, B_onesb], writes=[banks[4][1], banks[5][1], banks[6][1], banks[7][1]])
                                if kt == kept[qc][-1]:
                                    O0, D0, O1, D1 = banks[4][0], banks[5][0], banks[6][0], banks[7][0]
                                    B_O0, B_D0, B_O1, B_D1 = banks[4][1], banks[5][1], banks[6][1], banks[7][1]
                                    op("act", lambda: nc.scalar.activation(out=f[0][:, :], in_=D0[:, :], func=AF.Ln), reads=[B_D0], writes=[B_f[0]])
                                    op("act", lambda: nc.scalar.activation(out=f[1][:, :], in_=D1[:, :], func=AF.Ln), reads=[B_D1], writes=[B_f[1]])
                                    op("act", lambda: nc.scalar.activation(out=f[0][:, :], in_=f[0][:, :], func=AF.Exp, scale=-1.0), reads=[B_f[0]], writes=[B_f[0]])
                                    op("act", lambda: nc.scalar.activation(out=f[1][:, :], in_=f[1][:, :], func=AF.Exp, scale=-1.0), reads=[B_f[1]], writes=[B_f[1]])
                                    op("dve", lambda: nc.vector.tensor_tensor(out=f[2][:, :], in0=O0[:, :], in1=f[0][:, :], op=ALU.mult), reads=[B_O0, B_f[0]], writes=[B_f[2]])
                                    op("dve", lambda: nc.vector.tensor_tensor(out=f[3][:, :], in0=O1[:, :], in1=f[1][:, :], op=ALU.mult), reads=[B_O1, B_f[1]], writes=[B_f[3]])
                                    op("dve", lambda: nc.vector.scalar_tensor_tensor(out=f[2][:, :], in0=f[3][:, :], scalar=small[:, 6:7], in1=f[2][:, :],
                                                                                     op0=ALU.mult, op1=ALU.add), reads=[B_f[3], B_f[2], B_small], writes=[B_f[2]])
                                    op("act", lambda: nc.scalar.activation(out=sqb[:, :], in_=f[2][:, :], func=AF.Square), reads=[B_f[2]], writes=[B_sqb])
                                    op("pe", lambda: nc.tensor.matmul(D0[:, :], lhsT=onesb[:, :], rhs=sqb[:, :], start=True, stop=True),
                                       reads=[B_sqb, B_onesb], writes=[B_D0])
                                    op("act", lambda: nc.scalar.activation(out=f[0][:, :], in_=D0[:, :], func=AF.Ln, scale=1.0 / 128, bias=EPS),
                                       reads=[B_D0], writes=[B_f[0]])
                                    op("act", lambda: nc.scalar.activation(out=f[0][:, :], in_=f[0][:, :], func=AF.Exp, scale=-0.5), reads=[B_f[0]], writes=[B_f[0]])
                                    yaT, B_yaT = yaTs[ycount_box[0] % 2], B_yaTs[ycount_box[0] % 2]
                                    ycount_box[0] += 1
                                    op("dve", lambda yaT=yaT: nc.vector.scalar_tensor_tensor(out=yaT[:, :], in0=f[2][:, :], scalar=small[:, 4:5], in1=f[0][:, :],
                                                                                             op0=ALU.mult, op1=ALU.mult), reads=[B_f[2], B_f[0], B_small], writes=[B_yaT])
                                    dma(ysc[h, :, q0:q0 + 512], yaT[:, :], reads=[B_yaT], writes=[B_ysc], partial=True)
                            backs.append(back)
                            if len(backs) > LAG:
                                backs.pop(0)()
                        while backs:
                            backs.pop(0)()
                        ycount = ycount_box[0]
                    fw.sync_all()

            def phase_B():
                with ExitStack() as ph:
                    o = alloc_qkv(ph, need_v=False)
                    QT, KT, V = o["QT"], o["KT"], o["V"]
                    Vc = {d_: sbt(ph, "Vb%d" % d_, [128, NT, 2, 128], BF16) for d_ in (1, 4, 16)}
                    B_Vc = {d_: fw.buf("Vb%d" % d_) for d_ in (1, 4, 16)}
                    for d_ in (1, 4, 16):
                        op("pool", lambda d_=d_: nc.gpsimd.memset(Vc[d_][:, :, :, 64:128], 1.0), writes=[B_Vc[d_]], partial=True)
                    Bband = sbt(ph, "Bband", [128, 384], F32)
                    ibb = sbt(ph, "ibband", [128, 384], I32)
                    btmp = ibb[:, :].bitcast(F32)
                    B_ibb, B_Bband, B_btmp = fw.buf("ibb"), fw.buf("Bband"), fw.buf("btmp")
                    op("pool", lambda: nc.gpsimd.iota(ibb[:, :], pattern=[[1, 384]], base=-128, channel_multiplier=-1), writes=[B_ibb])
                    op("dve", lambda: nc.vector.tensor_copy(out=Bband[:, :], in_=ibb[:, :]), reads=[B_ibb], writes=[B_Bband])
                    op("dve", lambda: nc.vector.scalar_tensor_tensor(out=Bband[:, :], in0=Bband[:, :], scalar=-1.0, in1=Bband[:, :], op0=ALU.mult, op1=ALU.max),
                       reads=[B_Bband], writes=[B_Bband])
                    op("dve", lambda: nc.vector.tensor_scalar(out=btmp[:, :], in0=Bband[:, :], scalar1=64.0, scalar2=-NEG_BIG, op0=ALU.is_gt, op1=ALU.mult),
                       reads=[B_Bband], writes=[B_btmp])
                    op("dve", lambda: nc.vector.tensor_tensor(out=Bband[:, :], in0=Bband[:, :], in1=btmp[:, :], op=ALU.add), reads=[B_Bband, B_btmp], writes=[B_Bband])
                    op("dve", lambda: nc.vector.tensor_scalar(out=Bband[:, :], in0=Bband[:, :], scalar1=-1.0, scalar2=None, op0=ALU.mult),
                       reads=[B_Bband], writes=[B_Bband])
                    tts = [sbt(ph, "btt%d" % i, [128, 384], F32) for i in range(3)]
                    B_tts = [fw.buf("btt%d" % i) for i in range(3)]
                    PTs = [sbt(ph, "bPT%d" % i, [128, 384], BF16) for i in range(8)]
                    B_PTs = [fw.buf("bPT%d" % i) for i in range(8)]
                    acc = sbt(ph, "accB", [128, S], F32)
                    rden = sbt(ph, "rdenB", [64, 2048], F32)
                    ybT = sbt(ph, "ybT", [64, S], BF16)
                    B_acc, B_rden, B_ybT = fw.buf("accB"), fw.buf("rdenB"), fw.buf("ybT")
                    if cfg.b_pairs:
                        p0_ = cfg.b_pairs[0]
                        load_qkv_w(o, 0, OFF_BQ + p0_ * 128, OFF_BK + p0_ * 128, OFF_BV + p0_ * 128)
                    for pi_, hp in enumerate(cfg.b_pairs):
                        if pi_ + 1 < len(cfg.b_pairs):
                            pn_ = cfg.b_pairs[pi_ + 1]
                            load_qkv_w(o, (pi_ + 1) % 2, OFF_BQ + pn_ * 128, OFF_BK + pn_ * 128, OFF_BV + pn_ * 128)
                        qkv_project(o, pi_ % 2, 2, 3, vdst=(Vc[1], B_Vc[1]))
                        wq, bt = o["wq"][pi_ % 2], o["bt"][pi_ % 2]
                        B_wq_, B_bt_ = o["B_wq"][pi_ % 2], o["B_bt"][pi_ % 2]
                        vbr = Ring([0, 1])
                        for dil in (4, 16):
                            U = S // dil
                            nut = U // 128
                            tiles = [(r, ut) for r in range(dil) for ut in range(nut)]
                            for g0 in range(0, len(tiles), 4):
                                grp = tiles[g0:g0 + 4]
                                bi = vbr.next()
                                pv, B_pv = banks[bi]

                                def mmv(grp=grp, pv=pv, dil=dil):
                                    ins = None
                                    for j, (r, ut) in enumerate(grp):
                                        c0 = r + dil * ut * 128
                                        for k in range(8):
                                            ins = nc.tensor.matmul(pv[:, j * 128:(j + 1) * 128], lhsT=hTf[:, k, c0:c0 + dil * 127 + 1:dil],
                                                                   rhs=wq[:, k, 256:384], start=(k == 0), stop=(k == 7))
                                    return ins
                                op("pe", mmv, reads=[B_hTf, B_wq_], writes=[B_pv])
                                ng = len(grp)
                                op("dve", lambda pv=pv, g0=g0, ng=ng, dil=dil: nc.vector.tensor_tensor(
                                    out=Vc[dil][:, g0:g0 + ng, :, 0:64], in0=pv[:, 0:ng * 128].rearrange("p (j e c) -> p j e c", j=ng, e=2),
                                    in1=bt[:, 256:384].rearrange("p (o e c) -> p o e c", o=1, e=2).to_broadcast([128, ng, 2, 64]), op=ALU.add),
                                   reads=[B_pv, B_bt_], writes=[B_Vc[dil]], partial=True)
                        for e in range(2):
                            m = SLOPES_B[hp * 2 + e]
                            sring = Ring([0, 1, 4, 5, 6, 7])
                            LOOK = 3
                            steps = []
                            for (_win, dil) in PATTERNS:
                                nkt = (S // dil) // 128
                                for r in range(dil):
                                    for kt in range(nkt):
                                        steps.append((dil, r, kt, nkt))

                            def ccols(dil, r, u0, n):
                                c0 = r + dil * u0
                                return slice(c0, c0 + dil * (n - 1) + 1, dil)
                            sbank = {}
                            nxt = [0]

                            def emit_S(i):
                                dil, r, kt, nkt = steps[i]
                                qa, qb = max(kt - 1, 0), min(kt + 1, nkt - 1)
                                nq = (qb - qa + 1) * 128
                                si = sring.next()
                                sbank[i] = si
                                pS, B_pS = banks[si]
                                op("pe", lambda: nc.tensor.matmul(
                                    pS[:, 0:nq], lhsT=KT[e * 64:(e + 1) * 64, ccols(dil, r, kt * 128, 128)], rhs=QT[e * 64:(e + 1) * 64, ccols(dil, r, qa * 128, nq)],
                                    start=True, stop=True), reads=[o["B_QT"], o["B_KT"]], writes=[B_pS])
                            hist = {}
                            segctr = {}
                            nsegs = [0]
                            bq = []
                            LAGB = 2
                            for i, (dil, r, kt, nkt) in enumerate(steps):
                                while nxt[0] <= min(i + LOOK, len(steps) - 1):
                                    emit_S(nxt[0])
                                    nxt[0] += 1
                                qa, qb = max(kt - 1, 0), min(kt + 1, nkt - 1)
                                nq = (qb - qa + 1) * 128
                                pS, B_pS = banks[sbank.pop(i)]
                                tt, B_tt = tts[i % 3], B_tts[i % 3]
                                PT, B_PT = PTs[i % 8], B_PTs[i % 8]
                                boff = (qa - (kt - 1)) * 128
                                op("dve", lambda: nc.vector.scalar_tensor_tensor(
                                    out=tt[:, 0:nq], in0=Bband[:, boff:boff + nq], scalar=float(m * dil), in1=pS[:, 0:nq], op0=ALU.mult, op1=ALU.add),
                                   reads=[B_Bband, B_pS], writes=[B_tt])
                                op("act", lambda: nc.scalar.activation(out=PT[:, 0:nq], in_=tt[:, 0:nq], func=AF.Exp), reads=[B_tt], writes=[B_PT])
                                hist[(dil, r, kt)] = (PT, B_PT, qa)
                                qts = []
                                if kt >= 1:
                                    qts.append(kt - 1)
                                if kt == nkt - 1:
                                    qts.append(kt)
                                for qt in qts:
                                    bq.append((dil, r, nkt, qt))
                                while len(bq) > 0 and (len(bq) > LAGB * 2 or i == len(steps) - 1):
                                    dil, r, nkt, qt = bq.pop(0)
                                    seg = qt // 4
                                    if (dil, r, seg) not in segctr:
                                        segctr[(dil, r, seg)] = nsegs[0]
                                        nsegs[0] += 1
                                    par = segctr[(dil, r, seg)] % 2
                                    bN, B_bN = banks[2 + par]
                                    col = (qt % 4) * 128
                                    kts = list(range(max(qt - 1, 0), min(qt + 1, nkt - 1) + 1))

                                    def mmPV():
                                        ins = None
                                        for i_, k_ in enumerate(kts):
                                            PTk, _b, qak = hist[(dil, r, k_)]
                                            rhs = PTk[:, (qt - qak) * 128:(qt - qak + 1) * 128]
                                            ins = nc.tensor.matmul(bN[:, col:col + 128], lhsT=Vc[dil][:, r * nkt + k_, e, :], rhs=rhs,
                                                                   start=(i_ == 0), stop=(i_ == len(kts) - 1))
                                        return ins
                                    op("pe", mmPV, reads=[B_Vc[dil]] + [hist[(dil, r, k_)][1] for k_ in kts], writes=[B_bN], partial=True)
                                    if qt % 4 == 3 or qt == nkt - 1:
                                        ncols = (qt - seg * 4 + 1) * 128
                                        dst = ccols(dil, r, seg * 512, ncols)
                                        if dil == 1:
                                            op("act", lambda: nc.scalar.copy(out=acc[:, dst], in_=bN[:, 0:ncols]), reads=[B_bN], writes=[B_acc], partial=True)
                                        else:
                                            op("dve", lambda: nc.vector.tensor_tensor(out=acc[:, dst], in0=bN[:, 0:ncols], in1=acc[:, dst], op=ALU.add),
                                               reads=[B_bN, B_acc], writes=[B_acc], partial=True)
                            for c0 in range(0, S, 2048):
                                op("act", lambda c0=c0: nc.scalar.activation(out=acc[64:128, c0:c0 + 2048], in_=acc[64:128, c0:c0 + 2048], func=AF.Ln),
                                   reads=[B_acc], writes=[B_acc], partial=True)
                                op("act", lambda c0=c0: nc.scalar.activation(out=acc[64:128, c0:c0 + 2048], in_=acc[64:128, c0:c0 + 2048], func=AF.Exp, scale=-1.0),
                                   reads=[B_acc], writes=[B_acc], partial=True)
                            for c0 in range(0, S, 2048):
                                dma(rden[:, :], acc[64:128, c0:c0 + 2048], reads=[B_acc], writes=[B_rden])
                                op("dve", lambda c0=c0: nc.vector.tensor_tensor(out=ybT[:, c0:c0 + 2048], in0=acc[0:64, c0:c0 + 2048], in1=rden[:, :], op=ALU.mult),
                                   reads=[B_acc, B_rden], writes=[B_ybT], partial=True)
                            dma(ysc[8 + hp, e * 64:(e + 1) * 64, :], ybT[:, :], reads=[B_ybT], writes=[B_ysc], partial=True)
                    fw.sync_all()

            def phase_C():
                with ExitStack() as ph:
                    zT = sbt(ph, "zT", [128, 6, S + 30], BF16)
                    B_zT = fw.buf("zT")
                    op("pool", lambda: nc.gpsimd.memset(zT[:, :, 0:15], 0.0), writes=[B_zT], partial=True)
                    op("pool", lambda: nc.gpsimd.memset(zT[:, :, S + 15:S + 30], 0.0), writes=[B_zT], partial=True)
                    wcv = sbt(ph, "wcv", [128, 192], F32)
                    diagW = sbt(ph, "diagW", [128, 186, 128], BF16)
                    B_wcv, B_diagW = fw.buf("wcv"), fw.buf("diagW")
                    with ExitStack() as ph1:
                        stgc = sbt(ph1, "stgc", [96, 2, 128], F32)
                        B_stgc = fw.buf("stgc")
                        src = wl("c_dw_w").rearrange("j (ct p) -> (j ct) p", p=128)
                        dma(stgc[0:96, 0, :], src[0:96, :], writes=[B_stgc], partial=True)
                        dma(stgc[0:90, 1, :], src[96:186, :], writes=[B_stgc], partial=True)
                        pt, B_pt = banks[0]
                        op("pe", lambda: nc.tensor.transpose(out=pt[:, 0:96], in_=stgc[0:96, 0, :], identity=identf[0:96, 0:96]),
                           reads=[B_stgc, B_identf], writes=[B_pt], partial=True)
                        op("pe", lambda: nc.tensor.transpose(out=pt[:, 96:186], in_=stgc[0:90, 1, :], identity=identf[0:90, 0:90]),
                           reads=[B_stgc, B_identf], writes=[B_pt], partial=True)
                        op("act", lambda: nc.scalar.copy(out=wcv[:, 0:186], in_=pt[:, 0:186]), reads=[B_pt], writes=[B_wcv])
                        for idx in range(186):
                            eng = "dve" if idx % 2 == 0 else "pool"
                            e_ = nc.vector if eng == "dve" else nc.gpsimd
                            op(eng, lambda idx=idx, e_=e_: e_.tensor_scalar(out=diagW[:, idx, :], in0=identb[:, :], scalar1=wcv[:, idx:idx + 1], scalar2=None, op0=ALU.mult),
                               reads=[B_identb, B_wcv], writes=[B_diagW], partial=True)
                        wc = sbt(ph1, "wc", [128, 8, 1536], BF16)
                        B_wc = fw.buf("wc")
                        stgs = None
                        load_weight(wc, B_wc, lambda k, c0, c1: wl("w_in", slice(k * 128, (k + 1) * 128), slice(OFF_CU + c0, OFF_CU + c1)), 8, 1536, stgs, 1536)
                        sgs = [sbt(ph1, "csg%d" % i, [128, 512], F32) for i in range(2)]
                        B_sgs = [fw.buf("csg0"), fw.buf("csg1")]
                        ar, gr = Ring([0, 1]), Ring([2, 3])
                        n = 0
                        for c in range(NCH):
                            for ct in range(6):
                                pa, B_pa = banks[ar.next()]
                                pg, B_pg = banks[gr.next()]

                                def mm(pa=pa, pg=pg, ct=ct, c=c):
                                    ins = None
                                    for k in range(8):
                                        nc.tensor.matmul(pa[:, :], lhsT=wc[:, k, ct * 128:(ct + 1) * 128], rhs=hTf[:, k, c * 512:(c + 1) * 512], start=(k == 0), stop=(k == 7))
                                    for k in range(8):
                                        ins = nc.tensor.matmul(pg[:, :], lhsT=wc[:, k, 768 + ct * 128:768 + (ct + 1) * 128], rhs=hTf[:, k, c * 512:(c + 1) * 512],
                                                               start=(k == 0), stop=(k == 7))
                                    return ins
                                op("pe", mm, reads=[B_wc, B_hTf], writes=[B_pa, B_pg])
                                sg, B_sg = sgs[n % 2], B_sgs[n % 2]
                                n += 1
                                op("act", lambda pg=pg, sg=sg, ct=ct: nc.scalar.activation(out=sg[:, :], in_=pg[:, :], func=AF.Sigmoid,
                                                                                         bias=vecT[:, V_BC + 6 + ct:V_BC + 7 + ct]),
                                   reads=[B_pg, B_vecT], writes=[B_sg])
                                op("dve", lambda pa=pa, sg=sg, ct=ct, c=c: nc.vector.scalar_tensor_tensor(
                                    out=zT[:, ct, 15 + c * 512:15 + (c + 1) * 512], in0=pa[:, :], scalar=vecT[:, V_BC + ct:V_BC + ct + 1], in1=sg[:, :],
                                    op0=ALU.add, op1=ALU.mult), reads=[B_pa, B_sg, B_vecT], writes=[B_zT], partial=True)
                        fw.sync_all()
                    sqs = [sbt(ph, "csq%d" % i, [128, 512], BF16) for i in range(2)]
                    B_sqs = [fw.buf("csq0"), fw.buf("csq1")]
                    rstd = sbt(ph, "crstd", [128, 512], F32)
                    B_rstd = fw.buf("crstd")
                    tmps = [sbt(ph, "ctmp%d" % i, [128, 512], F32) for i in range(2)]
                    B_tmps = [fw.buf("ctmp0"), fw.buf("ctmp1")]
                    ycs = [sbt(ph, "cyc%d" % i, [128, 512], BF16) for i in range(3)]
                    B_ycs = [fw.buf("cyc%d" % i) for i in range(3)]
                    n = 0
                    for c in range(NCH):
                        for ct in range(6):
                            pc, B_pc = banks[ct]

                            def mmc(pc=pc, ct=ct, c=c):
                                ins = None
                                for j in range(CONV_W):
                                    ins = nc.tensor.matmul(pc[:, :], lhsT=diagW[:, j * 6 + ct, :], rhs=zT[:, ct, c * 512 + j:c * 512 + j + 512],
                                                           start=(j == 0), stop=(j == CONV_W - 1))
                                return ins
                            op("pe", mmc, reads=[B_diagW, B_zT], writes=[B_pc])
                            sq, B_sq = sqs[ct % 2], B_sqs[ct % 2]
                            op("act", lambda pc=pc, sq=sq, ct=ct: nc.scalar.activation(out=sq[:, :], in_=pc[:, :], func=AF.Square,
                                                                                     bias=vecT[:, V_DWB + ct:V_DWB + ct + 1]),
                               reads=[B_pc, B_vecT], writes=[B_sq])
                            pr, B_pr = banks[6]
                            op("pe", lambda pr=pr, sq=sq, ct=ct: nc.tensor.matmul(pr[:, :], lhsT=onesb[:, :], rhs=sq[:, :], start=(ct == 0), stop=(ct == 5)),
                               reads=[B_sq, B_onesb], writes=[B_pr], partial=(ct != 0))
                        pr, B_pr = banks[6]
                        op("act", lambda pr=pr: nc.scalar.activation(out=rstd[:, :], in_=pr[:, :], func=AF.Ln, scale=1.0 / C_CH, bias=EPS), reads=[B_pr], writes=[B_rstd])
                        op("act", lambda: nc.scalar.activation(out=rstd[:, :], in_=rstd[:, :], func=AF.Exp, scale=-0.5), reads=[B_rstd], writes=[B_rstd])
                        for ct in range(6):
                            pc, B_pc = banks[ct]
                            tmp, B_tmp = tmps[ct % 2], B_tmps[ct % 2]
                            yc, B_yc = ycs[n % 3], B_ycs[n % 3]
                            n += 1
                            op("dve", lambda pc=pc, tmp=tmp, ct=ct: nc.vector.scalar_tensor_tensor(out=tmp[:, :], in0=pc[:, :], scalar=vecT[:, V_DWB + ct:V_DWB + ct + 1],
                                                                                                 in1=rstd[:, :], op0=ALU.add, op1=ALU.mult),
                               reads=[B_pc, B_rstd, B_vecT], writes=[B_tmp])
                            op("act", lambda tmp=tmp, yc=yc, ct=ct: nc.scalar.activation(out=yc[:, :], in_=tmp[:, :], func=AF.Silu, scale=vecT[:, V_CN + ct:V_CN + ct + 1]),
                               reads=[B_tmp, B_vecT], writes=[B_yc])
                            dma(ysc[14 + ct, :, c * 512:(c + 1) * 512], yc[:, :], reads=[B_yc], writes=[B_ysc], partial=True)
                    fw.sync_all()

            def phase_merge():
                CW = 256
                with ExitStack() as ph:
                    wg = sbt(ph, "wg", [128, 8, 3072], BF16)
                    woa = sbt(ph, "woa", [128, 8, D], BF16)
                    wob = sbt(ph, "wob", [128, 6, D], BF16)
                    woc = sbt(ph, "woc", [128, 6, D], BF16)
                    wo = sbt(ph, "wo", [128, 8, D], BF16)
                    B_wg, B_woa, B_wob, B_woc, B_wo = (fw.buf(n_) for n_ in ("wg", "woa", "wob", "woc", "wo"))
                    stgs = None
                    load_weight(woa, B_woa, lambda k, c0, c1: wl("w_out_a", slice(k * 128, (k + 1) * 128), slice(c0, c1)), 8, D, stgs, 1024)
                    load_weight(wob, B_wob, lambda k, c0, c1: wl("w_out_b", slice(k * 128, (k + 1) * 128), slice(c0, c1)), 6, D, stgs, 1024)
                    load_weight(woc, B_woc, lambda k, c0, c1: wl("w_out_c", slice(k * 128, (k + 1) * 128), slice(c0, c1)), 6, D, stgs, 1024)
                    load_weight(wg, B_wg, lambda k, c0, c1: wl("w_in", slice(k * 128, (k + 1) * 128), slice(OFF_G + c0, OFF_G + c1)), 8, 3072, stgs, 1024)
                    load_weight(wo, B_wo, lambda k, c0, c1: wl("w_out", slice(k * 128, (k + 1) * 128), slice(c0, c1)), 8, D, stgs, 1024)
                    yT = sbt(ph, "myT", [128, 20, CW], BF16)
                    xc = sbt(ph, "mxc", [128, CW // 128, D], F32)
                    mT = sbt(ph, "mmT", [128, 8, CW], BF16)
                    B_yT, B_xc, B_mT = fw.buf("myT"), fw.buf("mxc"), fw.buf("mmT")
                    gs = [sbt(ph, "mg%d" % i, [128, 3, CW], F32) for i in range(2)]
                    B_gs = [fw.buf("mg0"), fw.buf("mg1")]
                    m1s = [sbt(ph, "mm1%d" % i, [128, 3, CW], F32) for i in range(1)] * 2
                    B_m1s = [fw.buf("mm10")] * 2
                    zr = Ring([0, 3])
                    xr = Ring([6, 7])
                    nn = 0
                    ntl = CW // 128

                    def slot(base, i):
                        return banks[base + i // 2][0][:, (i % 2) * CW:(i % 2 + 1) * CW]
                    for c in range(S // CW):
                        c0 = c * CW
                        dma(yT[:, :, :], ysc[:, :, c0:c0 + CW].rearrange("c p n -> p c n"), reads=[B_ysc], writes=[B_yT])
                        dma(xc[:, :, :], xchunk(c, CW), reads=[B_x[c0 // 512]], writes=[B_xc])
                        for m in range(8):
                            base = zr.next()
                            B_db = [banks[base][1], banks[base + 1][1], banks[base + 2][1]]

                            def mmz(base=base, m=m, c0=c0):
                                ins = None
                                for br, (wt, nk, ko) in enumerate(((woa, 8, 0), (wob, 6, 8), (woc, 6, 14))):
                                    for k in range(nk):
                                        nc.tensor.matmul(slot(base, br), lhsT=wt[:, k, m * 128:(m + 1) * 128], rhs=yT[:, ko + k, :], start=(k == 0), stop=(k == nk - 1))
                                for br in range(3):
                                    for k in range(8):
                                        ins = nc.tensor.matmul(slot(base, 3 + br), lhsT=wg[:, k, br * D + m * 128:br * D + (m + 1) * 128], rhs=hTf[:, k, c0:c0 + CW],
                                                               start=(k == 0), stop=(k == 7))
                                return ins
                            op("pe", mmz, reads=[B_woa, B_wob, B_woc, B_wg, B_yT, B_hTf], writes=B_db)

                            def zslice(db, br, base=base):
                                return slot(base, br)

                            def gslice(db, br, base=base):
                                return slot(base, 3 + br)
                            db = None
                            g, B_g = gs[nn % 2], B_gs[nn % 2]
                            m1, B_m1 = m1s[nn % 2], B_m1s[nn % 2]
                            nn += 1
                            for br in range(3):
                                op("act", lambda db=db, g=g, br=br, m=m: nc.scalar.activation(out=g[:, br, :], in_=gslice(db, br), func=AF.Sigmoid,
                                                                                             bias=vecT[:, V_BG + br * 8 + m:V_BG + br * 8 + m + 1]),
                                   reads=B_db + [B_vecT], writes=[B_g], partial=True)
                            for br in range(3):
                                op("dve", lambda db=db, g=g, m1=m1, br=br: nc.vector.tensor_tensor(out=m1[:, br, :], in0=g[:, br, :], in1=zslice(db, br), op=ALU.mult),
                                   reads=B_db + [B_g], writes=[B_m1], partial=True)
                            op("dve", lambda m1=m1: nc.vector.tensor_tensor(out=m1[:, 0, :], in0=m1[:, 0, :], in1=m1[:, 1, :], op=ALU.add), reads=[B_m1], writes=[B_m1], partial=True)
                            op("dve", lambda m1=m1, m=m: nc.vector.tensor_tensor(out=mT[:, m, :], in0=m1[:, 0, :], in1=m1[:, 2, :], op=ALU.add),
                               reads=[B_m1], writes=[B_mT], partial=True)
                        for t in range(ntl):
                            for oh in range(2):
                                px, B_px = banks[xr.next()]

                                def mmx(px=px, t=t, oh=oh):
                                    ins = None
                                    for k in range(8):
                                        ins = nc.tensor.matmul(px[:, :], lhsT=mT[:, k, t * 128:(t + 1) * 128], rhs=wo[:, k, oh * 512:(oh + 1) * 512], start=(k == 0), stop=(k == 7))
                                    return ins
                                op("pe", mmx, reads=[B_mT, B_wo], writes=[B_px])
                                op("dve", lambda px=px, t=t, oh=oh: nc.vector.tensor_tensor(out=xc[:, t, oh * 512:(oh + 1) * 512], in0=px[:, :], in1=xc[:, t, oh * 512:(oh + 1) * 512], op=ALU.add),
                                   reads=[B_px, B_xc], writes=[B_xc], partial=True)
                        dma(xchunk(c, CW), xc[:, :, :], reads=[B_xc], writes=[B_x[c0 // 512]])
                    fw.sync_all()


            if "ffn1" in cfg.phases:
                phase_ffn(0)
            for sq_ in seqs:
                cur[0] = sq_
                hT_stack = ExitStack()
                hTf = sbt(hT_stack, "hTf", [128, 8, S], BF16)
                B_hTf = fw.buf("hTf")
                if "hT" in cfg.phases:
                    phase_hT()
                if "A" in cfg.phases:
                    phase_A()
                if "B" in cfg.phases:
                    phase_B()
                if "C" in cfg.phases:
                    phase_C()
                if "merge" in cfg.phases:
                    phase_merge()
                hT_stack.close()
            if "ffn2" in cfg.phases:
                phase_ffn(1)

        with nc.Fori(0, L) as l:
            nc.sync.dma_start(out=wcur[:, :], in_=wblob[bass.ds(l, 1), 0:PIECE_ROWS, :].rearrange("o r c -> (o r) c")).then_inc(s_wc, 16)
            nc.sync.wait_ge(s_wc, 16)
            with ExitStack() as cv:
                HALF = PIECE_ROWS // 2
                RPP = HALF // 128
                NEL = RPP * BLOB_COLS
                st32 = [sbt(cv, "cv32_%d" % i, [128, NEL], F32) for i in range(2)]
                st16 = [sbt(cv, "cv16_%d" % i, [128, NEL], BF16) for i in range(2)]
                B32 = [fw.buf("cv32_0"), fw.buf("cv32_1")]
                B16 = [fw.buf("cv16_0"), fw.buf("cv16_1")]
                with nc.Fori(0, NPIECE) as pi:
                    for hf in range(2):
                        r0 = pi * PIECE_ROWS + hf * HALF
                        dma(st32[hf][:, :], wblob[bass.ds(l, 1), bass.ds(r0, HALF), :].rearrange("o (p r) c -> p (o r c)", p=128), writes=[B32[hf]])
                    for hf in range(2):
                        NCK = 8
                        CK = NEL // NCK
                        for ck in range(NCK):
                            ek = ("act", "dve", "pool", "act", "dve", "act", "dve", "pool")[ck]
                            if ek == "act":
                                op("act", lambda: nc.scalar.copy(out=st16[hf][:, ck * CK:(ck + 1) * CK], in_=st32[hf][:, ck * CK:(ck + 1) * CK]),
                                   reads=[B32[hf]], writes=[B16[hf]], partial=True)
                            elif ek == "dve":
                                op("dve", lambda: nc.vector.tensor_copy(out=st16[hf][:, ck * CK:(ck + 1) * CK], in_=st32[hf][:, ck * CK:(ck + 1) * CK]),
                                   reads=[B32[hf]], writes=[B16[hf]], partial=True)
                            else:
                                op("pool", lambda: nc.gpsimd.tensor_copy(out=st16[hf][:, ck * CK:(ck + 1) * CK], in_=st32[hf][:, ck * CK:(ck + 1) * CK]),
                                   reads=[B32[hf]], writes=[B16[hf]], partial=True)
                        r0 = pi * PIECE_ROWS + hf * HALF
                        dma(wcur16[bass.ds(r0, HALF), :].rearrange("(p r) c -> p (r c)", p=128), st16[hf][:, :], reads=[B16[hf]])
                    fw.hard_barrier()
            body(l, list(range(NSEQ)))
            fw.hard_barrier()
    return nc, fw


def kernel(**inputs):
    cfg = Cfg()
    return run_kernel(cfg, inputs)


def run_kernel(cfg, inputs, n_cores=8, trace=False):
    nc, fw = build_program(cfg)
    x = np.ascontiguousarray(inputs["x"], dtype=np.float32)
    lay, brows = blob_layout()
    blob = np.zeros((cfg.L, brows * BLOB_COLS), dtype=np.float32)
    for name, shp in W_SHAPES:
        off = lay[name][0]
        n = int(np.prod(shp))
        if name == "lam0":
            for l in range(cfg.L):
                blob[l, off] = 0.8 - 0.6 * math.exp(-0.3 * l)
                blob[l, off + 1] = 1.0 - (0.8 - 0.6 * math.exp(-0.3 * l))
        elif name == "aq_aug":
            j = np.arange(512)
            t = np.zeros((4, A_HEADS, 512), dtype=np.float32)
            for h in range(A_HEADS):
                t[0, h] = -SLOPES_A[h] * (j % 256)
                t[1, h] = -SLOPES_A[h] * 256.0 * (j // 256)
                t[2, h] = 1.0
            blob[:, off:off + n] = t.reshape(1, n)
        elif name == "ak_aug":
            i = np.arange(128)
            t = np.zeros((4, A_HEADS, 2, 128), dtype=np.float32)
            for h in range(A_HEADS):
                t[0, h, 0] = 1.0
                t[1, h, 0] = 1.0
                t[2, h, 0] = SLOPES_A[h] * i
                t[0, h, 1] = -1.0
                t[1, h, 1] = -1.0
                t[2, h, 1] = -SLOPES_A[h] * i
            blob[:, off:off + n] = t.reshape(1, n)
        else:
            blob[:, off:off + n] = np.asarray(inputs[name], dtype=np.float32).reshape(cfg.L, n)
    blob = blob.reshape(cfg.L, brows, BLOB_COLS)
    in_maps = []
    for c in range(n_cores):
        in_maps.append({"x": x[c * cfg.NSEQ:(c + 1) * cfg.NSEQ], "wblob": blob})
    res = run_bass_kernel_spmd(nc, in_maps, core_ids=list(range(n_cores)), **({"trace": True} if trace else {}))
    if trace:
        print("EXEC_NS", res.exec_time_ns, "n_instr", fw.n_instr)
    outs = np.concatenate([r["out"] for r in res.results], axis=0)
    if cfg.debug:
        return outs, res.results
    return outs
```

```python
import math
import os
from contextlib import ExitStack

import numpy as np
import concourse.bass as bass
import concourse.mybir as mybir
from concourse.bass_utils import run_bass_kernel_spmd

F32 = mybir.dt.float32
BF16 = mybir.dt.bfloat16
I32 = mybir.dt.int32
AF = mybir.ActivationFunctionType
ALU = mybir.AluOpType
AX = mybir.AxisListType

D = 1024
DEPTH = 4
HD = 64
A_HEADS = 8
B_HEADS = 12
C_CH = 768
CONV_W = 31
D_FF = 2816
OFF_AQ, OFF_AK, OFF_AV = 0, 1024, 2048
OFF_BQ, OFF_BK, OFF_BV = 3072, 3840, 4608
OFF_CU = 5376
OFF_G = 6912
IN_W = 9984
EPS = 1e-6
ATTN_SCALE = HD ** -0.5
PATTERNS = ((128, 1), (512, 4), (2048, 16))
SLOPES_A = [2.0 ** (-8.0 * i / A_HEADS) for i in range(1, A_HEADS + 1)]
SLOPES_B = [2.0 ** (-8.0 * i / B_HEADS) for i in range(1, B_HEADS + 1)]
NEG_BIG = -1.0e6


class Buf:
    __slots__ = ("name", "w", "r", "excl")

    def __init__(self, name="", excl=False):
        self.name = name
        self.w = {}
        self.r = {}
        self.excl = excl


class Eng:
    def __init__(self, fw, key, e, sem):
        self.fw, self.key, self.e, self.sem = fw, key, e, sem
        self.count = 0
        self.seen = {}

    def wait_ev(self, key, cnt):
        if self.seen.get(key, 0) >= cnt:
            return
        self.e.wait_ge(self.fw.sems[key], cnt)
        self.seen[key] = cnt


class FW:
    NDMA = 8

    def __init__(self, nc, stack):
        self.nc = nc
        self.sems = {}
        self.engs = {}
        for key, e in (("pe", nc.tensor), ("act", nc.scalar), ("dve", nc.vector),
                       ("pool", nc.gpsimd), ("sp", nc.sync)):
            sem = stack.enter_context(nc.semaphore("s_" + key))
            self.sems[key] = sem
            self.engs[key] = Eng(self, key, e, sem)
        self.dma_keys = []
        for i in range(self.NDMA):
            k = "dma%d" % i
            self.sems[k] = stack.enter_context(nc.semaphore("s_" + k))
            self.dma_keys.append(k)
        self.dma_cnt = {k: 0 for k in self.dma_keys}
        self.dma_rr = 0
        self.n_instr = 0
        self.bufs = []

    def buf(self, name="", excl=False):
        b = Buf(name, excl)
        self.bufs.append(b)
        return b

    def reset_state(self):
        for e in self.engs.values():
            e.count = 0
            e.seen = {}
        self.dma_cnt = {k: 0 for k in self.dma_keys}
        self.dma_rr = 0
        for b in self.bufs:
            b.w = {}
            b.r = {}

    def _deps(self, eng, reads, writes):
        for b in reads:
            for k, c in b.w.items():
                eng.wait_ev(k, c)
            if b.excl:
                for k, c in b.r.items():
                    if k != eng.key:
                        eng.wait_ev(k, c)
        for b in writes:
            for k, c in b.w.items():
                if k != eng.key:
                    eng.wait_ev(k, c)
            for k, c in b.r.items():
                if k != eng.key:
                    eng.wait_ev(k, c)

    def _record(self, key, cnt, reads, writes, partial):
        for b in reads:
            if b.r.get(key, 0) < cnt:
                b.r[key] = cnt
        for b in writes:
            if not partial:
                b.r = {}
                b.w = {}
            b.w[key] = cnt

    def op(self, ek, fn, reads=(), writes=(), partial=False):
        eng = self.engs[ek]
        self._deps(eng, reads, writes)
        ins = fn()
        eng.count += 1
        ins.then_inc(eng.sem, 1)
        self._record(ek, eng.count, reads, writes, partial)
        self.n_instr += 1

    def dma(self, out, in_, reads=(), writes=(), partial=False, **kw):
        sp = self.engs["sp"]
        self._deps(sp, reads, writes)
        k = self.dma_keys[self.dma_rr % self.NDMA]
        self.dma_rr += 1
        if self.dma_cnt[k] > 0:
            sp.wait_ev(k, self.dma_cnt[k])
        self.nc.sync.dma_start(out=out, in_=in_, **kw).then_inc(self.sems[k], 16)
        self.dma_cnt[k] += 16
        self._record(k, self.dma_cnt[k], reads, writes, partial)
        self.n_instr += 1

    def drain_dmas(self):
        sp = self.engs["sp"]
        for k in self.dma_keys:
            if self.dma_cnt[k] > 0:
                sp.wait_ev(k, self.dma_cnt[k])

    def sync_all(self):
        for e in self.engs.values():
            for f in self.engs.values():
                if f is not e and f.count > 0:
                    e.wait_ev(f.key, f.count)
            for k in self.dma_keys:
                if self.dma_cnt[k] > 0:
                    e.wait_ev(k, self.dma_cnt[k])

    def hard_barrier(self):
        self.drain_dmas()
        self.nc.all_engine_barrier()
        for s in list(self.sems.values()) + list(getattr(self, "extra_sems", [])):
            self.nc.gpsimd.sem_clear(s)
        self.nc.all_engine_barrier()
        self.reset_state()


class Ring:
    def __init__(self, items):
        self.items = list(items)
        self.i = 0

    def next(self):
        it = self.items[self.i % len(self.items)]
        self.i += 1
        return it


BIG_W = ("ffn1_w_up", "ffn1_w_down", "w_in", "w_out_a", "w_out_b", "w_out_c", "w_out", "ffn2_w_up", "ffn2_w_down")
W_SHAPES = (("ffn1_norm", (D,)), ("mix_norm", (D,)), ("b_in", (IN_W,)),
            ("a_q_norm", (HD,)), ("a_k_norm", (HD,)), ("a_lambda", (4, HD)),
            ("a_sub_norm", (2 * HD,)), ("b_q_norm", (HD,)),
            ("b_k_norm", (HD,)), ("c_dw_w", (CONV_W, C_CH)),
            ("c_dw_b", (C_CH,)), ("c_norm", (C_CH,)), ("ffn2_norm", (D,)),
            ("lam0", (2,)), ("aq_aug", (4, A_HEADS * 512)), ("ak_aug", (4, A_HEADS * 2 * 128)),
            ("ffn1_w_up", (D, 2 * D_FF)), ("ffn1_w_down", (D_FF, D)), ("w_in", (D, IN_W)),
            ("w_out_a", (D, D)), ("w_out_b", (768, D)), ("w_out_c", (C_CH, D)), ("w_out", (D, D)),
            ("ffn2_w_up", (D, 2 * D_FF)), ("ffn2_w_down", (D_FF, D)))
BLOB_COLS = 2048
PIECE_ROWS = 1024


def blob_layout():
    off = 0
    lay = {}
    for name, shp in W_SHAPES:
        n = int(np.prod(shp))
        lay[name] = (off, shp)
        off += (n + 63) // 64 * 64
    rows = (off + BLOB_COLS - 1) // BLOB_COLS
    rows = (rows + PIECE_ROWS - 1) // PIECE_ROWS * PIECE_ROWS
    return lay, rows


class Cfg:
    def __init__(self, S=4096, NSEQ=2, L=DEPTH, phases=None, a_heads=None, b_pairs=None, debug=False):
        self.S, self.NSEQ, self.L = S, NSEQ, L
        self.phases = phases or ("ffn1", "hT", "A", "B", "C", "merge", "ffn2")
        self.a_heads = list(range(A_HEADS)) if a_heads is None else a_heads
        self.b_pairs = list(range(B_HEADS // 2)) if b_pairs is None else b_pairs
        self.debug = debug


def build_program(cfg):
    S, NSEQ, L = cfg.S, cfg.NSEQ, cfg.L
    NT = S // 128
    NCH = S // 512
    nc = bass.Bass("TRN2", target_bir_lowering=False)

    def din(name, shape):
        return nc.dram_tensor(name, list(shape), F32, kind="ExternalInput").ap()

    x_in = din("x", [NSEQ, S, D])
    LAY, BROWS = blob_layout()
    NPIECE = BROWS // PIECE_ROWS
    wblob = din("wblob", [L, BROWS, BLOB_COLS])
    wcur = nc.dram_tensor("wcur", [PIECE_ROWS, BLOB_COLS], F32).ap()
    wcur16 = nc.dram_tensor("wcur16", [BROWS, BLOB_COLS], BF16).ap()
    xcur = nc.dram_tensor("xcur", [S, D], F32).ap()
    out = nc.dram_tensor("out", [NSEQ, S, D], F32, kind="ExternalOutput").ap()
    ysc_kind = "ExternalOutput" if cfg.debug else "Internal"
    ysc = nc.dram_tensor("ysc", [20, 128, S], BF16, kind=ysc_kind).ap()
    hdbg = nc.dram_tensor("hdbg", [8, 128, S], BF16, kind="ExternalOutput").ap() if cfg.debug else None

    with ExitStack() as st:
        fw = FW(nc, st)
        op, dma = fw.op, fw.dma
        s_wc = st.enter_context(nc.semaphore("s_wc"))
        s_xc = st.enter_context(nc.semaphore("s_xc"))
        fw.extra_sems = [s_wc, s_xc]

        uniq = [0]

        def sbt(stack, name, shape, dt):
            uniq[0] += 1
            return stack.enter_context(nc.sbuf_tensor("%s_%d" % (name, uniq[0]), list(shape), dt))

        identb = sbt(st, "identb", [128, 128], BF16)
        identf = sbt(st, "identf", [128, 128], F32)
        onesb = sbt(st, "onesb", [128, 128], BF16)
        vecT = sbt(st, "vecT", [128, 72], F32)
        small = sbt(st, "small", [128, 16], F32)
        B_identb, B_identf, B_onesb, B_vecT, B_small = (fw.buf(n) for n in ("identb", "identf", "onesb", "vecT", "small"))
        banks = []
        dbanks = []
        for i in range(4):
            t = st.enter_context(nc.psum_tensor("dbank%d" % i, [128, 1024], F32))
            dbanks.append(t)
            banks.append((t[:, 0:512], fw.buf("bank%d" % (2 * i), excl=True)))
            banks.append((t[:, 512:1024], fw.buf("bank%d" % (2 * i + 1), excl=True)))

        def bank_bf(i):
            return banks[i][0][:, :].bitcast(BF16)

        op("pool", lambda: nc.gpsimd.memset(identf[:], 0.0), writes=[B_identf])
        op("pool", lambda: nc.gpsimd.affine_select(out=identf[:], in_=identf[:], pattern=[[-1, 128]], compare_op=ALU.not_equal,
                                                   fill=1.0, base=0, channel_multiplier=1), reads=[B_identf], writes=[B_identf])
        op("pool", lambda: nc.gpsimd.memset(onesb[:], 1.0), writes=[B_onesb])
        op("dve", lambda: nc.vector.tensor_copy(out=identb[:], in_=identf[:]), reads=[B_identf], writes=[B_identb])

        for s_ in range(NSEQ):
            for c in range(NCH):
                dma(out[s_, c * 512:(c + 1) * 512, :], x_in[s_, c * 512:(c + 1) * 512, :])
        fw.hard_barrier()

        def body(l, s):
            def wl(name, *idx):
                off, shp = LAY[name]
                n = int(np.prod(shp))
                src_ = wcur16 if name in BIG_W else wcur
                flat = src_.rearrange("r c -> (r c)")[off:off + n]
                if len(shp) == 2:
                    flat = flat.rearrange("(a b) -> a b", b=shp[1])
                return flat[idx] if idx else flat

            def xchunk(c, n=512):
                return out[s, c * n:(c + 1) * n, :].rearrange("(t p) f -> p t f", p=128)

            B_x = [fw.buf("xdram%d" % c) for c in range(NCH)]
            B_ysc = fw.buf("ysc")

            with ExitStack() as ph:
                stgv = sbt(ph, "stgv", [72, 128], F32)
                B_stgv = fw.buf("stgv")
                rows = 0
                for name, n in (("ffn1_norm", 8), ("mix_norm", 8), ("ffn2_norm", 8)):
                    dma(stgv[rows:rows + n, :], wl(name).rearrange("(k p) -> k p", p=128), writes=[B_stgv], partial=True)
                    rows += n
                dma(stgv[24:36, :], wl("b_in", slice(OFF_CU, OFF_CU + 1536)).rearrange("(k p) -> k p", p=128), writes=[B_stgv], partial=True)
                dma(stgv[36:60, :], wl("b_in", slice(OFF_G, OFF_G + 3072)).rearrange("(k p) -> k p", p=128), writes=[B_stgv], partial=True)
                dma(stgv[60:66, :], wl("c_dw_b").rearrange("(k p) -> k p", p=128), writes=[B_stgv], partial=True)
                dma(stgv[66:72, :], wl("c_norm").rearrange("(k p) -> k p", p=128), writes=[B_stgv], partial=True)
                pt, B_pt = banks[0]
                op("pe", lambda: nc.tensor.transpose(out=pt[:, 0:72], in_=stgv[0:72, :], identity=identf[0:72, 0:72]),
                   reads=[B_stgv, B_identf], writes=[B_pt])
                op("act", lambda: nc.scalar.copy(out=vecT[:, :], in_=pt[:, 0:72]), reads=[B_pt], writes=[B_vecT])
                B_sm_in = fw.buf("sm_in")
                smi = sbt(ph, "smi", [128, 8], F32)
                for col, name in ((0, "a_q_norm"), (1, "a_k_norm"), (2, "b_q_norm"), (3, "b_k_norm")):
                    src = wl(name).rearrange("(d i) -> d i", i=1)
                    dma(smi[0:64, col:col + 1], src, writes=[B_sm_in], partial=True)
                    dma(smi[64:128, col:col + 1], src, writes=[B_sm_in], partial=True)
                dma(smi[:, 4:5], wl("a_sub_norm").rearrange("(d i) -> d i", i=1), writes=[B_sm_in], partial=True)
                dma(smi[:, 5:7], wl("lam0").rearrange("(o c) -> o c", o=1).partition_broadcast(128), writes=[B_sm_in], partial=True)
                lamt = sbt(ph, "lamt", [128, 4, HD], F32)
                B_lamt = fw.buf("lamt")
                dma(lamt[:, :, :].rearrange("p a d -> p (a d)"),
                    wl("a_lambda").rearrange("a d -> (a d)").rearrange("(o n) -> o n", o=1).partition_broadcast(128), writes=[B_lamt])
                lprod = sbt(ph, "lprod", [128, 2, HD], F32)
                lsum = sbt(ph, "lsum", [128, 4], F32)
                B_lp, B_ls = fw.buf("lprod"), fw.buf("lsum")
                op("dve", lambda: nc.vector.tensor_tensor(out=lprod[:, :, :], in0=lamt[:, 0:4:2, :], in1=lamt[:, 1:4:2, :], op=ALU.mult),
                   reads=[B_lamt], writes=[B_lp])
                op("dve", lambda: nc.vector.tensor_reduce(out=lsum[:, 0:2], in_=lprod[:, :, :], axis=AX.X, op=ALU.add),
                   reads=[B_lp], writes=[B_ls])
                op("act", lambda: nc.scalar.activation(out=lsum[:, 2:4], in_=lsum[:, 0:2], func=AF.Exp), reads=[B_ls], writes=[B_ls], partial=True)
                op("dve", lambda: nc.vector.tensor_tensor(out=small[:, 5:6], in0=lsum[:, 2:3], in1=lsum[:, 3:4], op=ALU.subtract),
                   reads=[B_ls], writes=[B_small], partial=True)
                op("dve", lambda: nc.vector.tensor_tensor(out=small[:, 5:6], in0=small[:, 5:6], in1=smi[:, 5:6], op=ALU.add),
                   reads=[B_small, B_sm_in], writes=[B_small], partial=True)
                op("dve", lambda: nc.vector.tensor_scalar(out=small[:, 6:7], in0=small[:, 5:6], scalar1=-1.0, scalar2=None, op0=ALU.mult),
                   reads=[B_small], writes=[B_small], partial=True)
                op("dve", lambda: nc.vector.tensor_scalar(out=small[:, 0:1], in0=smi[:, 0:1], scalar1=ATTN_SCALE, scalar2=None, op0=ALU.mult),
                   reads=[B_sm_in], writes=[B_small], partial=True)
                op("dve", lambda: nc.vector.tensor_copy(out=small[:, 1:2], in_=smi[:, 1:2]), reads=[B_sm_in], writes=[B_small], partial=True)
                op("dve", lambda: nc.vector.tensor_scalar(out=small[:, 2:3], in0=smi[:, 2:3], scalar1=ATTN_SCALE, scalar2=None, op0=ALU.mult),
                   reads=[B_sm_in], writes=[B_small], partial=True)
                op("dve", lambda: nc.vector.tensor_copy(out=small[:, 3:4], in_=smi[:, 3:4]), reads=[B_sm_in], writes=[B_small], partial=True)
                op("dve", lambda: nc.vector.tensor_tensor(out=small[:, 4:5], in0=smi[:, 4:5], in1=smi[:, 6:7], op=ALU.mult),
                   reads=[B_sm_in], writes=[B_small], partial=True)
                fw.sync_all()

            G_FFN1, G_MIX, G_FFN2, V_BC, V_BG, V_DWB, V_CN = 0, 8, 16, 24, 36, 60, 66

            def norm_transpose(xc, B_xc, ntile, xn, B_xn, stat, B_stat, gcol, dst_fn, B_dst, tbanks, tcols, part=0):
                for t in range(ntile if part in (0, 1) else 0):
                    op("act", lambda t=t: nc.scalar.activation(out=xn[:, t, :], in_=xc[:, t, :], func=AF.Square,
                                                               accum_out=stat[:, t:t + 1]),
                       reads=[B_xc], writes=[B_xn, B_stat], partial=True)
                if part in (0, 1):
                    op("dve", lambda: nc.vector.tensor_scalar(out=stat[:, 8:8 + ntile], in0=stat[:, 0:ntile], scalar1=1.0 / D, scalar2=EPS,
                                                              op0=ALU.mult, op1=ALU.add), reads=[B_stat], writes=[B_stat], partial=True)
                    op("act", lambda: nc.scalar.activation(out=stat[:, 16:16 + ntile], in_=stat[:, 8:8 + ntile], func=AF.Sqrt),
                       reads=[B_stat], writes=[B_stat], partial=True)
                    op("dve", lambda: nc.vector.reciprocal(out=stat[:, 24:24 + ntile], in_=stat[:, 16:16 + ntile]),
                       reads=[B_stat], writes=[B_stat], partial=True)
                for t in range(ntile if part in (0, 1) else 0):
                    op("dve", lambda t=t: nc.vector.tensor_scalar(out=xn[:, t, :], in0=xc[:, t, :], scalar1=stat[:, 24 + t:25 + t],
                                                                  scalar2=None, op0=ALU.mult),
                       reads=[B_xc, B_stat], writes=[B_xn], partial=True)
                for fc in range(8 if part in (0, 2) else 0):
                    bi = tbanks[fc % len(tbanks)]
                    ptb, B_ptb = bank_bf(bi), banks[bi][1]

                    def tr(fc=fc, ptb=ptb):
                        ins = None
                        for t in range(ntile):
                            ins = nc.tensor.transpose(out=ptb[:, t * 128:(t + 1) * 128], in_=xn[:, t, fc * 128:(fc + 1) * 128],
                                                      identity=identb[:, :])
                        return ins
                    op("pe", tr, reads=[B_xn, B_identb], writes=[B_ptb])
                    eng = "act" if fc % 2 == 0 else "dve"
                    if eng == "act":
                        op("act", lambda fc=fc, ptb=ptb: nc.scalar.activation(out=dst_fn(fc), in_=ptb[:, 0:ntile * 128], func=AF.Copy,
                                                                              scale=vecT[:, gcol + fc:gcol + fc + 1]),
                           reads=[B_ptb, B_vecT], writes=[B_dst], partial=True)
                    else:
                        op("dve", lambda fc=fc, ptb=ptb: nc.vector.tensor_scalar(out=dst_fn(fc), in0=ptb[:, 0:ntile * 128],
                                                                                 scalar1=vecT[:, gcol + fc:gcol + fc + 1], scalar2=None, op0=ALU.mult),
                           reads=[B_ptb, B_vecT], writes=[B_dst], partial=True)

            def load_weight(dst, B_dst, src_fn, nk, ncols, stg_ring=None, max_cols=None):
                for k in range(nk):
                    dma(dst[:, k, 0:ncols], src_fn(k, 0, ncols), writes=[B_dst], partial=True)

            def phase_ffn(which):
                gcol = G_FFN1 if which == 0 else G_FFN2
                wun, wdn_n = ("ffn1_w_up", "ffn1_w_down") if which == 0 else ("ffn2_w_up", "ffn2_w_down")
                with ExitStack() as ph:
                    wup = sbt(ph, "wup", [128, 8, 2 * D_FF], BF16)
                    wdn = sbt(ph, "wdn", [128, 22, D], BF16)
                    B_wup, B_wdn = fw.buf("wup"), fw.buf("wdn")
                    stgs = None
                    xc = sbt(ph, "fxc", [128, 4, D], F32)
                    xc2 = sbt(ph, "fxc2", [128, 4, D], F32)
                    B_xc2 = fw.buf("fxc2")
                    xn = sbt(ph, "fxn", [128, 4, D], BF16)
                    hTc = sbt(ph, "fhT", [128, 8, 512], BF16)
                    uT = [sbt(ph, "fuT%d" % i, [128, 11, 512], BF16) for i in range(2)]
                    sa = [sbt(ph, "fsa%d" % i, [128, 512], F32) for i in range(2)]
                    stat = sbt(ph, "fstat", [128, 32], F32)
                    B_xc, B_xn, B_hTc, B_stat = fw.buf("fxc"), fw.buf("fxn"), fw.buf("fhT"), fw.buf("fstat")
                    B_uT = [fw.buf("fuT0"), fw.buf("fuT1")]
                    B_sa = [fw.buf("fsa0"), fw.buf("fsa1")]
                    load_weight(wup, B_wup, lambda k, c0, c1: wl(wun, slice(k * 128, (k + 1) * 128), slice(c0, c1)), 8, 2 * D_FF, stgs, 1408)
                    load_weight(wdn, B_wdn, lambda k, c0, c1: wl(wdn_n, slice(k * 128, (k + 1) * 128), slice(c0, c1)), 22, D, stgs, 1408)
                    abank = Ring([2, 3])
                    bbank = Ring([4, 5])
                    ybank = Ring([6, 7])
                    sai = 0
                    xcs_ = [xc, xc2]
                    B_xcs_ = [B_xc, B_xc2]

                    def load_x(c):
                        for t_ in range(4):
                            dma(xcs_[c % 2][:, t_, :], xchunk(c)[:, t_, :], reads=[B_x[c]], writes=[B_xcs_[c % 2]], partial=(t_ > 0))

                    def nt(c, part):
                        norm_transpose(xcs_[c % 2], B_xcs_[c % 2], 4, xn, B_xn, stat, B_stat, gcol, lambda fc: hTc[:, fc, :], B_hTc, [0, 1], 512, part=part)
                    load_x(0)
                    nt(0, 0)
                    for c in range(NCH):
                        xc_, B_xc_ = xcs_[c % 2], B_xcs_[c % 2]
                        if c + 1 < NCH:
                            load_x(c + 1)
                        for half in range(2):
                            if half == 1 and c + 1 < NCH:
                                nt(c + 1, 1)
                            for jj in range(11):
                                j = half * 11 + jj
                                ai, bi = abank.next(), bbank.next()
                                pa, B_pa = banks[ai]
                                pb, B_pb = banks[bi]

                                def mm_up(pa=pa, pb=pb, j=j):
                                    ins = None
                                    for k in range(8):
                                        nc.tensor.matmul(pa[:, :], lhsT=wup[:, k, j * 128:(j + 1) * 128], rhs=hTc[:, k, :], start=(k == 0), stop=(k == 7))
                                    for k in range(8):
                                        ins = nc.tensor.matmul(pb[:, :], lhsT=wup[:, k, D_FF + j * 128:D_FF + (j + 1) * 128], rhs=hTc[:, k, :],
                                                               start=(k == 0), stop=(k == 7))
                                    return ins
                                op("pe", mm_up, reads=[B_wup, B_hTc], writes=[B_pa, B_pb])
                                sat, B_sat = sa[sai % 2], B_sa[sai % 2]
                                sai += 1
                                op("act", lambda pa=pa, sat=sat: nc.scalar.activation(out=sat[:, :], in_=pa[:, :], func=AF.Silu),
                                   reads=[B_pa], writes=[B_sat])
                                op("dve", lambda pb=pb, sat=sat, half=half, jj=jj: nc.vector.tensor_tensor(out=uT[half][:, jj, :], in0=sat[:, :], in1=pb[:, :], op=ALU.mult),
                                   reads=[B_sat, B_pb], writes=[B_uT[half]], partial=True)
                            if half == 1 and c + 1 < NCH:
                                nt(c + 1, 2)
                            for t in range(4):
                                for oh in range(2):
                                    yi = ybank.next()
                                    py, B_py = banks[yi]

                                    def mm_dn(py=py, t=t, oh=oh, half=half):
                                        ins = None
                                        for jj in range(11):
                                            ins = nc.tensor.matmul(py[:, :], lhsT=uT[half][:, jj, t * 128:(t + 1) * 128],
                                                                   rhs=wdn[:, half * 11 + jj, oh * 512:(oh + 1) * 512], start=(jj == 0), stop=(jj == 10))
                                        return ins
                                    op("pe", mm_dn, reads=[B_uT[half], B_wdn], writes=[B_py])
                                    op("dve", lambda py=py, t=t, oh=oh: nc.vector.scalar_tensor_tensor(
                                        out=xc_[:, t, oh * 512:(oh + 1) * 512], in0=py[:, :], scalar=0.5, in1=xc_[:, t, oh * 512:(oh + 1) * 512],
                                        op0=ALU.mult, op1=ALU.add), reads=[B_py, B_xc_], writes=[B_xc_], partial=True)
                        for t_ in range(4):
                            dma(xchunk(c)[:, t_, :], xc_[:, t_, :], reads=[B_xc_], writes=[B_x[c]], partial=(t_ > 0))
                    fw.sync_all()


            def phase_hT():
                with ExitStack() as ph:
                    xcs = [sbt(ph, "hxc%d" % i, [128, 4, D], F32) for i in range(2)]
                    B_xcs = [fw.buf("hxc0"), fw.buf("hxc1")]
                    xn = sbt(ph, "hxn", [128, 4, D], BF16)
                    stat = sbt(ph, "hstat", [128, 32], F32)
                    B_xn, B_stat = fw.buf("hxn"), fw.buf("hstat")
                    for c in range(NCH):
                        xc, B_xc = xcs[c % 2], B_xcs[c % 2]
                        dma(xc[:, :, :], xchunk(c), reads=[B_x[c]], writes=[B_xc])
                        norm_transpose(xc, B_xc, 4, xn, B_xn, stat, B_stat, G_MIX,
                                       lambda fc, c=c: hTf[:, fc, c * 512:(c + 1) * 512], B_hTf, [0, 1], 512)
                    if cfg.debug:
                        for fc in range(8):
                            dma(hdbg[fc, :, :], hTf[:, fc, :], reads=[B_hTf])
                    fw.sync_all()

            def alloc_qkv(ph, need_v=True):
                o = {}
                o["wq"] = [sbt(ph, "wqkv%d" % i, [128, 8, 384], BF16) for i in range(2)]
                o["bt"] = [sbt(ph, "bt%d" % i, [128, 384], F32) for i in range(2)]
                o["B_wq"] = [fw.buf("wq0"), fw.buf("wq1")]
                o["B_bt"] = [fw.buf("bt0"), fw.buf("bt1")]
                o["QT"] = sbt(ph, "QT", [128, S], BF16)
                o["KT"] = sbt(ph, "KT", [128, S], BF16)
                o["V"] = sbt(ph, "V", [128, NT, 128], BF16) if need_v else None
                for k in ("QT", "KT", "V"):
                    o["B_" + k] = fw.buf(k)
                NR = 3
                o["NR"] = NR
                o["qkv"] = [sbt(ph, "qkv%d" % i, [128, 384], F32) for i in range(NR)]
                o["sq"] = [sbt(ph, "sqt%d" % i, [128, 256], F32) for i in range(2)]
                o["qn"] = [sbt(ph, "qn%d" % i, [128, 256], BF16) for i in range(NR)]
                o["st"] = [sbt(ph, "qst%d" % i, [128, 16], F32) for i in range(NR)]
                o["B_qkv"] = [fw.buf("qkv%d" % i) for i in range(NR)]
                o["B_sq"] = [fw.buf("sq0"), fw.buf("sq1")]
                o["B_qn"] = [fw.buf("qn%d" % i) for i in range(NR)]
                o["B_st"] = [fw.buf("qst%d" % i) for i in range(NR)]
                return o

            def load_qkv_w(o, slot, cq, ck, cv):
                wq, bt = o["wq"][slot], o["bt"][slot]
                for i, c0 in enumerate((cq, ck, cv)):
                    dma(wq[:, :, i * 128:(i + 1) * 128], wl("w_in", slice(None), slice(c0, c0 + 128)).rearrange("(k p) c -> p k c", p=128),
                        writes=[o["B_wq"][slot]], partial=True)
                    dma(bt[:, i * 128:(i + 1) * 128],
                        wl("b_in", slice(c0, c0 + 128)).rearrange("(o n) -> o n", o=1).partition_broadcast(128),
                        writes=[o["B_bt"][slot]], partial=True)

            def qkv_project(o, slot, gqc, gkc, vdst=None):
                wq, bt, QT, KT, V = o["wq"][slot], o["bt"][slot], o["QT"], o["KT"], o["V"]
                B_wq, B_bt = o["B_wq"][slot], o["B_bt"][slot]
                NR = o["NR"]
                ptr = Ring([2, 3])
                ptbs = {}

                def stM(t):
                    pp, B_pp = banks[t % 2]

                    def mm():
                        ins = None
                        for k in range(8):
                            ins = nc.tensor.matmul(pp[:, 0:384], lhsT=hTf[:, k, t * 128:(t + 1) * 128], rhs=wq[:, k, :], start=(k == 0), stop=(k == 7))
                        return ins
                    op("pe", mm, reads=[B_hTf, B_wq], writes=[B_pp])

                def stA(t):
                    pp, B_pp = banks[t % 2]
                    qkv, B_qkv = o["qkv"][t % NR], o["B_qkv"][t % NR]
                    stt_, B_st = o["st"][t % NR], o["B_st"][t % NR]
                    sq, B_sq = o["sq"][t % 2], o["B_sq"][t % 2]
                    op("dve", lambda: nc.vector.tensor_tensor(out=qkv[:, :], in0=pp[:, 0:384], in1=bt[:, :], op=ALU.add), reads=[B_pp, B_bt], writes=[B_qkv])
                    for g_ in range(4):
                        op("act", lambda g_=g_: nc.scalar.activation(out=sq[:, g_ * 64:(g_ + 1) * 64], in_=qkv[:, g_ * 64:(g_ + 1) * 64], func=AF.Square,
                                                                     accum_out=stt_[:, g_:g_ + 1]),
                           reads=[B_qkv], writes=[B_sq, B_st], partial=True)
                    op("dve", lambda: nc.vector.tensor_scalar(out=stt_[:, 4:8], in0=stt_[:, 0:4], scalar1=1.0 / HD, scalar2=EPS, op0=ALU.mult, op1=ALU.add),
                       reads=[B_st], writes=[B_st], partial=True)
                    op("act", lambda: nc.scalar.activation(out=stt_[:, 8:12], in_=stt_[:, 4:8], func=AF.Sqrt), reads=[B_st], writes=[B_st], partial=True)
                    if vdst is None:
                        op("act", lambda: nc.scalar.copy(out=V[:, t, :], in_=qkv[:, 256:384]), reads=[B_qkv], writes=[o["B_V"]], partial=True)
                    else:
                        vt_, B_vt_ = vdst
                        op("act", lambda: nc.scalar.copy(out=vt_[:, t, :, 0:64], in_=qkv[:, 256:384].rearrange("p (e d) -> p e d", e=2)),
                           reads=[B_qkv], writes=[B_vt_], partial=True)

                def stB(t):
                    qkv, B_qkv = o["qkv"][t % NR], o["B_qkv"][t % NR]
                    qn, B_qn = o["qn"][t % NR], o["B_qn"][t % NR]
                    stt_, B_st = o["st"][t % NR], o["B_st"][t % NR]
                    op("dve", lambda: nc.vector.reciprocal(out=stt_[:, 12:16], in_=stt_[:, 8:12]), reads=[B_st], writes=[B_st], partial=True)
                    op("dve", lambda: nc.vector.tensor_tensor(
                        out=qn[:, :].rearrange("p (g d) -> p g d", g=4), in0=qkv[:, 0:256].rearrange("p (g d) -> p g d", g=4),
                        in1=stt_[:, 12:16].rearrange("p (g o) -> p g o", o=1).to_broadcast([128, 4, HD]), op=ALU.mult),
                       reads=[B_qkv, B_st], writes=[B_qn])

                def stT(t):
                    qn, B_qn = o["qn"][t % NR], o["B_qn"][t % NR]
                    if t % 4 == 0:
                        ti = ptr.next()
                        ptbs[t // 4] = (bank_bf(ti), banks[ti][1])
                    ptb, B_ptb = ptbs[t // 4]

                    def tr():
                        nc.tensor.transpose(out=ptb[:, (t % 4) * 128:(t % 4 + 1) * 128], in_=qn[:, 0:128], identity=identb[:, :])
                        return nc.tensor.transpose(out=ptb[:, 512 + (t % 4) * 128:512 + (t % 4 + 1) * 128], in_=qn[:, 128:256], identity=identb[:, :])
                    op("pe", tr, reads=[B_qn, B_identb], writes=[B_ptb], partial=True)
                    if t % 4 == 3:
                        t0 = t - 3
                        op("act", lambda: nc.scalar.activation(out=QT[:, t0 * 128:(t0 + 4) * 128], in_=ptb[:, 0:512], func=AF.Copy, scale=small[:, gqc:gqc + 1]),
                           reads=[B_ptb, B_small], writes=[o["B_QT"]], partial=True)
                        op("dve", lambda: nc.vector.tensor_scalar(out=KT[:, t0 * 128:(t0 + 4) * 128], in0=ptb[:, 512:1024],
                                                                  scalar1=small[:, gkc:gkc + 1], scalar2=None, op0=ALU.mult),
                           reads=[B_ptb, B_small], writes=[o["B_KT"]], partial=True)
                for s_ in range(NT + 3):
                    if s_ < NT:
                        stM(s_)
                    if 0 <= s_ - 1 < NT:
                        stA(s_ - 1)
                    if 0 <= s_ - 2 < NT:
                        stB(s_ - 2)
                    if 0 <= s_ - 3 < NT:
                        stT(s_ - 3)

            def phase_A():
                with ExitStack() as ph:
                    o = alloc_qkv(ph)
                    QT, KT, V = o["QT"], o["KT"], o["V"]
                    ib = sbt(ph, "ibias", [128, 4, 512], I32)
                    Bneg = sbt(ph, "Bneg", [128, 512], F32)
                    Bdiag = sbt(ph, "Bdiag", [128, 4, 512], F32)
                    B_ib, B_Bneg, B_Bdiag = fw.buf("ib"), fw.buf("Bneg"), fw.buf("Bdiag")
                    op("pool", lambda: nc.gpsimd.iota(ib[:, 0, :], pattern=[[-1, 512]], base=0, channel_multiplier=1), writes=[B_ib])
                    op("dve", lambda: nc.vector.tensor_copy(out=Bneg[:, :], in_=ib[:, 0, :]), reads=[B_ib], writes=[B_Bneg])
                    op("pool", lambda: nc.gpsimd.iota(ib[:, :, :], pattern=[[-128, 4], [1, 512]], base=0, channel_multiplier=-1), reads=[], writes=[B_ib])
                    op("dve", lambda: nc.vector.tensor_copy(out=Bdiag[:, :, :], in_=ib[:, :, :]), reads=[B_ib], writes=[B_Bdiag])
                    for tt_ in range(4):
                        op("dve", lambda tt_=tt_: nc.vector.scalar_tensor_tensor(out=Bdiag[:, tt_, :], in0=Bdiag[:, tt_, :], scalar=-1.0, in1=Bdiag[:, tt_, :], op0=ALU.mult, op1=ALU.min),
                           reads=[B_Bdiag], writes=[B_Bdiag], partial=True)
                    QA = KA = None
                    B_QA, B_KA = fw.buf("QAaug"), fw.buf("KAaug")
                    tts = [sbt(ph, "att%d" % i, [128, 2, 512], F32) for i in range(2)]
                    B_tts = [fw.buf("att0"), fw.buf("att1")]
                    NPT = 4
                    PTs = [sbt(ph, "aPT%d" % i, [128, 2, 512], BF16) for i in range(NPT)]
                    B_PTs = [fw.buf("aPT%d" % i) for i in range(NPT)]
                    f = [sbt(ph, "af%d" % i, [128, 512], F32) for i in range(4)]
                    B_f = [fw.buf("af%d" % i) for i in range(4)]
                    sqb = sbt(ph, "asqb", [128, 512], BF16)
                    B_sqb = fw.buf("asqb")
                    yaTs = [sbt(ph, "ayaT%d" % i, [128, 512], BF16) for i in range(2)]
                    B_yaTs = [fw.buf("ayaT0"), fw.buf("ayaT1")]
                    ycount = 0
                    if cfg.a_heads:
                        h0_ = cfg.a_heads[0]
                        load_qkv_w(o, 0, OFF_AQ + h0_ * 128, OFF_AK + h0_ * 128, OFF_AV + h0_ * 128)
                    for hi_, h in enumerate(cfg.a_heads):
                        if hi_ + 1 < len(cfg.a_heads):
                            hn_ = cfg.a_heads[hi_ + 1]
                            load_qkv_w(o, (hi_ + 1) % 2, OFF_AQ + hn_ * 128, OFF_AK + hn_ * 128, OFF_AV + hn_ * 128)
                        qkv_project(o, hi_ % 2, 0, 1)
                        m = SLOPES_A[h]
                        SKIP_T = 120.0
                        kept = {}
                        for qc in range(NCH):
                            kl = []
                            for kt in range(NT):
                                q0_, k0_ = qc * 512, kt * 128
                                dmin = max(q0_ - (k0_ + 127), k0_ - (q0_ + 511), 0)
                                if m * dmin <= SKIP_T:
                                    kl.append(kt)
                            kept[qc] = kl
                        steps = [(qc, kt) for qc in range(NCH) for kt in kept[qc]]
                        spair = Ring([0, 1])
                        pend = {}

                        USE_AUG = False

                        def side_of(qc, kt):
                            q0, k0 = qc * 512, kt * 128
                            if k0 + 128 <= q0:
                                return 0
                            if k0 >= q0 + 512:
                                return 1
                            return -1

                        def emit_S(idx):
                            qc, kt = steps[idx]
                            di = spair.next()
                            pend[idx] = di
                            sd = side_of(qc, kt) if USE_AUG else -1

                            def mmS(di=di, qc=qc, kt=kt):
                                ins = None
                                for g in range(2):
                                    ins = nc.tensor.matmul(dbanks[di][:, g * 512:(g + 1) * 512], lhsT=KT[g * 64:(g + 1) * 64, kt * 128:(kt + 1) * 128],
                                                           rhs=QT[g * 64:(g + 1) * 64, qc * 512:(qc + 1) * 512], start=True, stop=(sd < 0))
                                    if sd >= 0:
                                        ins = nc.tensor.matmul(dbanks[di][:, g * 512:(g + 1) * 512], lhsT=KA[:, h, sd, :], rhs=QA[:, h, :], start=False, stop=True)
                                return ins
                            op("pe", mmS, reads=[o["B_QT"], o["B_KT"], B_QA, B_KA], writes=[banks[2 * di][1], banks[2 * di + 1][1]])
                        emit_S(0)
                        backs = []
                        LAG = 1
                        ycount_box = [ycount]
                        for idx, (qc, kt) in enumerate(steps):
                            if idx + 1 < len(steps):
                                emit_S(idx + 1)
                            di = pend.pop(idx)
                            q0, k0 = qc * 512, kt * 128
                            tt, B_tt = tts[idx % 2], B_tts[idx % 2]
                            PT, B_PT = PTs[idx % NPT], B_PTs[idx % NPT]
                            sd = side_of(qc, kt)
                            if sd < 0:
                                tab, B_tab = Bdiag[:, (k0 - q0) // 128, :], B_Bdiag
                                op("dve", lambda tab=tab, di=di, tt=tt: nc.vector.scalar_tensor_tensor(
                                    out=tt[:, :, :], in0=tab.rearrange("p (o n) -> p o n", o=1).to_broadcast([128, 2, 512]), scalar=float(m),
                                    in1=dbanks[di][:, :].rearrange("p (g n) -> p g n", g=2), op0=ALU.mult, op1=ALU.add),
                                   reads=[B_tab, banks[2 * di][1], banks[2 * di + 1][1]], writes=[B_tt])
                                op("act", lambda tt=tt, PT=PT: nc.scalar.activation(out=PT[:, :, :], in_=tt[:, :, :], func=AF.Exp),
                                   reads=[B_tt], writes=[B_PT])
                            elif not USE_AUG:
                                cst = -m * (q0 - k0) if sd == 0 else -m * (k0 - q0)
                                sc = m if sd == 0 else -m
                                op("dve", lambda di=di, tt=tt, sc=sc: nc.vector.scalar_tensor_tensor(
                                    out=tt[:, :, :], in0=Bneg[:, :].rearrange("p (o n) -> p o n", o=1).to_broadcast([128, 2, 512]), scalar=float(sc),
                                    in1=dbanks[di][:, :].rearrange("p (g n) -> p g n", g=2), op0=ALU.mult, op1=ALU.add),
                                   reads=[B_Bneg, banks[2 * di][1], banks[2 * di + 1][1]], writes=[B_tt])
                                op("act", lambda tt=tt, PT=PT, cst=cst: nc.scalar.activation(out=PT[:, :, :], in_=tt[:, :, :], func=AF.Exp, bias=float(cst)),
                                   reads=[B_tt], writes=[B_PT])
                            else:
                                cst = -m * (q0 - k0) if sd == 0 else -m * (k0 - q0)
                                op("act", lambda di=di, PT=PT, cst=cst: nc.scalar.activation(
                                    out=PT[:, :, :], in_=dbanks[di][:, :].rearrange("p (g n) -> p g n", g=2), func=AF.Exp, bias=float(cst)),
                                   reads=[banks[2 * di][1], banks[2 * di + 1][1]], writes=[B_PT])

                            def back(idx=idx, qc=qc, kt=kt, q0=q0, PT=PT, B_PT=B_PT):
                                def mmPV(kt=kt, PT=PT):
                                    ins = None
                                    for g in range(2):
                                        nc.tensor.matmul(banks[4 + 2 * g][0][:, :], lhsT=V[:, kt, :], rhs=PT[:, g, :], start=(kt == kept[qc][0]), stop=(kt == kept[qc][-1]))
                                        ins = nc.tensor.matmul(banks[5 + 2 * g][0][:, :], lhsT=onesb[:, :], rhs=PT[:, g, :], start=(kt == kept[qc][0]), stop=(kt == kept[qc][-1]))
                                    return ins
                                op("pe", mmPV, reads=[o["B_V"], B_PT, B_onesb], writes=[banks[4][1], banks[5][1], banks[6][1], banks[7][1]])
                                if kt == kept[qc][-1]:
                                    O0, D0, O1, D1 = banks[4][0], banks[5][0], banks[6][0], banks[7][0]
                                    B_O0, B_D0, B_O1, B_D1 = banks[4][1], banks[5][1], banks[6][1], banks[7][1]
                                    op("act", lambda: nc.scalar.activation(out=f[0][:, :], in_=D0[:, :], func=AF.Ln), reads=[B_D0], writes=[B_f[0]])
                                    op("act", lambda: nc.scalar.activation(out=f[1][:, :], in_=D1[:, :], func=AF.Ln), reads=[B_D1], writes=[B_f[1]])
                                    op("act", lambda: nc.scalar.activation(out=f[0][:, :], in_=f[0][:, :], func=AF.Exp, scale=-1.0), reads=[B_f[0]], writes=[B_f[0]])
                                    op("act", lambda: nc.scalar.activation(out=f[1][:, :], in_=f[1][:, :], func=AF.Exp, scale=-1.0), reads=[B_f[1]], writes=[B_f[1]])
                                    op("dve", lambda: nc.vector.tensor_tensor(out=f[2][:, :], in0=O0[:, :], in1=f[0][:, :], op=ALU.mult), reads=[B_O0, B_f[0]], writes=[B_f[2]])
                                    op("dve", lambda: nc.vector.tensor_tensor(out=f[3][:, :], in0=O1[:, :], in1=f[1][:, :], op=ALU.mult), reads=[B_O1, B_f[1]], writes=[B_f[3]])
                                    op("dve", lambda: nc.vector.scalar_tensor_tensor(out=f[2][:, :], in0=f[3][:, :], scalar=small[:, 6:7], in1=f[2][:, :],
                                                                                     op0=ALU.mult, op1=ALU.add), reads=[B_f[3], B_f[2], B_small], writes=[B_f[2]])
                                    op("act", lambda: nc.scalar.activation(out=sqb[:, :], in_=f[2][:, :], func=AF.Square), reads=[B_f[2]], writes=[B_sqb])
                                    op("pe", lambda: nc.tensor.matmul(D0[:, :], lhsT=onesb[:, :], rhs=sqb[:, :], start=True, stop=True),
                                       reads=[B_sqb, B_onesb], writes=[B_D0])
                                    op("act", lambda: nc.scalar.activation(out=f[0][:, :], in_=D0[:, :], func=AF.Ln, scale=1.0 / 128, bias=EPS),
                                       reads=[B_D0], writes=[B_f[0]])
                                    op("act", lambda: nc.scalar.activation(out=f[0][:, :], in_=f[0][:, :], func=AF.Exp, scale=-0.5), reads=[B_f[0]], writes=[B_f[0]])
                                    yaT, B_yaT = yaTs[ycount_box[0] % 2], B_yaTs[ycount_box[0] % 2]
                                    ycount_box[0] += 1
                                    op("dve", lambda yaT=yaT: nc.vector.scalar_tensor_tensor(out=yaT[:, :], in0=f[2][:, :], scalar=small[:, 4:5], in1=f[0][:, :],
                                                                                             op0=ALU.mult, op1=ALU.mult), reads=[B_f[2], B_f[0], B_small], writes=[B_yaT])
                                    dma(ysc[h, :, q0:q0 + 512], yaT[:, :], reads=[B_yaT], writes=[B_ysc], partial=True)
                            backs.append(back)
                            if len(backs) > LAG:
                                backs.pop(0)()
                        while backs:
                            backs.pop(0)()
                        ycount = ycount_box[0]
                    fw.sync_all()

            def phase_B():
                with ExitStack() as ph:
                    o = alloc_qkv(ph, need_v=False)
                    QT, KT, V = o["QT"], o["KT"], o["V"]
                    Vc = {d_: sbt(ph, "Vb%d" % d_, [128, NT, 2, 128], BF16) for d_ in (1, 4, 16)}
                    B_Vc = {d_: fw.buf("Vb%d" % d_) for d_ in (1, 4, 16)}
                    for d_ in (1, 4, 16):
                        op("pool", lambda d_=d_: nc.gpsimd.memset(Vc[d_][:, :, :, 64:128], 1.0), writes=[B_Vc[d_]], partial=True)
                    Bband = sbt(ph, "Bband", [128, 384], F32)
                    ibb = sbt(ph, "ibband", [128, 384], I32)
                    btmp = ibb[:, :].bitcast(F32)
                    B_ibb, B_Bband, B_btmp = fw.buf("ibb"), fw.buf("Bband"), fw.buf("btmp")
                    op("pool", lambda: nc.gpsimd.iota(ibb[:, :], pattern=[[1, 384]], base=-128, channel_multiplier=-1), writes=[B_ibb])
                    op("dve", lambda: nc.vector.tensor_copy(out=Bband[:, :], in_=ibb[:, :]), reads=[B_ibb], writes=[B_Bband])
                    op("dve", lambda: nc.vector.scalar_tensor_tensor(out=Bband[:, :], in0=Bband[:, :], scalar=-1.0, in1=Bband[:, :], op0=ALU.mult, op1=ALU.max),
                       reads=[B_Bband], writes=[B_Bband])
                    op("dve", lambda: nc.vector.tensor_scalar(out=btmp[:, :], in0=Bband[:, :], scalar1=64.0, scalar2=-NEG_BIG, op0=ALU.is_gt, op1=ALU.mult),
                       reads=[B_Bband], writes=[B_btmp])
                    op("dve", lambda: nc.vector.tensor_tensor(out=Bband[:, :], in0=Bband[:, :], in1=btmp[:, :], op=ALU.add), reads=[B_Bband, B_btmp], writes=[B_Bband])
                    op("dve", lambda: nc.vector.tensor_scalar(out=Bband[:, :], in0=Bband[:, :], scalar1=-1.0, scalar2=None, op0=ALU.mult),
                       reads=[B_Bband], writes=[B_Bband])
                    tts = [sbt(ph, "btt%d" % i, [128, 384], F32) for i in range(3)]
                    B_tts = [fw.buf("btt%d" % i) for i in range(3)]
                    PTs = [sbt(ph, "bPT%d" % i, [128, 384], BF16) for i in range(8)]
                    B_PTs = [fw.buf("bPT%d" % i) for i in range(8)]
                    acc = sbt(ph, "accB", [128, S], F32)
                    rden = sbt(ph, "rdenB", [64, 2048], F32)
                    ybT = sbt(ph, "ybT", [64, S], BF16)
                    B_acc, B_rden, B_ybT = fw.buf("accB"), fw.buf("rdenB"), fw.buf("ybT")
                    if cfg.b_pairs:
                        p0_ = cfg.b_pairs[0]
                        load_qkv_w(o, 0, OFF_BQ + p0_ * 128, OFF_BK + p0_ * 128, OFF_BV + p0_ * 128)
                    for pi_, hp in enumerate(cfg.b_pairs):
                        if pi_ + 1 < len(cfg.b_pairs):
                            pn_ = cfg.b_pairs[pi_ + 1]
                            load_qkv_w(o, (pi_ + 1) % 2, OFF_BQ + pn_ * 128, OFF_BK + pn_ * 128, OFF_BV + pn_ * 128)
                        qkv_project(o, pi_ % 2, 2, 3, vdst=(Vc[1], B_Vc[1]))
                        wq, bt = o["wq"][pi_ % 2], o["bt"][pi_ % 2]
                        B_wq_, B_bt_ = o["B_wq"][pi_ % 2], o["B_bt"][pi_ % 2]
                        vbr = Ring([0, 1])
                        for dil in (4, 16):
                            U = S // dil
                            nut = U // 128
                            tiles = [(r, ut) for r in range(dil) for ut in range(nut)]
                            for g0 in range(0, len(tiles), 4):
                                grp = tiles[g0:g0 + 4]
                                bi = vbr.next()
                                pv, B_pv = banks[bi]

                                def mmv(grp=grp, pv=pv, dil=dil):
                                    ins = None
                                    for j, (r, ut) in enumerate(grp):
                                        c0 = r + dil * ut * 128
                                        for k in range(8):
                                            ins = nc.tensor.matmul(pv[:, j * 128:(j + 1) * 128], lhsT=hTf[:, k, c0:c0 + dil * 127 + 1:dil],
                                                                   rhs=wq[:, k, 256:384], start=(k == 0), stop=(k == 7))
                                    return ins
                                op("pe", mmv, reads=[B_hTf, B_wq_], writes=[B_pv])
                                ng = len(grp)
                                op("dve", lambda pv=pv, g0=g0, ng=ng, dil=dil: nc.vector.tensor_tensor(
                                    out=Vc[dil][:, g0:g0 + ng, :, 0:64], in0=pv[:, 0:ng * 128].rearrange("p (j e c) -> p j e c", j=ng, e=2),
                                    in1=bt[:, 256:384].rearrange("p (o e c) -> p o e c", o=1, e=2).to_broadcast([128, ng, 2, 64]), op=ALU.add),
                                   reads=[B_pv, B_bt_], writes=[B_Vc[dil]], partial=True)
                        for e in range(2):
                            m = SLOPES_B[hp * 2 + e]
                            sring = Ring([0, 1, 4, 5, 6, 7])
                            LOOK = 3
                            steps = []
                            for (_win, dil) in PATTERNS:
                                nkt = (S // dil) // 128
                                for r in range(dil):
                                    for kt in range(nkt):
                                        steps.append((dil, r, kt, nkt))

                            def ccols(dil, r, u0, n):
                                c0 = r + dil * u0
                                return slice(c0, c0 + dil * (n - 1) + 1, dil)
                            sbank = {}
                            nxt = [0]

                            def emit_S(i):
                                dil, r, kt, nkt = steps[i]
                                qa, qb = max(kt - 1, 0), min(kt + 1, nkt - 1)
                                nq = (qb - qa + 1) * 128
                                si = sring.next()
                                sbank[i] = si
                                pS, B_pS = banks[si]
                                op("pe", lambda: nc.tensor.matmul(
                                    pS[:, 0:nq], lhsT=KT[e * 64:(e + 1) * 64, ccols(dil, r, kt * 128, 128)], rhs=QT[e * 64:(e + 1) * 64, ccols(dil, r, qa * 128, nq)],
                                    start=True, stop=True), reads=[o["B_QT"], o["B_KT"]], writes=[B_pS])
                            hist = {}
                            segctr = {}
                            nsegs = [0]
                            bq = []
                            LAGB = 2
                            for i, (dil, r, kt, nkt) in enumerate(steps):
                                while nxt[0] <= min(i + LOOK, len(steps) - 1):
                                    emit_S(nxt[0])
                                    nxt[0] += 1
                                qa, qb = max(kt - 1, 0), min(kt + 1, nkt - 1)
                                nq = (qb - qa + 1) * 128
                                pS, B_pS = banks[sbank.pop(i)]
                                tt, B_tt = tts[i % 3], B_tts[i % 3]
                                PT, B_PT = PTs[i % 8], B_PTs[i % 8]
                                boff = (qa - (kt - 1)) * 128
                                op("dve", lambda: nc.vector.scalar_tensor_tensor(
                                    out=tt[:, 0:nq], in0=Bband[:, boff:boff + nq], scalar=float(m * dil), in1=pS[:, 0:nq], op0=ALU.mult, op1=ALU.add),
                                   reads=[B_Bband, B_pS], writes=[B_tt])
                                op("act", lambda: nc.scalar.activation(out=PT[:, 0:nq], in_=tt[:, 0:nq], func=AF.Exp), reads=[B_tt], writes=[B_PT])
                                hist[(dil, r, kt)] = (PT, B_PT, qa)
                                qts = []
                                if kt >= 1:
                                    qts.append(kt - 1)
                                if kt == nkt - 1:
                                    qts.append(kt)
                                for qt in qts:
                                    bq.append((dil, r, nkt, qt))
                                while len(bq) > 0 and (len(bq) > LAGB * 2 or i == len(steps) - 1):
                                    dil, r, nkt, qt = bq.pop(0)
                                    seg = qt // 4
                                    if (dil, r, seg) not in segctr:
                                        segctr[(dil, r, seg)] = nsegs[0]
                                        nsegs[0] += 1
                                    par = segctr[(dil, r, seg)] % 2
                                    bN, B_bN = banks[2 + par]
                                    col = (qt % 4) * 128
                                    kts = list(range(max(qt - 1, 0), min(qt + 1, nkt - 1) + 1))

                                    def mmPV():
                                        ins = None
                                        for i_, k_ in enumerate(kts):
                                            PTk, _b, qak = hist[(dil, r, k_)]
                                            rhs = PTk[:, (qt - qak) * 128:(qt - qak + 1) * 128]
                                            ins = nc.tensor.matmul(bN[:, col:col + 128], lhsT=Vc[dil][:, r * nkt + k_, e, :], rhs=rhs,
                                                                   start=(i_ == 0), stop=(i_ == len(kts) - 1))
                                        return ins
                                    op("pe", mmPV, reads=[B_Vc[dil]] + [hist[(dil, r, k_)][1] for k_ in kts], writes=[B_bN], partial=True)
                                    if qt % 4 == 3 or qt == nkt - 1:
                                        ncols = (qt - seg * 4 + 1) * 128
                                        dst = ccols(dil, r, seg * 512, ncols)
                                        if dil == 1:
                                            op("act", lambda: nc.scalar.copy(out=acc[:, dst], in_=bN[:, 0:ncols]), reads=[B_bN], writes=[B_acc], partial=True)
                                        else:
                                            op("dve", lambda: nc.vector.tensor_tensor(out=acc[:, dst], in0=bN[:, 0:ncols], in1=acc[:, dst], op=ALU.add),
                                               reads=[B_bN, B_acc], writes=[B_acc], partial=True)
                            for c0 in range(0, S, 2048):
                                op("act", lambda c0=c0: nc.scalar.activation(out=acc[64:128, c0:c0 + 2048], in_=acc[64:128, c0:c0 + 2048], func=AF.Ln),
                                   reads=[B_acc], writes=[B_acc], partial=True)
                                op("act", lambda c0=c0: nc.scalar.activation(out=acc[64:128, c0:c0 + 2048], in_=acc[64:128, c0:c0 + 2048], func=AF.Exp, scale=-1.0),
                                   reads=[B_acc], writes=[B_acc], partial=True)
                            for c0 in range(0, S, 2048):
                                dma(rden[:, :], acc[64:128, c0:c0 + 2048], reads=[B_acc], writes=[B_rden])
                                op("dve", lambda c0=c0: nc.vector.tensor_tensor(out=ybT[:, c0:c0 + 2048], in0=acc[0:64, c0:c0 + 2048], in1=rden[:, :], op=ALU.mult),
                                   reads=[B_acc, B_rden], writes=[B_ybT], partial=True)
                            dma(ysc[8 + hp, e * 64:(e + 1) * 64, :], ybT[:, :], reads=[B_ybT], writes=[B_ysc], partial=True)
                    fw.sync_all()

            def phase_C():
                with ExitStack() as ph:
                    zT = sbt(ph, "zT", [128, 6, S + 30], BF16)
                    B_zT = fw.buf("zT")
                    op("pool", lambda: nc.gpsimd.memset(zT[:, :, 0:15], 0.0), writes=[B_zT], partial=True)
                    op("pool", lambda: nc.gpsimd.memset(zT[:, :, S + 15:S + 30], 0.0), writes=[B_zT], partial=True)
                    wcv = sbt(ph, "wcv", [128, 192], F32)
                    diagW = sbt(ph, "diagW", [128, 186, 128], BF16)
                    B_wcv, B_diagW = fw.buf("wcv"), fw.buf("diagW")
                    with ExitStack() as ph1:
                        stgc = sbt(ph1, "stgc", [96, 2, 128], F32)
                        B_stgc = fw.buf("stgc")
                        src = wl("c_dw_w").rearrange("j (ct p) -> (j ct) p", p=128)
                        dma(stgc[0:96, 0, :], src[0:96, :], writes=[B_stgc], partial=True)
                        dma(stgc[0:90, 1, :], src[96:186, :], writes=[B_stgc], partial=True)
                        pt, B_pt = banks[0]
                        op("pe", lambda: nc.tensor.transpose(out=pt[:, 0:96], in_=stgc[0:96, 0, :], identity=identf[0:96, 0:96]),
                           reads=[B_stgc, B_identf], writes=[B_pt], partial=True)
                        op("pe", lambda: nc.tensor.transpose(out=pt[:, 96:186], in_=stgc[0:90, 1, :], identity=identf[0:90, 0:90]),
                           reads=[B_stgc, B_identf], writes=[B_pt], partial=True)
                        op("act", lambda: nc.scalar.copy(out=wcv[:, 0:186], in_=pt[:, 0:186]), reads=[B_pt], writes=[B_wcv])
                        for idx in range(186):
                            eng = "dve" if idx % 2 == 0 else "pool"
                            e_ = nc.vector if eng == "dve" else nc.gpsimd
                            op(eng, lambda idx=idx, e_=e_: e_.tensor_scalar(out=diagW[:, idx, :], in0=identb[:, :], scalar1=wcv[:, idx:idx + 1], scalar2=None, op0=ALU.mult),
                               reads=[B_identb, B_wcv], writes=[B_diagW], partial=True)
                        wc = sbt(ph1, "wc", [128, 8, 1536], BF16)
                        B_wc = fw.buf("wc")
                        stgs = None
                        load_weight(wc, B_wc, lambda k, c0, c1: wl("w_in", slice(k * 128, (k + 1) * 128), slice(OFF_CU + c0, OFF_CU + c1)), 8, 1536, stgs, 1536)
                        sgs = [sbt(ph1, "csg%d" % i, [128, 512], F32) for i in range(2)]
                        B_sgs = [fw.buf("csg0"), fw.buf("csg1")]
                        ar, gr = Ring([0, 1]), Ring([2, 3])
                        n = 0
                        for c in range(NCH):
                            for ct in range(6):
                                pa, B_pa = banks[ar.next()]
                                pg, B_pg = banks[gr.next()]

                                def mm(pa=pa, pg=pg, ct=ct, c=c):
                                    ins = None
                                    for k in range(8):
                                        nc.tensor.matmul(pa[:, :], lhsT=wc[:, k, ct * 128:(ct + 1) * 128], rhs=hTf[:, k, c * 512:(c + 1) * 512], start=(k == 0), stop=(k == 7))
                                    for k in range(8):
                                        ins = nc.tensor.matmul(pg[:, :], lhsT=wc[:, k, 768 + ct * 128:768 + (ct + 1) * 128], rhs=hTf[:, k, c * 512:(c + 1) * 512],
                                                               start=(k == 0), stop=(k == 7))
                                    return ins
                                op("pe", mm, reads=[B_wc, B_hTf], writes=[B_pa, B_pg])
                                sg, B_sg = sgs[n % 2], B_sgs[n % 2]
                                n += 1
                                op("act", lambda pg=pg, sg=sg, ct=ct: nc.scalar.activation(out=sg[:, :], in_=pg[:, :], func=AF.Sigmoid,
                                                                                         bias=vecT[:, V_BC + 6 + ct:V_BC + 7 + ct]),
                                   reads=[B_pg, B_vecT], writes=[B_sg])
                                op("dve", lambda pa=pa, sg=sg, ct=ct, c=c: nc.vector.scalar_tensor_tensor(
                                    out=zT[:, ct, 15 + c * 512:15 + (c + 1) * 512], in0=pa[:, :], scalar=vecT[:, V_BC + ct:V_BC + ct + 1], in1=sg[:, :],
                                    op0=ALU.add, op1=ALU.mult), reads=[B_pa, B_sg, B_vecT], writes=[B_zT], partial=True)
                        fw.sync_all()
                    sqs = [sbt(ph, "csq%d" % i, [128, 512], BF16) for i in range(2)]
                    B_sqs = [fw.buf("csq0"), fw.buf("csq1")]
                    rstd = sbt(ph, "crstd", [128, 512], F32)
                    B_rstd = fw.buf("crstd")
                    tmps = [sbt(ph, "ctmp%d" % i, [128, 512], F32) for i in range(2)]
                    B_tmps = [fw.buf("ctmp0"), fw.buf("ctmp1")]
                    ycs = [sbt(ph, "cyc%d" % i, [128, 512], BF16) for i in range(3)]
                    B_ycs = [fw.buf("cyc%d" % i) for i in range(3)]
                    n = 0
                    for c in range(NCH):
                        for ct in range(6):
                            pc, B_pc = banks[ct]

                            def mmc(pc=pc, ct=ct, c=c):
                                ins = None
                                for j in range(CONV_W):
                                    ins = nc.tensor.matmul(pc[:, :], lhsT=diagW[:, j * 6 + ct, :], rhs=zT[:, ct, c * 512 + j:c * 512 + j + 512],
                                                           start=(j == 0), stop=(j == CONV_W - 1))
                                return ins
                            op("pe", mmc, reads=[B_diagW, B_zT], writes=[B_pc])
                            sq, B_sq = sqs[ct % 2], B_sqs[ct % 2]
                            op("act", lambda pc=pc, sq=sq, ct=ct: nc.scalar.activation(out=sq[:, :], in_=pc[:, :], func=AF.Square,
                                                                                     bias=vecT[:, V_DWB + ct:V_DWB + ct + 1]),
                               reads=[B_pc, B_vecT], writes=[B_sq])
                            pr, B_pr = banks[6]
                            op("pe", lambda pr=pr, sq=sq, ct=ct: nc.tensor.matmul(pr[:, :], lhsT=onesb[:, :], rhs=sq[:, :], start=(ct == 0), stop=(ct == 5)),
                               reads=[B_sq, B_onesb], writes=[B_pr], partial=(ct != 0))
                        pr, B_pr = banks[6]
                        op("act", lambda pr=pr: nc.scalar.activation(out=rstd[:, :], in_=pr[:, :], func=AF.Ln, scale=1.0 / C_CH, bias=EPS), reads=[B_pr], writes=[B_rstd])
                        op("act", lambda: nc.scalar.activation(out=rstd[:, :], in_=rstd[:, :], func=AF.Exp, scale=-0.5), reads=[B_rstd], writes=[B_rstd])
                        for ct in range(6):
                            pc, B_pc = banks[ct]
                            tmp, B_tmp = tmps[ct % 2], B_tmps[ct % 2]
                            yc, B_yc = ycs[n % 3], B_ycs[n % 3]
                            n += 1
                            op("dve", lambda pc=pc, tmp=tmp, ct=ct: nc.vector.scalar_tensor_tensor(out=tmp[:, :], in0=pc[:, :], scalar=vecT[:, V_DWB + ct:V_DWB + ct + 1],
                                                                                                 in1=rstd[:, :], op0=ALU.add, op1=ALU.mult),
                               reads=[B_pc, B_rstd, B_vecT], writes=[B_tmp])
                            op("act", lambda tmp=tmp, yc=yc, ct=ct: nc.scalar.activation(out=yc[:, :], in_=tmp[:, :], func=AF.Silu, scale=vecT[:, V_CN + ct:V_CN + ct + 1]),
                               reads=[B_tmp, B_vecT], writes=[B_yc])
                            dma(ysc[14 + ct, :, c * 512:(c + 1) * 512], yc[:, :], reads=[B_yc], writes=[B_ysc], partial=True)
                    fw.sync_all()

            def phase_merge():
                CW = 256
                with ExitStack() as ph:
                    wg = sbt(ph, "wg", [128, 8, 3072], BF16)
                    woa = sbt(ph, "woa", [128, 8, D], BF16)
                    wob = sbt(ph, "wob", [128, 6, D], BF16)
                    woc = sbt(ph, "woc", [128, 6, D], BF16)
                    wo = sbt(ph, "wo", [128, 8, D], BF16)
                    B_wg, B_woa, B_wob, B_woc, B_wo = (fw.buf(n_) for n_ in ("wg", "woa", "wob", "woc", "wo"))
                    stgs = None
                    load_weight(woa, B_woa, lambda k, c0, c1: wl("w_out_a", slice(k * 128, (k + 1) * 128), slice(c0, c1)), 8, D, stgs, 1024)
                    load_weight(wob, B_wob, lambda k, c0, c1: wl("w_out_b", slice(k * 128, (k + 1) * 128), slice(c0, c1)), 6, D, stgs, 1024)
                    load_weight(woc, B_woc, lambda k, c0, c1: wl("w_out_c", slice(k * 128, (k + 1) * 128), slice(c0, c1)), 6, D, stgs, 1024)
                    load_weight(wg, B_wg, lambda k, c0, c1: wl("w_in", slice(k * 128, (k + 1) * 128), slice(OFF_G + c0, OFF_G + c1)), 8, 3072, stgs, 1024)
                    load_weight(wo, B_wo, lambda k, c0, c1: wl("w_out", slice(k * 128, (k + 1) * 128), slice(c0, c1)), 8, D, stgs, 1024)
                    yT = sbt(ph, "myT", [128, 20, CW], BF16)
                    xc = sbt(ph, "mxc", [128, CW // 128, D], F32)
                    mT = sbt(ph, "mmT", [128, 8, CW], BF16)
                    B_yT, B_xc, B_mT = fw.buf("myT"), fw.buf("mxc"), fw.buf("mmT")
                    gs = [sbt(ph, "mg%d" % i, [128, 3, CW], F32) for i in range(2)]
                    B_gs = [fw.buf("mg0"), fw.buf("mg1")]
                    m1s = [sbt(ph, "mm1%d" % i, [128, 3, CW], F32) for i in range(1)] * 2
                    B_m1s = [fw.buf("mm10")] * 2
                    zr = Ring([0, 3])
                    xr = Ring([6, 7])
                    nn = 0
                    ntl = CW // 128

                    def slot(base, i):
                        return banks[base + i // 2][0][:, (i % 2) * CW:(i % 2 + 1) * CW]
                    for c in range(S // CW):
                        c0 = c * CW
                        dma(yT[:, :, :], ysc[:, :, c0:c0 + CW].rearrange("c p n -> p c n"), reads=[B_ysc], writes=[B_yT])
                        dma(xc[:, :, :], xchunk(c, CW), reads=[B_x[c0 // 512]], writes=[B_xc])
                        for m in range(8):
                            base = zr.next()
                            B_db = [banks[base][1], banks[base + 1][1], banks[base + 2][1]]

                            def mmz(base=base, m=m, c0=c0):
                                ins = None
                                for br, (wt, nk, ko) in enumerate(((woa, 8, 0), (wob, 6, 8), (woc, 6, 14))):
                                    for k in range(nk):
                                        nc.tensor.matmul(slot(base, br), lhsT=wt[:, k, m * 128:(m + 1) * 128], rhs=yT[:, ko + k, :], start=(k == 0), stop=(k == nk - 1))
                                for br in range(3):
                                    for k in range(8):
                                        ins = nc.tensor.matmul(slot(base, 3 + br), lhsT=wg[:, k, br * D + m * 128:br * D + (m + 1) * 128], rhs=hTf[:, k, c0:c0 + CW],
                                                               start=(k == 0), stop=(k == 7))
                                return ins
                            op("pe", mmz, reads=[B_woa, B_wob, B_woc, B_wg, B_yT, B_hTf], writes=B_db)

                            def zslice(db, br, base=base):
                                return slot(base, br)

                            def gslice(db, br, base=base):
                                return slot(base, 3 + br)
                            db = None
                            g, B_g = gs[nn % 2], B_gs[nn % 2]
                            m1, B_m1 = m1s[nn % 2], B_m1s[nn % 2]
                            nn += 1
                            for br in range(3):
                                op("act", lambda db=db, g=g, br=br, m=m: nc.scalar.activation(out=g[:, br, :], in_=gslice(db, br), func=AF.Sigmoid,
                                                                                             bias=vecT[:, V_BG + br * 8 + m:V_BG + br * 8 + m + 1]),
                                   reads=B_db + [B_vecT], writes=[B_g], partial=True)
                            for br in range(3):
                                op("dve", lambda db=db, g=g, m1=m1, br=br: nc.vector.tensor_tensor(out=m1[:, br, :], in0=g[:, br, :], in1=zslice(db, br), op=ALU.mult),
                                   reads=B_db + [B_g], writes=[B_m1], partial=True)
                            op("dve", lambda m1=m1: nc.vector.tensor_tensor(out=m1[:, 0, :], in0=m1[:, 0, :], in1=m1[:, 1, :], op=ALU.add), reads=[B_m1], writes=[B_m1], partial=True)
                            op("dve", lambda m1=m1, m=m: nc.vector.tensor_tensor(out=mT[:, m, :], in0=m1[:, 0, :], in1=m1[:, 2, :], op=ALU.add),
                               reads=[B_m1], writes=[B_mT], partial=True)
                        for t in range(ntl):
                            for oh in range(2):
                                px, B_px = banks[xr.next()]

                                def mmx(px=px, t=t, oh=oh):
                                    ins = None
                                    for k in range(8):
                                        ins = nc.tensor.matmul(px[:, :], lhsT=mT[:, k, t * 128:(t + 1) * 128], rhs=wo[:, k, oh * 512:(oh + 1) * 512], start=(k == 0), stop=(k == 7))
                                    return ins
                                op("pe", mmx, reads=[B_mT, B_wo], writes=[B_px])
                                op("dve", lambda px=px, t=t, oh=oh: nc.vector.tensor_tensor(out=xc[:, t, oh * 512:(oh + 1) * 512], in0=px[:, :], in1=xc[:, t, oh * 512:(oh + 1) * 512], op=ALU.add),
                                   reads=[B_px, B_xc], writes=[B_xc], partial=True)
                        dma(xchunk(c, CW), xc[:, :, :], reads=[B_xc], writes=[B_x[c0 // 512]])
                    fw.sync_all()


            if "ffn1" in cfg.phases:
                phase_ffn(0)
            hT_stack = ExitStack()
            hTf = sbt(hT_stack, "hTf", [128, 8, S], BF16)
            B_hTf = fw.buf("hTf")
            if "hT" in cfg.phases:
                phase_hT()
            if "A" in cfg.phases:
                phase_A()
            if "B" in cfg.phases:
                phase_B()
            if "C" in cfg.phases:
                phase_C()
            if "merge" in cfg.phases:
                phase_merge()
            hT_stack.close()
            if "ffn2" in cfg.phases:
                phase_ffn(1)

        with nc.Fori(0, L) as l:
            nc.sync.dma_start(out=wcur[:, :], in_=wblob[bass.ds(l, 1), 0:PIECE_ROWS, :].rearrange("o r c -> (o r) c")).then_inc(s_wc, 16)
            nc.sync.wait_ge(s_wc, 16)
            with ExitStack() as cv:
                HALF = PIECE_ROWS // 2
                RPP = HALF // 128
                NEL = RPP * BLOB_COLS
                st32 = [sbt(cv, "cv32_%d" % i, [128, NEL], F32) for i in range(2)]
                st16 = [sbt(cv, "cv16_%d" % i, [128, NEL], BF16) for i in range(2)]
                B32 = [fw.buf("cv32_0"), fw.buf("cv32_1")]
                B16 = [fw.buf("cv16_0"), fw.buf("cv16_1")]
                with nc.Fori(0, NPIECE) as pi:
                    for hf in range(2):
                        r0 = pi * PIECE_ROWS + hf * HALF
                        dma(st32[hf][:, :], wblob[bass.ds(l, 1), bass.ds(r0, HALF), :].rearrange("o (p r) c -> p (o r c)", p=128), writes=[B32[hf]])
                    for hf in range(2):
                        NCK = 8
                        CK = NEL // NCK
                        for ck in range(NCK):
                            ek = ("act", "dve", "pool", "act", "dve", "act", "dve", "pool")[ck]
                            if ek == "act":
                                op("act", lambda: nc.scalar.copy(out=st16[hf][:, ck * CK:(ck + 1) * CK], in_=st32[hf][:, ck * CK:(ck + 1) * CK]),
                                   reads=[B32[hf]], writes=[B16[hf]], partial=True)
                            elif ek == "dve":
                                op("dve", lambda: nc.vector.tensor_copy(out=st16[hf][:, ck * CK:(ck + 1) * CK], in_=st32[hf][:, ck * CK:(ck + 1) * CK]),
                                   reads=[B32[hf]], writes=[B16[hf]], partial=True)
                            else:
                                op("pool", lambda: nc.gpsimd.tensor_copy(out=st16[hf][:, ck * CK:(ck + 1) * CK], in_=st32[hf][:, ck * CK:(ck + 1) * CK]),
                                   reads=[B32[hf]], writes=[B16[hf]], partial=True)
                        r0 = pi * PIECE_ROWS + hf * HALF
                        dma(wcur16[bass.ds(r0, HALF), :].rearrange("(p r) c -> p (r c)", p=128), st16[hf][:, :], reads=[B16[hf]])
                    fw.hard_barrier()
            for s_ in range(NSEQ):
                body(l, s_)
                fw.hard_barrier()
    return nc, fw


def kernel(**inputs):
    cfg = Cfg()
    return run_kernel(cfg, inputs)


def run_kernel(cfg, inputs, n_cores=8, trace=False):
    nc, fw = build_program(cfg)
    x = np.ascontiguousarray(inputs["x"], dtype=np.float32)
    lay, brows = blob_layout()
    blob = np.zeros((cfg.L, brows * BLOB_COLS), dtype=np.float32)
    for name, shp in W_SHAPES:
        off = lay[name][0]
        n = int(np.prod(shp))
        if name == "lam0":
            for l in range(cfg.L):
                blob[l, off] = 0.8 - 0.6 * math.exp(-0.3 * l)
                blob[l, off + 1] = 1.0 - (0.8 - 0.6 * math.exp(-0.3 * l))
        elif name == "aq_aug":
            j = np.arange(512)
            t = np.zeros((4, A_HEADS, 512), dtype=np.float32)
            for h in range(A_HEADS):
                t[0, h] = -SLOPES_A[h] * (j % 256)
                t[1, h] = -SLOPES_A[h] * 256.0 * (j // 256)
                t[2, h] = 1.0
            blob[:, off:off + n] = t.reshape(1, n)
        elif name == "ak_aug":
            i = np.arange(128)
            t = np.zeros((4, A_HEADS, 2, 128), dtype=np.float32)
            for h in range(A_HEADS):
                t[0, h, 0] = 1.0
                t[1, h, 0] = 1.0
                t[2, h, 0] = SLOPES_A[h] * i
                t[0, h, 1] = -1.0
                t[1, h, 1] = -1.0
                t[2, h, 1] = -SLOPES_A[h] * i
            blob[:, off:off + n] = t.reshape(1, n)
        else:
            blob[:, off:off + n] = np.asarray(inputs[name], dtype=np.float32).reshape(cfg.L, n)
    blob = blob.reshape(cfg.L, brows, BLOB_COLS)
    in_maps = []
    for c in range(n_cores):
        in_maps.append({"x": x[c * cfg.NSEQ:(c + 1) * cfg.NSEQ], "wblob": blob})
    res = run_bass_kernel_spmd(nc, in_maps, core_ids=list(range(n_cores)), **({"trace": True} if trace else {}))
    if trace:
        print("EXEC_NS", res.exec_time_ns, "n_instr", fw.n_instr)
    outs = np.concatenate([r["out"] for r in res.results], axis=0)
    if cfg.debug:
        return outs, res.results
    return outs
```

```python
import math
import os
from contextlib import ExitStack

import numpy as np
import concourse.bass as bass
import concourse.mybir as mybir
from concourse.bass_utils import run_bass_kernel_spmd

F32 = mybir.dt.float32
BF16 = mybir.dt.bfloat16
I32 = mybir.dt.int32
AF = mybir.ActivationFunctionType
ALU = mybir.AluOpType
AX = mybir.AxisListType

D = 1024
DEPTH = 4
HD = 64
A_HEADS = 8
B_HEADS = 12
C_CH = 768
CONV_W = 31
D_FF = 2816
OFF_AQ, OFF_AK, OFF_AV = 0, 1024, 2048
OFF_BQ, OFF_BK, OFF_BV = 3072, 3840, 4608
OFF_CU = 5376
OFF_G = 6912
IN_W = 9984
EPS = 1e-6
ATTN_SCALE = HD ** -0.5
PATTERNS = ((128, 1), (512, 4), (2048, 16))
SLOPES_A = [2.0 ** (-8.0 * i / A_HEADS) for i in range(1, A_HEADS + 1)]
SLOPES_B = [2.0 ** (-8.0 * i / B_HEADS) for i in range(1, B_HEADS + 1)]
NEG_BIG = -1.0e6


class Buf:
    __slots__ = ("name", "w", "r", "excl")

    def __init__(self, name="", excl=False):
        self.name = name
        self.w = {}
        self.r = {}
        self.excl = excl


class Eng:
    def __init__(self, fw, key, e, sem):
        self.fw, self.key, self.e, self.sem = fw, key, e, sem
        self.count = 0
        self.seen = {}

    def wait_ev(self, key, cnt):
        if self.seen.get(key, 0) >= cnt:
            return
        self.e.wait_ge(self.fw.sems[key], cnt)
        self.seen[key] = cnt


class FW:
    NDMA = 16

    def __init__(self, nc, stack):
        self.nc = nc
        self.sems = {}
        self.engs = {}
        for key, e in (("pe", nc.tensor), ("act", nc.scalar), ("dve", nc.vector),
                       ("pool", nc.gpsimd), ("sp", nc.sync)):
            sem = stack.enter_context(nc.semaphore("s_" + key))
            self.sems[key] = sem
            self.engs[key] = Eng(self, key, e, sem)
        self.dma_keys = []
        for i in range(self.NDMA):
            k = "dma%d" % i
            self.sems[k] = stack.enter_context(nc.semaphore("s_" + k))
            self.dma_keys.append(k)
        self.dma_cnt = {k: 0 for k in self.dma_keys}
        self.dma_rr = 0
        self.n_instr = 0
        self.bufs = []

    def buf(self, name="", excl=False):
        b = Buf(name, excl)
        self.bufs.append(b)
        return b

    def reset_state(self):
        for e in self.engs.values():
            e.count = 0
            e.seen = {}
        self.dma_cnt = {k: 0 for k in self.dma_keys}
        self.dma_rr = 0
        for b in self.bufs:
            b.w = {}
            b.r = {}

    def _deps(self, eng, reads, writes):
        for b in reads:
            for k, c in b.w.items():
                eng.wait_ev(k, c)
            if b.excl:
                for k, c in b.r.items():
                    if k != eng.key:
                        eng.wait_ev(k, c)
        for b in writes:
            for k, c in b.w.items():
                if k != eng.key:
                    eng.wait_ev(k, c)
            for k, c in b.r.items():
                if k != eng.key:
                    eng.wait_ev(k, c)

    def _record(self, key, cnt, reads, writes, partial):
        for b in reads:
            if b.r.get(key, 0) < cnt:
                b.r[key] = cnt
        for b in writes:
            if not partial:
                b.r = {}
                b.w = {}
            b.w[key] = cnt

    def op(self, ek, fn, reads=(), writes=(), partial=False):
        eng = self.engs[ek]
        self._deps(eng, reads, writes)
        ins = fn()
        eng.count += 1
        ins.then_inc(eng.sem, 1)
        self._record(ek, eng.count, reads, writes, partial)
        self.n_instr += 1

    def dma(self, out, in_, reads=(), writes=(), partial=False, **kw):
        sp = self.engs["sp"]
        self._deps(sp, reads, writes)
        k = self.dma_keys[self.dma_rr % self.NDMA]
        self.dma_rr += 1
        if self.dma_cnt[k] > 0:
            sp.wait_ev(k, self.dma_cnt[k])
        self.nc.sync.dma_start(out=out, in_=in_, **kw).then_inc(self.sems[k], 16)
        self.dma_cnt[k] += 16
        self._record(k, self.dma_cnt[k], reads, writes, partial)
        self.n_instr += 1

    def drain_dmas(self):
        sp = self.engs["sp"]
        for k in self.dma_keys:
            if self.dma_cnt[k] > 0:
                sp.wait_ev(k, self.dma_cnt[k])

    def sync_all(self):
        for e in self.engs.values():
            for f in self.engs.values():
                if f is not e and f.count > 0:
                    e.wait_ev(f.key, f.count)
            for k in self.dma_keys:
                if self.dma_cnt[k] > 0:
                    e.wait_ev(k, self.dma_cnt[k])

    def hard_barrier(self):
        self.drain_dmas()
        self.nc.all_engine_barrier()
        for s in list(self.sems.values()) + list(getattr(self, "extra_sems", [])):
            self.nc.gpsimd.sem_clear(s)
        self.nc.all_engine_barrier()
        self.reset_state()


class Ring:
    def __init__(self, items):
        self.items = list(items)
        self.i = 0

    def next(self):
        it = self.items[self.i % len(self.items)]
        self.i += 1
        return it


BIG_W = ("ffn1_w_up", "ffn1_w_down", "w_in", "w_out_a", "w_out_b", "w_out_c", "w_out", "ffn2_w_up", "ffn2_w_down")
W_SHAPES = (("ffn1_norm", (D,)), ("mix_norm", (D,)), ("b_in", (IN_W,)),
            ("a_q_norm", (HD,)), ("a_k_norm", (HD,)), ("a_lambda", (4, HD)),
            ("a_sub_norm", (2 * HD,)), ("b_q_norm", (HD,)),
            ("b_k_norm", (HD,)), ("c_dw_w", (CONV_W, C_CH)),
            ("c_dw_b", (C_CH,)), ("c_norm", (C_CH,)), ("ffn2_norm", (D,)),
            ("lam0", (2,)), ("aq_aug", (4, A_HEADS * 512)), ("ak_aug", (4, A_HEADS * 2 * 128)),
            ("ffn1_w_up", (D, 2 * D_FF)), ("ffn1_w_down", (D_FF, D)), ("w_in", (D, IN_W)),
            ("w_out_a", (D, D)), ("w_out_b", (768, D)), ("w_out_c", (C_CH, D)), ("w_out", (D, D)),
            ("ffn2_w_up", (D, 2 * D_FF)), ("ffn2_w_down", (D_FF, D)))
BLOB_COLS = 2048
PIECE_ROWS = 1024


def blob_layout():
    off = 0
    lay = {}
    for name, shp in W_SHAPES:
        n = int(np.prod(shp))
        lay[name] = (off, shp)
        off += (n + 63) // 64 * 64
    rows = (off + BLOB_COLS - 1) // BLOB_COLS
    rows = (rows + PIECE_ROWS - 1) // PIECE_ROWS * PIECE_ROWS
    return lay, rows


class Cfg:
    def __init__(self, S=4096, NSEQ=2, L=DEPTH, phases=None, a_heads=None, b_pairs=None, debug=False):
        self.S, self.NSEQ, self.L = S, NSEQ, L
        self.phases = phases or ("ffn1", "hT", "A", "B", "C", "merge", "ffn2")
        self.a_heads = list(range(A_HEADS)) if a_heads is None else a_heads
        self.b_pairs = list(range(B_HEADS // 2)) if b_pairs is None else b_pairs
        self.debug = debug


def build_program(cfg):
    S, NSEQ, L = cfg.S, cfg.NSEQ, cfg.L
    NT = S // 128
    NCH = S // 512
    nc = bass.Bass("TRN2", target_bir_lowering=False)

    def din(name, shape):
        return nc.dram_tensor(name, list(shape), F32, kind="ExternalInput").ap()

    x_in = din("x", [NSEQ, S, D])
    LAY, BROWS = blob_layout()
    NPIECE = BROWS // PIECE_ROWS
    wblob = din("wblob", [L, BROWS, BLOB_COLS])
    wcur = nc.dram_tensor("wcur", [PIECE_ROWS, BLOB_COLS], F32).ap()
    wcur16 = nc.dram_tensor("wcur16", [BROWS, BLOB_COLS], BF16).ap()
    xcur = nc.dram_tensor("xcur", [S, D], F32).ap()
    out = nc.dram_tensor("out", [NSEQ, S, D], F32, kind="ExternalOutput").ap()
    ysc_kind = "ExternalOutput" if cfg.debug else "Internal"
    ysc = nc.dram_tensor("ysc", [20, 128, S], BF16, kind=ysc_kind).ap()
    hdbg = nc.dram_tensor("hdbg", [8, 128, S], BF16, kind="ExternalOutput").ap() if cfg.debug else None

    with ExitStack() as st:
        fw = FW(nc, st)
        op, dma = fw.op, fw.dma
        s_wc = st.enter_context(nc.semaphore("s_wc"))
        s_xc = st.enter_context(nc.semaphore("s_xc"))
        fw.extra_sems = [s_wc, s_xc]

        uniq = [0]

        def sbt(stack, name, shape, dt):
            uniq[0] += 1
            return stack.enter_context(nc.sbuf_tensor("%s_%d" % (name, uniq[0]), list(shape), dt))

        identb = sbt(st, "identb", [128, 128], BF16)
        identf = sbt(st, "identf", [128, 128], F32)
        onesb = sbt(st, "onesb", [128, 128], BF16)
        vecT = sbt(st, "vecT", [128, 72], F32)
        small = sbt(st, "small", [128, 16], F32)
        B_identb, B_identf, B_onesb, B_vecT, B_small = (fw.buf(n) for n in ("identb", "identf", "onesb", "vecT", "small"))
        banks = []
        dbanks = []
        for i in range(4):
            t = st.enter_context(nc.psum_tensor("dbank%d" % i, [128, 1024], F32))
            dbanks.append(t)
            banks.append((t[:, 0:512], fw.buf("bank%d" % (2 * i), excl=True)))
            banks.append((t[:, 512:1024], fw.buf("bank%d" % (2 * i + 1), excl=True)))

        def bank_bf(i):
            return banks[i][0][:, :].bitcast(BF16)

        op("pool", lambda: nc.gpsimd.memset(identf[:], 0.0), writes=[B_identf])
        op("pool", lambda: nc.gpsimd.affine_select(out=identf[:], in_=identf[:], pattern=[[-1, 128]], compare_op=ALU.not_equal,
                                                   fill=1.0, base=0, channel_multiplier=1), reads=[B_identf], writes=[B_identf])
        op("pool", lambda: nc.gpsimd.memset(onesb[:], 1.0), writes=[B_onesb])
        op("dve", lambda: nc.vector.tensor_copy(out=identb[:], in_=identf[:]), reads=[B_identf], writes=[B_identb])

        for s_ in range(NSEQ):
            for c in range(NCH):
                dma(out[s_, c * 512:(c + 1) * 512, :], x_in[s_, c * 512:(c + 1) * 512, :])
        fw.hard_barrier()

        def body(l, s):
            def wl(name, *idx):
                off, shp = LAY[name]
                n = int(np.prod(shp))
                src_ = wcur16 if name in BIG_W else wcur
                flat = src_.rearrange("r c -> (r c)")[off:off + n]
                if len(shp) == 2:
                    flat = flat.rearrange("(a b) -> a b", b=shp[1])
                return flat[idx] if idx else flat

            def xchunk(c, n=512):
                return out[s, c * n:(c + 1) * n, :].rearrange("(t p) f -> p t f", p=128)

            B_x = [fw.buf("xdram%d" % c) for c in range(NCH)]
            B_ysc = fw.buf("ysc")

            with ExitStack() as ph:
                stgv = sbt(ph, "stgv", [72, 128], F32)
                B_stgv = fw.buf("stgv")
                rows = 0
                for name, n in (("ffn1_norm", 8), ("mix_norm", 8), ("ffn2_norm", 8)):
                    dma(stgv[rows:rows + n, :], wl(name).rearrange("(k p) -> k p", p=128), writes=[B_stgv], partial=True)
                    rows += n
                dma(stgv[24:36, :], wl("b_in", slice(OFF_CU, OFF_CU + 1536)).rearrange("(k p) -> k p", p=128), writes=[B_stgv], partial=True)
                dma(stgv[36:60, :], wl("b_in", slice(OFF_G, OFF_G + 3072)).rearrange("(k p) -> k p", p=128), writes=[B_stgv], partial=True)
                dma(stgv[60:66, :], wl("c_dw_b").rearrange("(k p) -> k p", p=128), writes=[B_stgv], partial=True)
                dma(stgv[66:72, :], wl("c_norm").rearrange("(k p) -> k p", p=128), writes=[B_stgv], partial=True)
                pt, B_pt = banks[0]
                op("pe", lambda: nc.tensor.transpose(out=pt[:, 0:72], in_=stgv[0:72, :], identity=identf[0:72, 0:72]),
                   reads=[B_stgv, B_identf], writes=[B_pt])
                op("act", lambda: nc.scalar.copy(out=vecT[:, :], in_=pt[:, 0:72]), reads=[B_pt], writes=[B_vecT])
                B_sm_in = fw.buf("sm_in")
                smi = sbt(ph, "smi", [128, 8], F32)
                for col, name in ((0, "a_q_norm"), (1, "a_k_norm"), (2, "b_q_norm"), (3, "b_k_norm")):
                    src = wl(name).rearrange("(d i) -> d i", i=1)
                    dma(smi[0:64, col:col + 1], src, writes=[B_sm_in], partial=True)
                    dma(smi[64:128, col:col + 1], src, writes=[B_sm_in], partial=True)
                dma(smi[:, 4:5], wl("a_sub_norm").rearrange("(d i) -> d i", i=1), writes=[B_sm_in], partial=True)
                dma(smi[:, 5:7], wl("lam0").rearrange("(o c) -> o c", o=1).partition_broadcast(128), writes=[B_sm_in], partial=True)
                lamt = sbt(ph, "lamt", [128, 4, HD], F32)
                B_lamt = fw.buf("lamt")
                dma(lamt[:, :, :].rearrange("p a d -> p (a d)"),
                    wl("a_lambda").rearrange("a d -> (a d)").rearrange("(o n) -> o n", o=1).partition_broadcast(128), writes=[B_lamt])
                lprod = sbt(ph, "lprod", [128, 2, HD], F32)
                lsum = sbt(ph, "lsum", [128, 4], F32)
                B_lp, B_ls = fw.buf("lprod"), fw.buf("lsum")
                op("dve", lambda: nc.vector.tensor_tensor(out=lprod[:, :, :], in0=lamt[:, 0:4:2, :], in1=lamt[:, 1:4:2, :], op=ALU.mult),
                   reads=[B_lamt], writes=[B_lp])
                op("dve", lambda: nc.vector.tensor_reduce(out=lsum[:, 0:2], in_=lprod[:, :, :], axis=AX.X, op=ALU.add),
                   reads=[B_lp], writes=[B_ls])
                op("act", lambda: nc.scalar.activation(out=lsum[:, 2:4], in_=lsum[:, 0:2], func=AF.Exp), reads=[B_ls], writes=[B_ls], partial=True)
                op("dve", lambda: nc.vector.tensor_tensor(out=small[:, 5:6], in0=lsum[:, 2:3], in1=lsum[:, 3:4], op=ALU.subtract),
                   reads=[B_ls], writes=[B_small], partial=True)
                op("dve", lambda: nc.vector.tensor_tensor(out=small[:, 5:6], in0=small[:, 5:6], in1=smi[:, 5:6], op=ALU.add),
                   reads=[B_small, B_sm_in], writes=[B_small], partial=True)
                op("dve", lambda: nc.vector.tensor_scalar(out=small[:, 6:7], in0=small[:, 5:6], scalar1=-1.0, scalar2=None, op0=ALU.mult),
                   reads=[B_small], writes=[B_small], partial=True)
                op("dve", lambda: nc.vector.tensor_scalar(out=small[:, 0:1], in0=smi[:, 0:1], scalar1=ATTN_SCALE, scalar2=None, op0=ALU.mult),
                   reads=[B_sm_in], writes=[B_small], partial=True)
                op("dve", lambda: nc.vector.tensor_copy(out=small[:, 1:2], in_=smi[:, 1:2]), reads=[B_sm_in], writes=[B_small], partial=True)
                op("dve", lambda: nc.vector.tensor_scalar(out=small[:, 2:3], in0=smi[:, 2:3], scalar1=ATTN_SCALE, scalar2=None, op0=ALU.mult),
                   reads=[B_sm_in], writes=[B_small], partial=True)
                op("dve", lambda: nc.vector.tensor_copy(out=small[:, 3:4], in_=smi[:, 3:4]), reads=[B_sm_in], writes=[B_small], partial=True)
                op("dve", lambda: nc.vector.tensor_tensor(out=small[:, 4:5], in0=smi[:, 4:5], in1=smi[:, 6:7], op=ALU.mult),
                   reads=[B_sm_in], writes=[B_small], partial=True)
                fw.sync_all()

            G_FFN1, G_MIX, G_FFN2, V_BC, V_BG, V_DWB, V_CN = 0, 8, 16, 24, 36, 60, 66

            def norm_transpose(xc, B_xc, ntile, xn, B_xn, stat, B_stat, gcol, dst_fn, B_dst, tbanks, tcols, part=0):
                for t in range(ntile if part in (0, 1) else 0):
                    op("act", lambda t=t: nc.scalar.activation(out=xn[:, t, :], in_=xc[:, t, :], func=AF.Square,
                                                               accum_out=stat[:, t:t + 1]),
                       reads=[B_xc], writes=[B_xn, B_stat], partial=True)
                if part in (0, 1):
                    op("dve", lambda: nc.vector.tensor_scalar(out=stat[:, 8:8 + ntile], in0=stat[:, 0:ntile], scalar1=1.0 / D, scalar2=EPS,
                                                              op0=ALU.mult, op1=ALU.add), reads=[B_stat], writes=[B_stat], partial=True)
                    op("act", lambda: nc.scalar.activation(out=stat[:, 16:16 + ntile], in_=stat[:, 8:8 + ntile], func=AF.Sqrt),
                       reads=[B_stat], writes=[B_stat], partial=True)
                    op("dve", lambda: nc.vector.reciprocal(out=stat[:, 24:24 + ntile], in_=stat[:, 16:16 + ntile]),
                       reads=[B_stat], writes=[B_stat], partial=True)
                for t in range(ntile if part in (0, 1) else 0):
                    op("dve", lambda t=t: nc.vector.tensor_scalar(out=xn[:, t, :], in0=xc[:, t, :], scalar1=stat[:, 24 + t:25 + t],
                                                                  scalar2=None, op0=ALU.mult),
                       reads=[B_xc, B_stat], writes=[B_xn], partial=True)
                for fc in range(8 if part in (0, 2) else 0):
                    bi = tbanks[fc % len(tbanks)]
                    ptb, B_ptb = bank_bf(bi), banks[bi][1]

                    def tr(fc=fc, ptb=ptb):
                        ins = None
                        for t in range(ntile):
                            ins = nc.tensor.transpose(out=ptb[:, t * 128:(t + 1) * 128], in_=xn[:, t, fc * 128:(fc + 1) * 128],
                                                      identity=identb[:, :])
                        return ins
                    op("pe", tr, reads=[B_xn, B_identb], writes=[B_ptb])
                    eng = "act" if fc % 2 == 0 else "dve"
                    if eng == "act":
                        op("act", lambda fc=fc, ptb=ptb: nc.scalar.activation(out=dst_fn(fc), in_=ptb[:, 0:ntile * 128], func=AF.Copy,
                                                                              scale=vecT[:, gcol + fc:gcol + fc + 1]),
                           reads=[B_ptb, B_vecT], writes=[B_dst], partial=True)
                    else:
                        op("dve", lambda fc=fc, ptb=ptb: nc.vector.tensor_scalar(out=dst_fn(fc), in0=ptb[:, 0:ntile * 128],
                                                                                 scalar1=vecT[:, gcol + fc:gcol + fc + 1], scalar2=None, op0=ALU.mult),
                           reads=[B_ptb, B_vecT], writes=[B_dst], partial=True)

            def load_weight(dst, B_dst, src_fn, nk, ncols, stg_ring=None, max_cols=None):
                for k in range(nk):
                    dma(dst[:, k, 0:ncols], src_fn(k, 0, ncols), writes=[B_dst], partial=True)

            def phase_ffn(which):
                gcol = G_FFN1 if which == 0 else G_FFN2
                wun, wdn_n = ("ffn1_w_up", "ffn1_w_down") if which == 0 else ("ffn2_w_up", "ffn2_w_down")
                with ExitStack() as ph:
                    wup = sbt(ph, "wup", [128, 8, 2 * D_FF], BF16)
                    wdn = sbt(ph, "wdn", [128, 22, D], BF16)
                    B_wup, B_wdn = fw.buf("wup"), fw.buf("wdn")
                    stgs = None
                    xc = sbt(ph, "fxc", [128, 4, D], F32)
                    xc2 = sbt(ph, "fxc2", [128, 4, D], F32)
                    B_xc2 = fw.buf("fxc2")
                    xn = sbt(ph, "fxn", [128, 4, D], BF16)
                    hTc = sbt(ph, "fhT", [128, 8, 512], BF16)
                    uT = [sbt(ph, "fuT%d" % i, [128, 11, 512], BF16) for i in range(2)]
                    sa = [sbt(ph, "fsa%d" % i, [128, 512], F32) for i in range(2)]
                    stat = sbt(ph, "fstat", [128, 32], F32)
                    B_xc, B_xn, B_hTc, B_stat = fw.buf("fxc"), fw.buf("fxn"), fw.buf("fhT"), fw.buf("fstat")
                    B_uT = [fw.buf("fuT0"), fw.buf("fuT1")]
                    B_sa = [fw.buf("fsa0"), fw.buf("fsa1")]
                    load_weight(wup, B_wup, lambda k, c0, c1: wl(wun, slice(k * 128, (k + 1) * 128), slice(c0, c1)), 8, 2 * D_FF, stgs, 1408)
                    load_weight(wdn, B_wdn, lambda k, c0, c1: wl(wdn_n, slice(k * 128, (k + 1) * 128), slice(c0, c1)), 22, D, stgs, 1408)
                    abank = Ring([2, 3])
                    bbank = Ring([4, 5])
                    ybank = Ring([6, 7])
                    sai = 0
                    xcs_ = [xc, xc2]
                    B_xcs_ = [B_xc, B_xc2]

                    def load_x(c):
                        for t_ in range(4):
                            dma(xcs_[c % 2][:, t_, :], xchunk(c)[:, t_, :], reads=[B_x[c]], writes=[B_xcs_[c % 2]], partial=(t_ > 0))

                    def nt(c, part):
                        norm_transpose(xcs_[c % 2], B_xcs_[c % 2], 4, xn, B_xn, stat, B_stat, gcol, lambda fc: hTc[:, fc, :], B_hTc, [0, 1], 512, part=part)
                    load_x(0)
                    nt(0, 0)
                    for c in range(NCH):
                        xc_, B_xc_ = xcs_[c % 2], B_xcs_[c % 2]
                        if c + 1 < NCH:
                            load_x(c + 1)
                        for half in range(2):
                            if half == 1 and c + 1 < NCH:
                                nt(c + 1, 1)
                            for jj in range(11):
                                j = half * 11 + jj
                                ai, bi = abank.next(), bbank.next()
                                pa, B_pa = banks[ai]
                                pb, B_pb = banks[bi]

                                def mm_up(pa=pa, pb=pb, j=j):
                                    ins = None
                                    for k in range(8):
                                        nc.tensor.matmul(pa[:, :], lhsT=wup[:, k, j * 128:(j + 1) * 128], rhs=hTc[:, k, :], start=(k == 0), stop=(k == 7))
                                    for k in range(8):
                                        ins = nc.tensor.matmul(pb[:, :], lhsT=wup[:, k, D_FF + j * 128:D_FF + (j + 1) * 128], rhs=hTc[:, k, :],
                                                               start=(k == 0), stop=(k == 7))
                                    return ins
                                op("pe", mm_up, reads=[B_wup, B_hTc], writes=[B_pa, B_pb])
                                sat, B_sat = sa[sai % 2], B_sa[sai % 2]
                                sai += 1
                                op("act", lambda pa=pa, sat=sat: nc.scalar.activation(out=sat[:, :], in_=pa[:, :], func=AF.Silu),
                                   reads=[B_pa], writes=[B_sat])
                                op("dve", lambda pb=pb, sat=sat, half=half, jj=jj: nc.vector.tensor_tensor(out=uT[half][:, jj, :], in0=sat[:, :], in1=pb[:, :], op=ALU.mult),
                                   reads=[B_sat, B_pb], writes=[B_uT[half]], partial=True)
                            if half == 1 and c + 1 < NCH:
                                nt(c + 1, 2)
                            for t in range(4):
                                for oh in range(2):
                                    yi = ybank.next()
                                    py, B_py = banks[yi]

                                    def mm_dn(py=py, t=t, oh=oh, half=half):
                                        ins = None
                                        for jj in range(11):
                                            ins = nc.tensor.matmul(py[:, :], lhsT=uT[half][:, jj, t * 128:(t + 1) * 128],
                                                                   rhs=wdn[:, half * 11 + jj, oh * 512:(oh + 1) * 512], start=(jj == 0), stop=(jj == 10))
                                        return ins
                                    op("pe", mm_dn, reads=[B_uT[half], B_wdn], writes=[B_py])
                                    op("dve", lambda py=py, t=t, oh=oh: nc.vector.scalar_tensor_tensor(
                                        out=xc_[:, t, oh * 512:(oh + 1) * 512], in0=py[:, :], scalar=0.5, in1=xc_[:, t, oh * 512:(oh + 1) * 512],
                                        op0=ALU.mult, op1=ALU.add), reads=[B_py, B_xc_], writes=[B_xc_], partial=True)
                        for t_ in range(4):
                            dma(xchunk(c)[:, t_, :], xc_[:, t_, :], reads=[B_xc_], writes=[B_x[c]], partial=(t_ > 0))
                    fw.sync_all()


            def phase_hT():
                with ExitStack() as ph:
                    xcs = [sbt(ph, "hxc%d" % i, [128, 4, D], F32) for i in range(2)]
                    B_xcs = [fw.buf("hxc0"), fw.buf("hxc1")]
                    xn = sbt(ph, "hxn", [128, 4, D], BF16)
                    stat = sbt(ph, "hstat", [128, 32], F32)
                    B_xn, B_stat = fw.buf("hxn"), fw.buf("hstat")
                    for c in range(NCH):
                        xc, B_xc = xcs[c % 2], B_xcs[c % 2]
                        dma(xc[:, :, :], xchunk(c), reads=[B_x[c]], writes=[B_xc])
                        norm_transpose(xc, B_xc, 4, xn, B_xn, stat, B_stat, G_MIX,
                                       lambda fc, c=c: hTf[:, fc, c * 512:(c + 1) * 512], B_hTf, [0, 1], 512)
                    if cfg.debug:
                        for fc in range(8):
                            dma(hdbg[fc, :, :], hTf[:, fc, :], reads=[B_hTf])
                    fw.sync_all()

            def alloc_qkv(ph, need_v=True):
                o = {}
                o["wq"] = [sbt(ph, "wqkv%d" % i, [128, 8, 384], BF16) for i in range(2)]
                o["bt"] = [sbt(ph, "bt%d" % i, [128, 384], F32) for i in range(2)]
                o["B_wq"] = [fw.buf("wq0"), fw.buf("wq1")]
                o["B_bt"] = [fw.buf("bt0"), fw.buf("bt1")]
                o["QT"] = sbt(ph, "QT", [128, S], BF16)
                o["KT"] = sbt(ph, "KT", [128, S], BF16)
                o["V"] = sbt(ph, "V", [128, NT, 128], BF16) if need_v else None
                for k in ("QT", "KT", "V"):
                    o["B_" + k] = fw.buf(k)
                NR = 3
                o["NR"] = NR
                o["qkv"] = [sbt(ph, "qkv%d" % i, [128, 384], F32) for i in range(NR)]
                o["sq"] = [sbt(ph, "sqt%d" % i, [128, 256], F32) for i in range(2)]
                o["qn"] = [sbt(ph, "qn%d" % i, [128, 256], BF16) for i in range(NR)]
                o["st"] = [sbt(ph, "qst%d" % i, [128, 16], F32) for i in range(NR)]
                o["B_qkv"] = [fw.buf("qkv%d" % i) for i in range(NR)]
                o["B_sq"] = [fw.buf("sq0"), fw.buf("sq1")]
                o["B_qn"] = [fw.buf("qn%d" % i) for i in range(NR)]
                o["B_st"] = [fw.buf("qst%d" % i) for i in range(NR)]
                return o

            def load_qkv_w(o, slot, cq, ck, cv):
                wq, bt = o["wq"][slot], o["bt"][slot]
                for i, c0 in enumerate((cq, ck, cv)):
                    dma(wq[:, :, i * 128:(i + 1) * 128], wl("w_in", slice(None), slice(c0, c0 + 128)).rearrange("(k p) c -> p k c", p=128),
                        writes=[o["B_wq"][slot]], partial=True)
                    dma(bt[:, i * 128:(i + 1) * 128],
                        wl("b_in", slice(c0, c0 + 128)).rearrange("(o n) -> o n", o=1).partition_broadcast(128),
                        writes=[o["B_bt"][slot]], partial=True)

            def qkv_project(o, slot, gqc, gkc, vdst=None):
                wq, bt, QT, KT, V = o["wq"][slot], o["bt"][slot], o["QT"], o["KT"], o["V"]
                B_wq, B_bt = o["B_wq"][slot], o["B_bt"][slot]
                NR = o["NR"]
                ptr = Ring([2, 3])
                ptbs = {}

                def stM(t):
                    pp, B_pp = banks[t % 2]

                    def mm():
                        ins = None
                        for k in range(8):
                            ins = nc.tensor.matmul(pp[:, 0:384], lhsT=hTf[:, k, t * 128:(t + 1) * 128], rhs=wq[:, k, :], start=(k == 0), stop=(k == 7))
                        return ins
                    op("pe", mm, reads=[B_hTf, B_wq], writes=[B_pp])

                def stA(t):
                    pp, B_pp = banks[t % 2]
                    qkv, B_qkv = o["qkv"][t % NR], o["B_qkv"][t % NR]
                    stt_, B_st = o["st"][t % NR], o["B_st"][t % NR]
                    sq, B_sq = o["sq"][t % 2], o["B_sq"][t % 2]
                    op("dve", lambda: nc.vector.tensor_tensor(out=qkv[:, :], in0=pp[:, 0:384], in1=bt[:, :], op=ALU.add), reads=[B_pp, B_bt], writes=[B_qkv])
                    for g_ in range(4):
                        op("act", lambda g_=g_: nc.scalar.activation(out=sq[:, g_ * 64:(g_ + 1) * 64], in_=qkv[:, g_ * 64:(g_ + 1) * 64], func=AF.Square,
                                                                     accum_out=stt_[:, g_:g_ + 1]),
                           reads=[B_qkv], writes=[B_sq, B_st], partial=True)
                    op("dve", lambda: nc.vector.tensor_scalar(out=stt_[:, 4:8], in0=stt_[:, 0:4], scalar1=1.0 / HD, scalar2=EPS, op0=ALU.mult, op1=ALU.add),
                       reads=[B_st], writes=[B_st], partial=True)
                    op("act", lambda: nc.scalar.activation(out=stt_[:, 8:12], in_=stt_[:, 4:8], func=AF.Sqrt), reads=[B_st], writes=[B_st], partial=True)
                    if vdst is None:
                        op("act", lambda: nc.scalar.copy(out=V[:, t, :], in_=qkv[:, 256:384]), reads=[B_qkv], writes=[o["B_V"]], partial=True)
                    else:
                        vt_, B_vt_ = vdst
                        op("act", lambda: nc.scalar.copy(out=vt_[:, t, :, 0:64], in_=qkv[:, 256:384].rearrange("p (e d) -> p e d", e=2)),
                           reads=[B_qkv], writes=[B_vt_], partial=True)

                def stB(t):
                    qkv, B_qkv = o["qkv"][t % NR], o["B_qkv"][t % NR]
                    qn, B_qn = o["qn"][t % NR], o["B_qn"][t % NR]
                    stt_, B_st = o["st"][t % NR], o["B_st"][t % NR]
                    op("dve", lambda: nc.vector.reciprocal(out=stt_[:, 12:16], in_=stt_[:, 8:12]), reads=[B_st], writes=[B_st], partial=True)
                    op("dve", lambda: nc.vector.tensor_tensor(
                        out=qn[:, :].rearrange("p (g d) -> p g d", g=4), in0=qkv[:, 0:256].rearrange("p (g d) -> p g d", g=4),
                        in1=stt_[:, 12:16].rearrange("p (g o) -> p g o", o=1).to_broadcast([128, 4, HD]), op=ALU.mult),
                       reads=[B_qkv, B_st], writes=[B_qn])

                def stT(t):
                    qn, B_qn = o["qn"][t % NR], o["B_qn"][t % NR]
                    if t % 4 == 0:
                        ti = ptr.next()
                        ptbs[t // 4] = (bank_bf(ti), banks[ti][1])
                    ptb, B_ptb = ptbs[t // 4]

                    def tr():
                        nc.tensor.transpose(out=ptb[:, (t % 4) * 128:(t % 4 + 1) * 128], in_=qn[:, 0:128], identity=identb[:, :])
                        return nc.tensor.transpose(out=ptb[:, 512 + (t % 4) * 128:512 + (t % 4 + 1) * 128], in_=qn[:, 128:256], identity=identb[:, :])
                    op("pe", tr, reads=[B_qn, B_identb], writes=[B_ptb], partial=True)
                    if t % 4 == 3:
                        t0 = t - 3
                        op("act", lambda: nc.scalar.activation(out=QT[:, t0 * 128:(t0 + 4) * 128], in_=ptb[:, 0:512], func=AF.Copy, scale=small[:, gqc:gqc + 1]),
                           reads=[B_ptb, B_small], writes=[o["B_QT"]], partial=True)
                        op("dve", lambda: nc.vector.tensor_scalar(out=KT[:, t0 * 128:(t0 + 4) * 128], in0=ptb[:, 512:1024],
                                                                  scalar1=small[:, gkc:gkc + 1], scalar2=None, op0=ALU.mult),
                           reads=[B_ptb, B_small], writes=[o["B_KT"]], partial=True)
                for s_ in range(NT + 3):
                    if s_ < NT:
                        stM(s_)
                    if 0 <= s_ - 1 < NT:
                        stA(s_ - 1)
                    if 0 <= s_ - 2 < NT:
                        stB(s_ - 2)
                    if 0 <= s_ - 3 < NT:
                        stT(s_ - 3)

            def phase_A():
                with ExitStack() as ph:
                    o = alloc_qkv(ph)
                    QT, KT, V = o["QT"], o["KT"], o["V"]
                    ib = sbt(ph, "ibias", [128, 4, 512], I32)
                    Bneg = sbt(ph, "Bneg", [128, 512], F32)
                    Bdiag = sbt(ph, "Bdiag", [128, 4, 512], F32)
                    B_ib, B_Bneg, B_Bdiag = fw.buf("ib"), fw.buf("Bneg"), fw.buf("Bdiag")
                    op("pool", lambda: nc.gpsimd.iota(ib[:, 0, :], pattern=[[-1, 512]], base=0, channel_multiplier=1), writes=[B_ib])
                    op("dve", lambda: nc.vector.tensor_copy(out=Bneg[:, :], in_=ib[:, 0, :]), reads=[B_ib], writes=[B_Bneg])
                    op("pool", lambda: nc.gpsimd.iota(ib[:, :, :], pattern=[[-128, 4], [1, 512]], base=0, channel_multiplier=-1), reads=[], writes=[B_ib])
                    op("dve", lambda: nc.vector.tensor_copy(out=Bdiag[:, :, :], in_=ib[:, :, :]), reads=[B_ib], writes=[B_Bdiag])
                    for tt_ in range(4):
                        op("dve", lambda tt_=tt_: nc.vector.scalar_tensor_tensor(out=Bdiag[:, tt_, :], in0=Bdiag[:, tt_, :], scalar=-1.0, in1=Bdiag[:, tt_, :], op0=ALU.mult, op1=ALU.min),
                           reads=[B_Bdiag], writes=[B_Bdiag], partial=True)
                    QA = KA = None
                    B_QA, B_KA = fw.buf("QAaug"), fw.buf("KAaug")
                    tts = [sbt(ph, "att%d" % i, [128, 2, 512], F32) for i in range(2)]
                    B_tts = [fw.buf("att0"), fw.buf("att1")]
                    NPT = 4
                    PTs = [sbt(ph, "aPT%d" % i, [128, 2, 512], BF16) for i in range(NPT)]
                    B_PTs = [fw.buf("aPT%d" % i) for i in range(NPT)]
                    f = [sbt(ph, "af%d" % i, [128, 512], F32) for i in range(4)]
                    B_f = [fw.buf("af%d" % i) for i in range(4)]
                    sqb = sbt(ph, "asqb", [128, 512], BF16)
                    B_sqb = fw.buf("asqb")
                    yaTs = [sbt(ph, "ayaT%d" % i, [128, 512], BF16) for i in range(2)]
                    B_yaTs = [fw.buf("ayaT0"), fw.buf("ayaT1")]
                    ycount = 0
                    if cfg.a_heads:
                        h0_ = cfg.a_heads[0]
                        load_qkv_w(o, 0, OFF_AQ + h0_ * 128, OFF_AK + h0_ * 128, OFF_AV + h0_ * 128)
                    for hi_, h in enumerate(cfg.a_heads):
                        if hi_ + 1 < len(cfg.a_heads):
                            hn_ = cfg.a_heads[hi_ + 1]
                            load_qkv_w(o, (hi_ + 1) % 2, OFF_AQ + hn_ * 128, OFF_AK + hn_ * 128, OFF_AV + hn_ * 128)
                        qkv_project(o, hi_ % 2, 0, 1)
                        m = SLOPES_A[h]
                        SKIP_T = 120.0
                        kept = {}
                        for qc in range(NCH):
                            kl = []
                            for kt in range(NT):
                                q0_, k0_ = qc * 512, kt * 128
                                dmin = max(q0_ - (k0_ + 127), k0_ - (q0_ + 511), 0)
                                if m * dmin <= SKIP_T:
                                    kl.append(kt)
                            kept[qc] = kl
                        steps = [(qc, kt) for qc in range(NCH) for kt in kept[qc]]
                        spair = Ring([0, 1])
                        pend = {}

                        USE_AUG = False

                        def side_of(qc, kt):
                            q0, k0 = qc * 512, kt * 128
                            if k0 + 128 <= q0:
                                return 0
                            if k0 >= q0 + 512:
                                return 1
                            return -1

                        def emit_S(idx):
                            qc, kt = steps[idx]
                            di = spair.next()
                            pend[idx] = di
                            sd = side_of(qc, kt) if USE_AUG else -1

                            def mmS(di=di, qc=qc, kt=kt):
                                ins = None
                                for g in range(2):
                                    ins = nc.tensor.matmul(dbanks[di][:, g * 512:(g + 1) * 512], lhsT=KT[g * 64:(g + 1) * 64, kt * 128:(kt + 1) * 128],
                                                           rhs=QT[g * 64:(g + 1) * 64, qc * 512:(qc + 1) * 512], start=True, stop=(sd < 0))
                                    if sd >= 0:
                                        ins = nc.tensor.matmul(dbanks[di][:, g * 512:(g + 1) * 512], lhsT=KA[:, h, sd, :], rhs=QA[:, h, :], start=False, stop=True)
                                return ins
                            op("pe", mmS, reads=[o["B_QT"], o["B_KT"], B_QA, B_KA], writes=[banks[2 * di][1], banks[2 * di + 1][1]])
                        emit_S(0)
                        backs = []
                        LAG = 1
                        ycount_box = [ycount]
                        for idx, (qc, kt) in enumerate(steps):
                            if idx + 1 < len(steps):
                                emit_S(idx + 1)
                            di = pend.pop(idx)
                            q0, k0 = qc * 512, kt * 128
                            tt, B_tt = tts[idx % 2], B_tts[idx % 2]
                            PT, B_PT = PTs[idx % NPT], B_PTs[idx % NPT]
                            sd = side_of(qc, kt)
                            if sd < 0:
                                tab, B_tab = Bdiag[:, (k0 - q0) // 128, :], B_Bdiag
                                op("dve", lambda tab=tab, di=di, tt=tt: nc.vector.scalar_tensor_tensor(
                                    out=tt[:, :, :], in0=tab.rearrange("p (o n) -> p o n", o=1).to_broadcast([128, 2, 512]), scalar=float(m),
                                    in1=dbanks[di][:, :].rearrange("p (g n) -> p g n", g=2), op0=ALU.mult, op1=ALU.add),
                                   reads=[B_tab, banks[2 * di][1], banks[2 * di + 1][1]], writes=[B_tt])
                                op("act", lambda tt=tt, PT=PT: nc.scalar.activation(out=PT[:, :, :], in_=tt[:, :, :], func=AF.Exp),
                                   reads=[B_tt], writes=[B_PT])
                            elif not USE_AUG:
                                cst = -m * (q0 - k0) if sd == 0 else -m * (k0 - q0)
                                sc = m if sd == 0 else -m
                                op("dve", lambda di=di, tt=tt, sc=sc: nc.vector.scalar_tensor_tensor(
                                    out=tt[:, :, :], in0=Bneg[:, :].rearrange("p (o n) -> p o n", o=1).to_broadcast([128, 2, 512]), scalar=float(sc),
                                    in1=dbanks[di][:, :].rearrange("p (g n) -> p g n", g=2), op0=ALU.mult, op1=ALU.add),
                                   reads=[B_Bneg, banks[2 * di][1], banks[2 * di + 1][1]], writes=[B_tt])
                                op("act", lambda tt=tt, PT=PT, cst=cst: nc.scalar.activation(out=PT[:, :, :], in_=tt[:, :, :], func=AF.Exp, bias=float(cst)),
                                   reads=[B_tt], writes=[B_PT])
                            else:
                                cst = -m * (q0 - k0) if sd == 0 else -m * (k0 - q0)
                                op("act", lambda di=di, PT=PT, cst=cst: nc.scalar.activation(
                                    out=PT[:, :, :], in_=dbanks[di][:, :].rearrange("p (g n) -> p g n", g=2), func=AF.Exp, bias=float(cst)),
                                   reads=[banks[2 * di][1], banks[2 * di + 1][1]], writes=[B_PT])

                            def back(idx=idx, qc=qc, kt=kt, q0=q0, PT=PT, B_PT=B_PT):
                                def mmPV(kt=kt, PT=PT):
                                    ins = None
                                    for g in range(2):
                                        nc.tensor.matmul(banks[4 + 2 * g][0][:, :], lhsT=V[:, kt, :], rhs=PT[:, g, :], start=(kt == kept[qc][0]), stop=(kt == kept[qc][-1]))
                                        ins = nc.tensor.matmul(banks[5 + 2 * g][0][:, :], lhsT=onesb[:, :], rhs=PT[:, g, :], start=(kt == kept[qc][0]), stop=(kt == kept[qc][-1]))
                                    return ins
                                op("pe", mmPV, reads=[o["B_V"], B_PT, B_onesb], writes=[banks[4][1], banks[5][1], banks[6][1], banks[7][1]])
                                if kt == kept[qc][-1]:
                                    O0, D0, O1, D1 = banks[4][0], banks[5][0], banks[6][0], banks[7][0]
                                    B_O0, B_D0, B_O1, B_D1 = banks[4][1], banks[5][1], banks[6][1], banks[7][1]
                                    op("act", lambda: nc.scalar.activation(out=f[0][:, :], in_=D0[:, :], func=AF.Ln), reads=[B_D0], writes=[B_f[0]])
                                    op("act", lambda: nc.scalar.activation(out=f[1][:, :], in_=D1[:, :], func=AF.Ln), reads=[B_D1], writes=[B_f[1]])
                                    op("act", lambda: nc.scalar.activation(out=f[0][:, :], in_=f[0][:, :], func=AF.Exp, scale=-1.0), reads=[B_f[0]], writes=[B_f[0]])
                                    op("act", lambda: nc.scalar.activation(out=f[1][:, :], in_=f[1][:, :], func=AF.Exp, scale=-1.0), reads=[B_f[1]], writes=[B_f[1]])
                                    op("dve", lambda: nc.vector.tensor_tensor(out=f[2][:, :], in0=O0[:, :], in1=f[0][:, :], op=ALU.mult), reads=[B_O0, B_f[0]], writes=[B_f[2]])
                                    op("dve", lambda: nc.vector.tensor_tensor(out=f[3][:, :], in0=O1[:, :], in1=f[1][:, :], op=ALU.mult), reads=[B_O1, B_f[1]], writes=[B_f[3]])
                                    op("dve", lambda: nc.vector.scalar_tensor_tensor(out=f[2][:, :], in0=f[3][:, :], scalar=small[:, 6:7], in1=f[2][:, :],
                                                                                     op0=ALU.mult, op1=ALU.add), reads=[B_f[3], B_f[2], B_small], writes=[B_f[2]])
                                    op("act", lambda: nc.scalar.activation(out=sqb[:, :], in_=f[2][:, :], func=AF.Square), reads=[B_f[2]], writes=[B_sqb])
                                    op("pe", lambda: nc.tensor.matmul(D0[:, :], lhsT=onesb[:, :], rhs=sqb[:, :], start=True, stop=True),
                                       reads=[B_sqb, B_onesb], writes=[B_D0])
                                    op("act", lambda: nc.scalar.activation(out=f[0][:, :], in_=D0[:, :], func=AF.Ln, scale=1.0 / 128, bias=EPS),
                                       reads=[B_D0], writes=[B_f[0]])
                                    op("act", lambda: nc.scalar.activation(out=f[0][:, :], in_=f[0][:, :], func=AF.Exp, scale=-0.5), reads=[B_f[0]], writes=[B_f[0]])
                                    yaT, B_yaT = yaTs[ycount_box[0] % 2], B_yaTs[ycount_box[0] % 2]
                                    ycount_box[0] += 1
                                    op("dve", lambda yaT=yaT: nc.vector.scalar_tensor_tensor(out=yaT[:, :], in0=f[2][:, :], scalar=small[:, 4:5], in1=f[0][:, :],
                                                                                             op0=ALU.mult, op1=ALU.mult), reads=[B_f[2], B_f[0], B_small], writes=[B_yaT])
                                    dma(ysc[h, :, q0:q0 + 512], yaT[:, :], reads=[B_yaT], writes=[B_ysc], partial=True)
                            backs.append(back)
                            if len(backs) > LAG:
                                backs.pop(0)()
                        while backs:
                            backs.pop(0)()
                        ycount = ycount_box[0]
                    fw.sync_all()

            def phase_B():
                with ExitStack() as ph:
                    o = alloc_qkv(ph, need_v=False)
                    QT, KT, V = o["QT"], o["KT"], o["V"]
                    Vc = {d_: sbt(ph, "Vb%d" % d_, [128, NT, 2, 128], BF16) for d_ in (1, 4, 16)}
                    B_Vc = {d_: fw.buf("Vb%d" % d_) for d_ in (1, 4, 16)}
                    for d_ in (1, 4, 16):
                        op("pool", lambda d_=d_: nc.gpsimd.memset(Vc[d_][:, :, :, 64:128], 1.0), writes=[B_Vc[d_]], partial=True)
                    Bband = sbt(ph, "Bband", [128, 384], F32)
                    ibb = sbt(ph, "ibband", [128, 384], I32)
                    btmp = ibb[:, :].bitcast(F32)
                    B_ibb, B_Bband, B_btmp = fw.buf("ibb"), fw.buf("Bband"), fw.buf("btmp")
                    op("pool", lambda: nc.gpsimd.iota(ibb[:, :], pattern=[[1, 384]], base=-128, channel_multiplier=-1), writes=[B_ibb])
                    op("dve", lambda: nc.vector.tensor_copy(out=Bband[:, :], in_=ibb[:, :]), reads=[B_ibb], writes=[B_Bband])
                    op("dve", lambda: nc.vector.scalar_tensor_tensor(out=Bband[:, :], in0=Bband[:, :], scalar=-1.0, in1=Bband[:, :], op0=ALU.mult, op1=ALU.max),
                       reads=[B_Bband], writes=[B_Bband])
                    op("dve", lambda: nc.vector.tensor_scalar(out=btmp[:, :], in0=Bband[:, :], scalar1=64.0, scalar2=-NEG_BIG, op0=ALU.is_gt, op1=ALU.mult),
                       reads=[B_Bband], writes=[B_btmp])
                    op("dve", lambda: nc.vector.tensor_tensor(out=Bband[:, :], in0=Bband[:, :], in1=btmp[:, :], op=ALU.add), reads=[B_Bband, B_btmp], writes=[B_Bband])
                    op("dve", lambda: nc.vector.tensor_scalar(out=Bband[:, :], in0=Bband[:, :], scalar1=-1.0, scalar2=None, op0=ALU.mult),
                       reads=[B_Bband], writes=[B_Bband])
                    tts = [sbt(ph, "btt%d" % i, [128, 384], F32) for i in range(3)]
                    B_tts = [fw.buf("btt%d" % i) for i in range(3)]
                    PTs = [sbt(ph, "bPT%d" % i, [128, 384], BF16) for i in range(8)]
                    B_PTs = [fw.buf("bPT%d" % i) for i in range(8)]
                    acc = sbt(ph, "accB", [128, S], F32)
                    rden = sbt(ph, "rdenB", [64, 2048], F32)
                    ybT = sbt(ph, "ybT", [64, S], BF16)
                    B_acc, B_rden, B_ybT = fw.buf("accB"), fw.buf("rdenB"), fw.buf("ybT")
                    if cfg.b_pairs:
                        p0_ = cfg.b_pairs[0]
                        load_qkv_w(o, 0, OFF_BQ + p0_ * 128, OFF_BK + p0_ * 128, OFF_BV + p0_ * 128)
                    for pi_, hp in enumerate(cfg.b_pairs):
                        if pi_ + 1 < len(cfg.b_pairs):
                            pn_ = cfg.b_pairs[pi_ + 1]
                            load_qkv_w(o, (pi_ + 1) % 2, OFF_BQ + pn_ * 128, OFF_BK + pn_ * 128, OFF_BV + pn_ * 128)
                        qkv_project(o, pi_ % 2, 2, 3, vdst=(Vc[1], B_Vc[1]))
                        wq, bt = o["wq"][pi_ % 2], o["bt"][pi_ % 2]
                        B_wq_, B_bt_ = o["B_wq"][pi_ % 2], o["B_bt"][pi_ % 2]
                        vbr = Ring([0, 1])
                        for dil in (4, 16):
                            U = S // dil
                            nut = U // 128
                            tiles = [(r, ut) for r in range(dil) for ut in range(nut)]
                            for g0 in range(0, len(tiles), 4):
                                grp = tiles[g0:g0 + 4]
                                bi = vbr.next()
                                pv, B_pv = banks[bi]

                                def mmv(grp=grp, pv=pv, dil=dil):
                                    ins = None
                                    for j, (r, ut) in enumerate(grp):
                                        c0 = r + dil * ut * 128
                                        for k in range(8):
                                            ins = nc.tensor.matmul(pv[:, j * 128:(j + 1) * 128], lhsT=hTf[:, k, c0:c0 + dil * 127 + 1:dil],
                                                                   rhs=wq[:, k, 256:384], start=(k == 0), stop=(k == 7))
                                    return ins
                                op("pe", mmv, reads=[B_hTf, B_wq_], writes=[B_pv])
                                ng = len(grp)
                                op("dve", lambda pv=pv, g0=g0, ng=ng, dil=dil: nc.vector.tensor_tensor(
                                    out=Vc[dil][:, g0:g0 + ng, :, 0:64], in0=pv[:, 0:ng * 128].rearrange("p (j e c) -> p j e c", j=ng, e=2),
                                    in1=bt[:, 256:384].rearrange("p (o e c) -> p o e c", o=1, e=2).to_broadcast([128, ng, 2, 64]), op=ALU.add),
                                   reads=[B_pv, B_bt_], writes=[B_Vc[dil]], partial=True)
                        for e in range(2):
                            m = SLOPES_B[hp * 2 + e]
                            sring = Ring([0, 1, 4, 5, 6, 7])
                            LOOK = 3
                            steps = []
                            for (_win, dil) in PATTERNS:
                                nkt = (S // dil) // 128
                                for r in range(dil):
                                    for kt in range(nkt):
                                        steps.append((dil, r, kt, nkt))

                            def ccols(dil, r, u0, n):
                                c0 = r + dil * u0
                                return slice(c0, c0 + dil * (n - 1) + 1, dil)
                            sbank = {}
                            nxt = [0]

                            def emit_S(i):
                                dil, r, kt, nkt = steps[i]
                                qa, qb = max(kt - 1, 0), min(kt + 1, nkt - 1)
                                nq = (qb - qa + 1) * 128
                                si = sring.next()
                                sbank[i] = si
                                pS, B_pS = banks[si]
                                op("pe", lambda: nc.tensor.matmul(
                                    pS[:, 0:nq], lhsT=KT[e * 64:(e + 1) * 64, ccols(dil, r, kt * 128, 128)], rhs=QT[e * 64:(e + 1) * 64, ccols(dil, r, qa * 128, nq)],
                                    start=True, stop=True), reads=[o["B_QT"], o["B_KT"]], writes=[B_pS])
                            hist = {}
                            segctr = {}
                            nsegs = [0]
                            bq = []
                            LAGB = 2
                            for i, (dil, r, kt, nkt) in enumerate(steps):
                                while nxt[0] <= min(i + LOOK, len(steps) - 1):
                                    emit_S(nxt[0])
                                    nxt[0] += 1
                                qa, qb = max(kt - 1, 0), min(kt + 1, nkt - 1)
                                nq = (qb - qa + 1) * 128
                                pS, B_pS = banks[sbank.pop(i)]
                                tt, B_tt = tts[i % 3], B_tts[i % 3]
                                PT, B_PT = PTs[i % 8], B_PTs[i % 8]
                                boff = (qa - (kt - 1)) * 128
                                op("dve", lambda: nc.vector.scalar_tensor_tensor(
                                    out=tt[:, 0:nq], in0=Bband[:, boff:boff + nq], scalar=float(m * dil), in1=pS[:, 0:nq], op0=ALU.mult, op1=ALU.add),
                                   reads=[B_Bband, B_pS], writes=[B_tt])
                                op("act", lambda: nc.scalar.activation(out=PT[:, 0:nq], in_=tt[:, 0:nq], func=AF.Exp), reads=[B_tt], writes=[B_PT])
                                hist[(dil, r, kt)] = (PT, B_PT, qa)
                                qts = []
                                if kt >= 1:
                                    qts.append(kt - 1)
                                if kt == nkt - 1:
                                    qts.append(kt)
                                for qt in qts:
                                    bq.append((dil, r, nkt, qt))
                                while len(bq) > 0 and (len(bq) > LAGB * 2 or i == len(steps) - 1):
                                    dil, r, nkt, qt = bq.pop(0)
                                    seg = qt // 4
                                    if (dil, r, seg) not in segctr:
                                        segctr[(dil, r, seg)] = nsegs[0]
                                        nsegs[0] += 1
                                    par = segctr[(dil, r, seg)] % 2
                                    bN, B_bN = banks[2 + par]
                                    col = (qt % 4) * 128
                                    kts = list(range(max(qt - 1, 0), min(qt + 1, nkt - 1) + 1))

                                    def mmPV():
                                        ins = None
                                        for i_, k_ in enumerate(kts):
                                            PTk, _b, qak = hist[(dil, r, k_)]
                                            rhs = PTk[:, (qt - qak) * 128:(qt - qak + 1) * 128]
                                            ins = nc.tensor.matmul(bN[:, col:col + 128], lhsT=Vc[dil][:, r * nkt + k_, e, :], rhs=rhs,
                                                                   start=(i_ == 0), stop=(i_ == len(kts) - 1))
                                        return ins
                                    op("pe", mmPV, reads=[B_Vc[dil]] + [hist[(dil, r, k_)][1] for k_ in kts], writes=[B_bN], partial=True)
                                    if qt % 4 == 3 or qt == nkt - 1:
                                        ncols = (qt - seg * 4 + 1) * 128
                                        dst = ccols(dil, r, seg * 512, ncols)
                                        if dil == 1:
                                            op("act", lambda: nc.scalar.copy(out=acc[:, dst], in_=bN[:, 0:ncols]), reads=[B_bN], writes=[B_acc], partial=True)
                                        else:
                                            op("dve", lambda: nc.vector.tensor_tensor(out=acc[:, dst], in0=bN[:, 0:ncols], in1=acc[:, dst], op=ALU.add),
                                               reads=[B_bN, B_acc], writes=[B_acc], partial=True)
                            for c0 in range(0, S, 2048):
                                op("act", lambda c0=c0: nc.scalar.activation(out=acc[64:128, c0:c0 + 2048], in_=acc[64:128, c0:c0 + 2048], func=AF.Ln),
                                   reads=[B_acc], writes=[B_acc], partial=True)
                                op("act", lambda c0=c0: nc.scalar.activation(out=acc[64:128, c0:c0 + 2048], in_=acc[64:128, c0:c0 + 2048], func=AF.Exp, scale=-1.0),
                                   reads=[B_acc], writes=[B_acc], partial=True)
                            for c0 in range(0, S, 2048):
                                dma(rden[:, :], acc[64:128, c0:c0 + 2048], reads=[B_acc], writes=[B_rden])
                                op("dve", lambda c0=c0: nc.vector.tensor_tensor(out=ybT[:, c0:c0 + 2048], in0=acc[0:64, c0:c0 + 2048], in1=rden[:, :], op=ALU.mult),
                                   reads=[B_acc, B_rden], writes=[B_ybT], partial=True)
                            dma(ysc[8 + hp, e * 64:(e + 1) * 64, :], ybT[:, :], reads=[B_ybT], writes=[B_ysc], partial=True)
                    fw.sync_all()

            def phase_C():
                with ExitStack() as ph:
                    zT = sbt(ph, "zT", [128, 6, S + 30], BF16)
                    B_zT = fw.buf("zT")
                    op("pool", lambda: nc.gpsimd.memset(zT[:, :, 0:15], 0.0), writes=[B_zT], partial=True)
                    op("pool", lambda: nc.gpsimd.memset(zT[:, :, S + 15:S + 30], 0.0), writes=[B_zT], partial=True)
                    wcv = sbt(ph, "wcv", [128, 192], F32)
                    diagW = sbt(ph, "diagW", [128, 186, 128], BF16)
                    B_wcv, B_diagW = fw.buf("wcv"), fw.buf("diagW")
                    with ExitStack() as ph1:
                        stgc = sbt(ph1, "stgc", [96, 2, 128], F32)
                        B_stgc = fw.buf("stgc")
                        src = wl("c_dw_w").rearrange("j (ct p) -> (j ct) p", p=128)
                        dma(stgc[0:96, 0, :], src[0:96, :], writes=[B_stgc], partial=True)
                        dma(stgc[0:90, 1, :], src[96:186, :], writes=[B_stgc], partial=True)
                        pt, B_pt = banks[0]
                        op("pe", lambda: nc.tensor.transpose(out=pt[:, 0:96], in_=stgc[0:96, 0, :], identity=identf[0:96, 0:96]),
                           reads=[B_stgc, B_identf], writes=[B_pt], partial=True)
                        op("pe", lambda: nc.tensor.transpose(out=pt[:, 96:186], in_=stgc[0:90, 1, :], identity=identf[0:90, 0:90]),
                           reads=[B_stgc, B_identf], writes=[B_pt], partial=True)
                        op("act", lambda: nc.scalar.copy(out=wcv[:, 0:186], in_=pt[:, 0:186]), reads=[B_pt], writes=[B_wcv])
                        for idx in range(186):
                            eng = "dve" if idx % 2 == 0 else "pool"
                            e_ = nc.vector if eng == "dve" else nc.gpsimd
                            op(eng, lambda idx=idx, e_=e_: e_.tensor_scalar(out=diagW[:, idx, :], in0=identb[:, :], scalar1=wcv[:, idx:idx + 1], scalar2=None, op0=ALU.mult),
                               reads=[B_identb, B_wcv], writes=[B_diagW], partial=True)
                        wc = sbt(ph1, "wc", [128, 8, 1536], BF16)
                        B_wc = fw.buf("wc")
                        stgs = None
                        load_weight(wc, B_wc, lambda k, c0, c1: wl("w_in", slice(k * 128, (k + 1) * 128), slice(OFF_CU + c0, OFF_CU + c1)), 8, 1536, stgs, 1536)
                        sgs = [sbt(ph1, "csg%d" % i, [128, 512], F32) for i in range(2)]
                        B_sgs = [fw.buf("csg0"), fw.buf("csg1")]
                        ar, gr = Ring([0, 1]), Ring([2, 3])
                        n = 0
                        for c in range(NCH):
                            for ct in range(6):
                                pa, B_pa = banks[ar.next()]
                                pg, B_pg = banks[gr.next()]

                                def mm(pa=pa, pg=pg, ct=ct, c=c):
                                    ins = None
                                    for k in range(8):
                                        nc.tensor.matmul(pa[:, :], lhsT=wc[:, k, ct * 128:(ct + 1) * 128], rhs=hTf[:, k, c * 512:(c + 1) * 512], start=(k == 0), stop=(k == 7))
                                    for k in range(8):
                                        ins = nc.tensor.matmul(pg[:, :], lhsT=wc[:, k, 768 + ct * 128:768 + (ct + 1) * 128], rhs=hTf[:, k, c * 512:(c + 1) * 512],
                                                               start=(k == 0), stop=(k == 7))
                                    return ins
                                op("pe", mm, reads=[B_wc, B_hTf], writes=[B_pa, B_pg])
                                sg, B_sg = sgs[n % 2], B_sgs[n % 2]
                                n += 1
                                op("act", lambda pg=pg, sg=sg, ct=ct: nc.scalar.activation(out=sg[:, :], in_=pg[:, :], func=AF.Sigmoid,
                                                                                         bias=vecT[:, V_BC + 6 + ct:V_BC + 7 + ct]),
                                   reads=[B_pg, B_vecT], writes=[B_sg])
                                op("dve", lambda pa=pa, sg=sg, ct=ct, c=c: nc.vector.scalar_tensor_tensor(
                                    out=zT[:, ct, 15 + c * 512:15 + (c + 1) * 512], in0=pa[:, :], scalar=vecT[:, V_BC + ct:V_BC + ct + 1], in1=sg[:, :],
                                    op0=ALU.add, op1=ALU.mult), reads=[B_pa, B_sg, B_vecT], writes=[B_zT], partial=True)
                        fw.sync_all()
                    sqs = [sbt(ph, "csq%d" % i, [128, 512], BF16) for i in range(2)]
                    B_sqs = [fw.buf("csq0"), fw.buf("csq1")]
                    rstd = sbt(ph, "crstd", [128, 512], F32)
                    B_rstd = fw.buf("crstd")
                    tmps = [sbt(ph, "ctmp%d" % i, [128, 512], F32) for i in range(2)]
                    B_tmps = [fw.buf("ctmp0"), fw.buf("ctmp1")]
                    ycs = [sbt(ph, "cyc%d" % i, [128, 512], BF16) for i in range(3)]
                    B_ycs = [fw.buf("cyc%d" % i) for i in range(3)]
                    n = 0
                    for c in range(NCH):
                        for ct in range(6):
                            pc, B_pc = banks[ct]

                            def mmc(pc=pc, ct=ct, c=c):
                                ins = None
                                for j in range(CONV_W):
                                    ins = nc.tensor.matmul(pc[:, :], lhsT=diagW[:, j * 6 + ct, :], rhs=zT[:, ct, c * 512 + j:c * 512 + j + 512],
                                                           start=(j == 0), stop=(j == CONV_W - 1))
                                return ins
                            op("pe", mmc, reads=[B_diagW, B_zT], writes=[B_pc])
                            sq, B_sq = sqs[ct % 2], B_sqs[ct % 2]
                            op("act", lambda pc=pc, sq=sq, ct=ct: nc.scalar.activation(out=sq[:, :], in_=pc[:, :], func=AF.Square,
                                                                                     bias=vecT[:, V_DWB + ct:V_DWB + ct + 1]),
                               reads=[B_pc, B_vecT], writes=[B_sq])
                            pr, B_pr = banks[6]
                            op("pe", lambda pr=pr, sq=sq, ct=ct: nc.tensor.matmul(pr[:, :], lhsT=onesb[:, :], rhs=sq[:, :], start=(ct == 0), stop=(ct == 5)),
                               reads=[B_sq, B_onesb], writes=[B_pr], partial=(ct != 0))
                        pr, B_pr = banks[6]
                        op("act", lambda pr=pr: nc.scalar.activation(out=rstd[:, :], in_=pr[:, :], func=AF.Ln, scale=1.0 / C_CH, bias=EPS), reads=[B_pr], writes=[B_rstd])
                        op("act", lambda: nc.scalar.activation(out=rstd[:, :], in_=rstd[:, :], func=AF.Exp, scale=-0.5), reads=[B_rstd], writes=[B_rstd])
                        for ct in range(6):
                            pc, B_pc = banks[ct]
                            tmp, B_tmp = tmps[ct % 2], B_tmps[ct % 2]
                            yc, B_yc = ycs[n % 3], B_ycs[n % 3]
                            n += 1
                            op("dve", lambda pc=pc, tmp=tmp, ct=ct: nc.vector.scalar_tensor_tensor(out=tmp[:, :], in0=pc[:, :], scalar=vecT[:, V_DWB + ct:V_DWB + ct + 1],
                                                                                                 in1=rstd[:, :], op0=ALU.add, op1=ALU.mult),
                               reads=[B_pc, B_rstd, B_vecT], writes=[B_tmp])
                            op("act", lambda tmp=tmp, yc=yc, ct=ct: nc.scalar.activation(out=yc[:, :], in_=tmp[:, :], func=AF.Silu, scale=vecT[:, V_CN + ct:V_CN + ct + 1]),
                               reads=[B_tmp, B_vecT], writes=[B_yc])
                            dma(ysc[14 + ct, :, c * 512:(c + 1) * 512], yc[:, :], reads=[B_yc], writes=[B_ysc], partial=True)
                    fw.sync_all()

            def phase_merge():
                CW = 256
                with ExitStack() as ph:
                    wg = sbt(ph, "wg", [128, 8, 3072], BF16)
                    woa = sbt(ph, "woa", [128, 8, D], BF16)
                    wob = sbt(ph, "wob", [128, 6, D], BF16)
                    woc = sbt(ph, "woc", [128, 6, D], BF16)
                    wo = sbt(ph, "wo", [128, 8, D], BF16)
                    B_wg, B_woa, B_wob, B_woc, B_wo = (fw.buf(n_) for n_ in ("wg", "woa", "wob", "woc", "wo"))
                    stgs = None
                    load_weight(woa, B_woa, lambda k, c0, c1: wl("w_out_a", slice(k * 128, (k + 1) * 128), slice(c0, c1)), 8, D, stgs, 1024)
                    load_weight(wob, B_wob, lambda k, c0, c1: wl("w_out_b", slice(k * 128, (k + 1) * 128), slice(c0, c1)), 6, D, stgs, 1024)
                    load_weight(woc, B_woc, lambda k, c0, c1: wl("w_out_c", slice(k * 128, (k + 1) * 128), slice(c0, c1)), 6, D, stgs, 1024)
                    load_weight(wg, B_wg, lambda k, c0, c1: wl("w_in", slice(k * 128, (k + 1) * 128), slice(OFF_G + c0, OFF_G + c1)), 8, 3072, stgs, 1024)
                    load_weight(wo, B_wo, lambda k, c0, c1: wl("w_out", slice(k * 128, (k + 1) * 128), slice(c0, c1)), 8, D, stgs, 1024)
                    yT = sbt(ph, "myT", [128, 20, CW], BF16)
                    xc = sbt(ph, "mxc", [128, CW // 128, D], F32)
                    mT = sbt(ph, "mmT", [128, 8, CW], BF16)
                    B_yT, B_xc, B_mT = fw.buf("myT"), fw.buf("mxc"), fw.buf("mmT")
                    gs = [sbt(ph, "mg%d" % i, [128, 3, CW], F32) for i in range(2)]
                    B_gs = [fw.buf("mg0"), fw.buf("mg1")]
                    m1s = [sbt(ph, "mm1%d" % i, [128, 3, CW], F32) for i in range(1)] * 2
                    B_m1s = [fw.buf("mm10")] * 2
                    zr = Ring([0, 3])
                    xr = Ring([6, 7])
                    nn = 0
                    ntl = CW // 128

                    def slot(base, i):
                        return banks[base + i // 2][0][:, (i % 2) * CW:(i % 2 + 1) * CW]
                    for c in range(S // CW):
                        c0 = c * CW
                        dma(yT[:, :, :], ysc[:, :, c0:c0 + CW].rearrange("c p n -> p c n"), reads=[B_ysc], writes=[B_yT])
                        dma(xc[:, :, :], xchunk(c, CW), reads=[B_x[c0 // 512]], writes=[B_xc])
                        for m in range(8):
                            base = zr.next()
                            B_db = [banks[base][1], banks[base + 1][1], banks[base + 2][1]]

                            def mmz(base=base, m=m, c0=c0):
                                ins = None
                                for br, (wt, nk, ko) in enumerate(((woa, 8, 0), (wob, 6, 8), (woc, 6, 14))):
                                    for k in range(nk):
                                        nc.tensor.matmul(slot(base, br), lhsT=wt[:, k, m * 128:(m + 1) * 128], rhs=yT[:, ko + k, :], start=(k == 0), stop=(k == nk - 1))
                                for br in range(3):
                                    for k in range(8):
                                        ins = nc.tensor.matmul(slot(base, 3 + br), lhsT=wg[:, k, br * D + m * 128:br * D + (m + 1) * 128], rhs=hTf[:, k, c0:c0 + CW],
                                                               start=(k == 0), stop=(k == 7))
                                return ins
                            op("pe", mmz, reads=[B_woa, B_wob, B_woc, B_wg, B_yT, B_hTf], writes=B_db)

                            def zslice(db, br, base=base):
                                return slot(base, br)

                            def gslice(db, br, base=base):
                                return slot(base, 3 + br)
                            db = None
                            g, B_g = gs[nn % 2], B_gs[nn % 2]
                            m1, B_m1 = m1s[nn % 2], B_m1s[nn % 2]
                            nn += 1
                            for br in range(3):
                                op("act", lambda db=db, g=g, br=br, m=m: nc.scalar.activation(out=g[:, br, :], in_=gslice(db, br), func=AF.Sigmoid,
                                                                                             bias=vecT[:, V_BG + br * 8 + m:V_BG + br * 8 + m + 1]),
                                   reads=B_db + [B_vecT], writes=[B_g], partial=True)
                            for br in range(3):
                                op("dve", lambda db=db, g=g, m1=m1, br=br: nc.vector.tensor_tensor(out=m1[:, br, :], in0=g[:, br, :], in1=zslice(db, br), op=ALU.mult),
                                   reads=B_db + [B_g], writes=[B_m1], partial=True)
                            op("dve", lambda m1=m1: nc.vector.tensor_tensor(out=m1[:, 0, :], in0=m1[:, 0, :], in1=m1[:, 1, :], op=ALU.add), reads=[B_m1], writes=[B_m1], partial=True)
                            op("dve", lambda m1=m1, m=m: nc.vector.tensor_tensor(out=mT[:, m, :], in0=m1[:, 0, :], in1=m1[:, 2, :], op=ALU.add),
                               reads=[B_m1], writes=[B_mT], partial=True)
                        for t in range(ntl):
                            for oh in range(2):
                                px, B_px = banks[xr.next()]

                                def mmx(px=px, t=t, oh=oh):
                                    ins = None
                                    for k in range(8):
                                        ins = nc.tensor.matmul(px[:, :], lhsT=mT[:, k, t * 128:(t + 1) * 128], rhs=wo[:, k, oh * 512:(oh + 1) * 512], start=(k == 0), stop=(k == 7))
                                    return ins
                                op("pe", mmx, reads=[B_mT, B_wo], writes=[B_px])
                                op("dve", lambda px=px, t=t, oh=oh: nc.vector.tensor_tensor(out=xc[:, t, oh * 512:(oh + 1) * 512], in0=px[:, :], in1=xc[:, t, oh * 512:(oh + 1) * 512], op=ALU.add),
                                   reads=[B_px, B_xc], writes=[B_xc], partial=True)
                        dma(xchunk(c, CW), xc[:, :, :], reads=[B_xc], writes=[B_x[c0 // 512]])
                    fw.sync_all()


            if "ffn1" in cfg.phases:
                phase_ffn(0)
            hT_stack = ExitStack()
            hTf = sbt(hT_stack, "hTf", [128, 8, S], BF16)
            B_hTf = fw.buf("hTf")
            if "hT" in cfg.phases:
                phase_hT()
            if "A" in cfg.phases:
                phase_A()
            if "B" in cfg.phases:
                phase_B()
            if "C" in cfg.phases:
                phase_C()
            if "merge" in cfg.phases:
                phase_merge()
            hT_stack.close()
            if "ffn2" in cfg.phases:
                phase_ffn(1)

        with nc.Fori(0, L) as l:
            nc.sync.dma_start(out=wcur[:, :], in_=wblob[bass.ds(l, 1), 0:PIECE_ROWS, :].rearrange("o r c -> (o r) c")).then_inc(s_wc, 16)
            nc.sync.wait_ge(s_wc, 16)
            with ExitStack() as cv:
                HALF = PIECE_ROWS // 2
                RPP = HALF // 128
                NEL = RPP * BLOB_COLS
                st32 = [sbt(cv, "cv32_%d" % i, [128, NEL], F32) for i in range(2)]
                st16 = [sbt(cv, "cv16_%d" % i, [128, NEL], BF16) for i in range(2)]
                B32 = [fw.buf("cv32_0"), fw.buf("cv32_1")]
                B16 = [fw.buf("cv16_0"), fw.buf("cv16_1")]
                with nc.Fori(0, NPIECE) as pi:
                    for hf in range(2):
                        r0 = pi * PIECE_ROWS + hf * HALF
                        dma(st32[hf][:, :], wblob[bass.ds(l, 1), bass.ds(r0, HALF), :].rearrange("o (p r) c -> p (o r c)", p=128), writes=[B32[hf]])
                    for hf in range(2):
                        NCK = 8
                        CK = NEL // NCK
                        for ck in range(NCK):
                            ek = ("act", "dve", "pool", "act", "dve", "act", "dve", "pool")[ck]
                            if ek == "act":
                                op("act", lambda: nc.scalar.copy(out=st16[hf][:, ck * CK:(ck + 1) * CK], in_=st32[hf][:, ck * CK:(ck + 1) * CK]),
                                   reads=[B32[hf]], writes=[B16[hf]], partial=True)
                            elif ek == "dve":
                                op("dve", lambda: nc.vector.tensor_copy(out=st16[hf][:, ck * CK:(ck + 1) * CK], in_=st32[hf][:, ck * CK:(ck + 1) * CK]),
                                   reads=[B32[hf]], writes=[B16[hf]], partial=True)
                            else:
                                op("pool", lambda: nc.gpsimd.tensor_copy(out=st16[hf][:, ck * CK:(ck + 1) * CK], in_=st32[hf][:, ck * CK:(ck + 1) * CK]),
                                   reads=[B32[hf]], writes=[B16[hf]], partial=True)
                        r0 = pi * PIECE_ROWS + hf * HALF
                        dma(wcur16[bass.ds(r0, HALF), :].rearrange("(p r) c -> p (r c)", p=128), st16[hf][:, :], reads=[B16[hf]])
                    fw.hard_barrier()
            for s_ in range(NSEQ):
                body(l, s_)
                fw.hard_barrier()
    return nc, fw


def kernel(**inputs):
    cfg = Cfg()
    return run_kernel(cfg, inputs)


def run_kernel(cfg, inputs, n_cores=8, trace=False):
    nc, fw = build_program(cfg)
    x = np.ascontiguousarray(inputs["x"], dtype=np.float32)
    lay, brows = blob_layout()
    blob = np.zeros((cfg.L, brows * BLOB_COLS), dtype=np.float32)
    for name, shp in W_SHAPES:
        off = lay[name][0]
        n = int(np.prod(shp))
        if name == "lam0":
            for l in range(cfg.L):
                blob[l, off] = 0.8 - 0.6 * math.exp(-0.3 * l)
                blob[l, off + 1] = 1.0 - (0.8 - 0.6 * math.exp(-0.3 * l))
        elif name == "aq_aug":
            j = np.arange(512)
            t = np.zeros((4, A_HEADS, 512), dtype=np.float32)
            for h in range(A_HEADS):
                t[0, h] = -SLOPES_A[h] * (j % 256)
                t[1, h] = -SLOPES_A[h] * 256.0 * (j // 256)
                t[2, h] = 1.0
            blob[:, off:off + n] = t.reshape(1, n)
        elif name == "ak_aug":
            i = np.arange(128)
            t = np.zeros((4, A_HEADS, 2, 128), dtype=np.float32)
            for h in range(A_HEADS):
                t[0, h, 0] = 1.0
                t[1, h, 0] = 1.0
                t[2, h, 0] = SLOPES_A[h] * i
                t[0, h, 1] = -1.0
                t[1, h, 1] = -1.0
                t[2, h, 1] = -SLOPES_A[h] * i
            blob[:, off:off + n] = t.reshape(1, n)
        else:
            blob[:, off:off + n] = np.asarray(inputs[name], dtype=np.float32).reshape(cfg.L, n)
    blob = blob.reshape(cfg.L, brows, BLOB_COLS)
    in_maps = []
    for c in range(n_cores):
        in_maps.append({"x": x[c * cfg.NSEQ:(c + 1) * cfg.NSEQ], "wblob": blob})
    res = run_bass_kernel_spmd(nc, in_maps, core_ids=list(range(n_cores)), **({"trace": True} if trace else {}))
    if trace:
        print("EXEC_NS", res.exec_time_ns, "n_instr", fw.n_instr)
    outs = np.concatenate([r["out"] for r in res.results], axis=0)
    if cfg.debug:
        return outs, res.results
    return outs
```

```python
import math
import os
from contextlib import ExitStack

import numpy as np
import concourse.bass as bass
import concourse.mybir as mybir
from concourse.bass_utils import run_bass_kernel_spmd

F32 = mybir.dt.float32
BF16 = mybir.dt.bfloat16
I32 = mybir.dt.int32
AF = mybir.ActivationFunctionType
ALU = mybir.AluOpType
AX = mybir.AxisListType

D = 1024
DEPTH = 4
HD = 64
A_HEADS = 8
B_HEADS = 12
C_CH = 768
CONV_W = 31
D_FF = 2816
OFF_AQ, OFF_AK, OFF_AV = 0, 1024, 2048
OFF_BQ, OFF_BK, OFF_BV = 3072, 3840, 4608
OFF_CU = 5376
OFF_G = 6912
IN_W = 9984
EPS = 1e-6
ATTN_SCALE = HD ** -0.5
PATTERNS = ((128, 1), (512, 4), (2048, 16))
SLOPES_A = [2.0 ** (-8.0 * i / A_HEADS) for i in range(1, A_HEADS + 1)]
SLOPES_B = [2.0 ** (-8.0 * i / B_HEADS) for i in range(1, B_HEADS + 1)]
NEG_BIG = -1.0e6


class Buf:
    __slots__ = ("name", "w", "r", "excl")

    def __init__(self, name="", excl=False):
        self.name = name
        self.w = {}
        self.r = {}
        self.excl = excl


class Eng:
    def __init__(self, fw, key, e, sem):
        self.fw, self.key, self.e, self.sem = fw, key, e, sem
        self.count = 0
        self.seen = {}

    def wait_ev(self, key, cnt):
        if self.seen.get(key, 0) >= cnt:
            return
        self.e.wait_ge(self.fw.sems[key], cnt)
        self.seen[key] = cnt


class FW:
    NDMA = 16

    def __init__(self, nc, stack):
        self.nc = nc
        self.sems = {}
        self.engs = {}
        for key, e in (("pe", nc.tensor), ("act", nc.scalar), ("dve", nc.vector),
                       ("pool", nc.gpsimd), ("sp", nc.sync)):
            sem = stack.enter_context(nc.semaphore("s_" + key))
            self.sems[key] = sem
            self.engs[key] = Eng(self, key, e, sem)
        self.dma_keys = []
        for i in range(self.NDMA):
            k = "dma%d" % i
            self.sems[k] = stack.enter_context(nc.semaphore("s_" + k))
            self.dma_keys.append(k)
        self.dma_cnt = {k: 0 for k in self.dma_keys}
        self.dma_rr = 0
        self.n_instr = 0
        self.bufs = []

    def buf(self, name="", excl=False):
        b = Buf(name, excl)
        self.bufs.append(b)
        return b

    def reset_state(self):
        for e in self.engs.values():
            e.count = 0
            e.seen = {}
        self.dma_cnt = {k: 0 for k in self.dma_keys}
        self.dma_rr = 0
        for b in self.bufs:
            b.w = {}
            b.r = {}

    def _deps(self, eng, reads, writes):
        for b in reads:
            for k, c in b.w.items():
                eng.wait_ev(k, c)
            if b.excl:
                for k, c in b.r.items():
                    if k != eng.key:
                        eng.wait_ev(k, c)
        for b in writes:
            for k, c in b.w.items():
                if k != eng.key:
                    eng.wait_ev(k, c)
            for k, c in b.r.items():
                if k != eng.key:
                    eng.wait_ev(k, c)

    def _record(self, key, cnt, reads, writes, partial):
        for b in reads:
            if b.r.get(key, 0) < cnt:
                b.r[key] = cnt
        for b in writes:
            if not partial:
                b.r = {}
                b.w = {}
            b.w[key] = cnt

    def op(self, ek, fn, reads=(), writes=(), partial=False):
        eng = self.engs[ek]
        self._deps(eng, reads, writes)
        ins = fn()
        eng.count += 1
        ins.then_inc(eng.sem, 1)
        self._record(ek, eng.count, reads, writes, partial)
        self.n_instr += 1

    def dma(self, out, in_, reads=(), writes=(), partial=False, **kw):
        sp = self.engs["sp"]
        self._deps(sp, reads, writes)
        k = self.dma_keys[self.dma_rr % self.NDMA]
        self.dma_rr += 1
        if self.dma_cnt[k] > 0:
            sp.wait_ev(k, self.dma_cnt[k])
        self.nc.sync.dma_start(out=out, in_=in_, **kw).then_inc(self.sems[k], 16)
        self.dma_cnt[k] += 16
        self._record(k, self.dma_cnt[k], reads, writes, partial)
        self.n_instr += 1

    def drain_dmas(self):
        sp = self.engs["sp"]
        for k in self.dma_keys:
            if self.dma_cnt[k] > 0:
                sp.wait_ev(k, self.dma_cnt[k])

    def sync_all(self):
        for e in self.engs.values():
            for f in self.engs.values():
                if f is not e and f.count > 0:
                    e.wait_ev(f.key, f.count)
            for k in self.dma_keys:
                if self.dma_cnt[k] > 0:
                    e.wait_ev(k, self.dma_cnt[k])

    def hard_barrier(self):
        self.drain_dmas()
        self.nc.all_engine_barrier()
        for s in list(self.sems.values()) + list(getattr(self, "extra_sems", [])):
            self.nc.gpsimd.sem_clear(s)
        self.nc.all_engine_barrier()
        self.reset_state()


class Ring:
    def __init__(self, items):
        self.items = list(items)
        self.i = 0

    def next(self):
        it = self.items[self.i % len(self.items)]
        self.i += 1
        return it


BIG_W = ("ffn1_w_up", "ffn1_w_down", "w_in", "w_out_a", "w_out_b", "w_out_c", "w_out", "ffn2_w_up", "ffn2_w_down")
W_SHAPES = (("ffn1_norm", (D,)), ("mix_norm", (D,)), ("b_in", (IN_W,)),
            ("a_q_norm", (HD,)), ("a_k_norm", (HD,)), ("a_lambda", (4, HD)),
            ("a_sub_norm", (2 * HD,)), ("b_q_norm", (HD,)),
            ("b_k_norm", (HD,)), ("c_dw_w", (CONV_W, C_CH)),
            ("c_dw_b", (C_CH,)), ("c_norm", (C_CH,)), ("ffn2_norm", (D,)),
            ("lam0", (2,)), ("aq_aug", (4, A_HEADS * 512)), ("ak_aug", (4, A_HEADS * 2 * 128)),
            ("ffn1_w_up", (D, 2 * D_FF)), ("ffn1_w_down", (D_FF, D)), ("w_in", (D, IN_W)),
            ("w_out_a", (D, D)), ("w_out_b", (768, D)), ("w_out_c", (C_CH, D)), ("w_out", (D, D)),
            ("ffn2_w_up", (D, 2 * D_FF)), ("ffn2_w_down", (D_FF, D)))
BLOB_COLS = 2048
PIECE_ROWS = 1024


def blob_layout():
    off = 0
    lay = {}
    for name, shp in W_SHAPES:
        n = int(np.prod(shp))
        lay[name] = (off, shp)
        off += (n + 63) // 64 * 64
    rows = (off + BLOB_COLS - 1) // BLOB_COLS
    rows = (rows + PIECE_ROWS - 1) // PIECE_ROWS * PIECE_ROWS
    return lay, rows


class Cfg:
    def __init__(self, S=4096, NSEQ=2, L=DEPTH, phases=None, a_heads=None, b_pairs=None, debug=False):
        self.S, self.NSEQ, self.L = S, NSEQ, L
        self.phases = phases or ("ffn1", "hT", "A", "B", "C", "merge", "ffn2")
        self.a_heads = list(range(A_HEADS)) if a_heads is None else a_heads
        self.b_pairs = list(range(B_HEADS // 2)) if b_pairs is None else b_pairs
        self.debug = debug


def build_program(cfg):
    S, NSEQ, L = cfg.S, cfg.NSEQ, cfg.L
    NT = S // 128
    NCH = S // 512
    nc = bass.Bass("TRN2", target_bir_lowering=False)

    def din(name, shape):
        return nc.dram_tensor(name, list(shape), F32, kind="ExternalInput").ap()

    x_in = din("x", [NSEQ, S, D])
    LAY, BROWS = blob_layout()
    NPIECE = BROWS // PIECE_ROWS
    wblob = din("wblob", [L, BROWS, BLOB_COLS])
    wcur = nc.dram_tensor("wcur", [PIECE_ROWS, BLOB_COLS], F32).ap()
    wcur16 = nc.dram_tensor("wcur16", [BROWS, BLOB_COLS], BF16).ap()
    xcur = nc.dram_tensor("xcur", [S, D], F32).ap()
    out = nc.dram_tensor("out", [NSEQ, S, D], F32, kind="ExternalOutput").ap()
    ysc_kind = "ExternalOutput" if cfg.debug else "Internal"
    ysc = nc.dram_tensor("ysc", [20, 128, S], BF16, kind=ysc_kind).ap()
    hdbg = nc.dram_tensor("hdbg", [8, 128, S], BF16, kind="ExternalOutput").ap() if cfg.debug else None

    with ExitStack() as st:
        fw = FW(nc, st)
        op, dma = fw.op, fw.dma
        s_wc = st.enter_context(nc.semaphore("s_wc"))
        s_xc = st.enter_context(nc.semaphore("s_xc"))
        fw.extra_sems = [s_wc, s_xc]

        uniq = [0]

        def sbt(stack, name, shape, dt):
            uniq[0] += 1
            return stack.enter_context(nc.sbuf_tensor("%s_%d" % (name, uniq[0]), list(shape), dt))

        identb = sbt(st, "identb", [128, 128], BF16)
        identf = sbt(st, "identf", [128, 128], F32)
        onesb = sbt(st, "onesb", [128, 128], BF16)
        vecT = sbt(st, "vecT", [128, 72], F32)
        small = sbt(st, "small", [128, 16], F32)
        B_identb, B_identf, B_onesb, B_vecT, B_small = (fw.buf(n) for n in ("identb", "identf", "onesb", "vecT", "small"))
        banks = []
        dbanks = []
        for i in range(4):
            t = st.enter_context(nc.psum_tensor("dbank%d" % i, [128, 1024], F32))
            dbanks.append(t)
            banks.append((t[:, 0:512], fw.buf("bank%d" % (2 * i), excl=True)))
            banks.append((t[:, 512:1024], fw.buf("bank%d" % (2 * i + 1), excl=True)))

        def bank_bf(i):
            return banks[i][0][:, :].bitcast(BF16)

        op("pool", lambda: nc.gpsimd.memset(identf[:], 0.0), writes=[B_identf])
        op("pool", lambda: nc.gpsimd.affine_select(out=identf[:], in_=identf[:], pattern=[[-1, 128]], compare_op=ALU.not_equal,
                                                   fill=1.0, base=0, channel_multiplier=1), reads=[B_identf], writes=[B_identf])
        op("pool", lambda: nc.gpsimd.memset(onesb[:], 1.0), writes=[B_onesb])
        op("dve", lambda: nc.vector.tensor_copy(out=identb[:], in_=identf[:]), reads=[B_identf], writes=[B_identb])

        for s_ in range(NSEQ):
            for c in range(NCH):
                dma(out[s_, c * 512:(c + 1) * 512, :], x_in[s_, c * 512:(c + 1) * 512, :])
        fw.hard_barrier()

        def body(l, s):
            def wl(name, *idx):
                off, shp = LAY[name]
                n = int(np.prod(shp))
                src_ = wcur16 if name in BIG_W else wcur
                flat = src_.rearrange("r c -> (r c)")[off:off + n]
                if len(shp) == 2:
                    flat = flat.rearrange("(a b) -> a b", b=shp[1])
                return flat[idx] if idx else flat

            def xchunk(c, n=512):
                return out[s, c * n:(c + 1) * n, :].rearrange("(t p) f -> p t f", p=128)

            B_x = [fw.buf("xdram%d" % c) for c in range(NCH)]
            B_ysc = fw.buf("ysc")

            def phase0():
                with ExitStack() as ph:
                    stgv = sbt(ph, "stgv", [72, 128], F32)
                    B_stgv = fw.buf("stgv")
                    rows = 0
                    for name, n in (("ffn1_norm", 8), ("mix_norm", 8), ("ffn2_norm", 8)):
                        dma(stgv[rows:rows + n, :], wl(name).rearrange("(k p) -> k p", p=128), writes=[B_stgv], partial=True)
                        rows += n
                    dma(stgv[24:36, :], wl("b_in", slice(OFF_CU, OFF_CU + 1536)).rearrange("(k p) -> k p", p=128), writes=[B_stgv], partial=True)
                    dma(stgv[36:60, :], wl("b_in", slice(OFF_G, OFF_G + 3072)).rearrange("(k p) -> k p", p=128), writes=[B_stgv], partial=True)
                    dma(stgv[60:66, :], wl("c_dw_b").rearrange("(k p) -> k p", p=128), writes=[B_stgv], partial=True)
                    dma(stgv[66:72, :], wl("c_norm").rearrange("(k p) -> k p", p=128), writes=[B_stgv], partial=True)
                    pt, B_pt = banks[0]
                    op("pe", lambda: nc.tensor.transpose(out=pt[:, 0:72], in_=stgv[0:72, :], identity=identf[0:72, 0:72]),
                       reads=[B_stgv, B_identf], writes=[B_pt])
                    op("act", lambda: nc.scalar.copy(out=vecT[:, :], in_=pt[:, 0:72]), reads=[B_pt], writes=[B_vecT])
                    B_sm_in = fw.buf("sm_in")
                    smi = sbt(ph, "smi", [128, 8], F32)
                    for col, name in ((0, "a_q_norm"), (1, "a_k_norm"), (2, "b_q_norm"), (3, "b_k_norm")):
                        src = wl(name).rearrange("(d i) -> d i", i=1)
                        dma(smi[0:64, col:col + 1], src, writes=[B_sm_in], partial=True)
                        dma(smi[64:128, col:col + 1], src, writes=[B_sm_in], partial=True)
                    dma(smi[:, 4:5], wl("a_sub_norm").rearrange("(d i) -> d i", i=1), writes=[B_sm_in], partial=True)
                    dma(smi[:, 5:7], wl("lam0").rearrange("(o c) -> o c", o=1).partition_broadcast(128), writes=[B_sm_in], partial=True)
                    lamt = sbt(ph, "lamt", [128, 4, HD], F32)
                    B_lamt = fw.buf("lamt")
                    dma(lamt[:, :, :].rearrange("p a d -> p (a d)"),
                        wl("a_lambda").rearrange("a d -> (a d)").rearrange("(o n) -> o n", o=1).partition_broadcast(128), writes=[B_lamt])
                    lprod = sbt(ph, "lprod", [128, 2, HD], F32)
                    lsum = sbt(ph, "lsum", [128, 4], F32)
                    B_lp, B_ls = fw.buf("lprod"), fw.buf("lsum")
                    op("dve", lambda: nc.vector.tensor_tensor(out=lprod[:, :, :], in0=lamt[:, 0:4:2, :], in1=lamt[:, 1:4:2, :], op=ALU.mult),
                       reads=[B_lamt], writes=[B_lp])
                    op("dve", lambda: nc.vector.tensor_reduce(out=lsum[:, 0:2], in_=lprod[:, :, :], axis=AX.X, op=ALU.add),
                       reads=[B_lp], writes=[B_ls])
                    op("act", lambda: nc.scalar.activation(out=lsum[:, 2:4], in_=lsum[:, 0:2], func=AF.Exp), reads=[B_ls], writes=[B_ls], partial=True)
                    op("dve", lambda: nc.vector.tensor_tensor(out=small[:, 5:6], in0=lsum[:, 2:3], in1=lsum[:, 3:4], op=ALU.subtract),
                       reads=[B_ls], writes=[B_small], partial=True)
                    op("dve", lambda: nc.vector.tensor_tensor(out=small[:, 5:6], in0=small[:, 5:6], in1=smi[:, 5:6], op=ALU.add),
                       reads=[B_small, B_sm_in], writes=[B_small], partial=True)
                    op("dve", lambda: nc.vector.tensor_scalar(out=small[:, 6:7], in0=small[:, 5:6], scalar1=-1.0, scalar2=None, op0=ALU.mult),
                       reads=[B_small], writes=[B_small], partial=True)
                    op("dve", lambda: nc.vector.tensor_scalar(out=small[:, 0:1], in0=smi[:, 0:1], scalar1=ATTN_SCALE, scalar2=None, op0=ALU.mult),
                       reads=[B_sm_in], writes=[B_small], partial=True)
                    op("dve", lambda: nc.vector.tensor_copy(out=small[:, 1:2], in_=smi[:, 1:2]), reads=[B_sm_in], writes=[B_small], partial=True)
                    op("dve", lambda: nc.vector.tensor_scalar(out=small[:, 2:3], in0=smi[:, 2:3], scalar1=ATTN_SCALE, scalar2=None, op0=ALU.mult),
                       reads=[B_sm_in], writes=[B_small], partial=True)
                    op("dve", lambda: nc.vector.tensor_copy(out=small[:, 3:4], in_=smi[:, 3:4]), reads=[B_sm_in], writes=[B_small], partial=True)
                    op("dve", lambda: nc.vector.tensor_tensor(out=small[:, 4:5], in0=smi[:, 4:5], in1=smi[:, 6:7], op=ALU.mult),
                       reads=[B_sm_in], writes=[B_small], partial=True)
                    fw.sync_all()
            if s == 0:
                phase0()

            G_FFN1, G_MIX, G_FFN2, V_BC, V_BG, V_DWB, V_CN = 0, 8, 16, 24, 36, 60, 66

            def norm_transpose(xc, B_xc, ntile, xn, B_xn, stat, B_stat, gcol, dst_fn, B_dst, tbanks, tcols, part=0):
                for t in range(ntile if part in (0, 1) else 0):
                    op("act", lambda t=t: nc.scalar.activation(out=xn[:, t, :], in_=xc[:, t, :], func=AF.Square,
                                                               accum_out=stat[:, t:t + 1]),
                       reads=[B_xc], writes=[B_xn, B_stat], partial=True)
                if part in (0, 1):
                    op("dve", lambda: nc.vector.tensor_scalar(out=stat[:, 8:8 + ntile], in0=stat[:, 0:ntile], scalar1=1.0 / D, scalar2=EPS,
                                                              op0=ALU.mult, op1=ALU.add), reads=[B_stat], writes=[B_stat], partial=True)
                    op("act", lambda: nc.scalar.activation(out=stat[:, 16:16 + ntile], in_=stat[:, 8:8 + ntile], func=AF.Sqrt),
                       reads=[B_stat], writes=[B_stat], partial=True)
                    op("dve", lambda: nc.vector.reciprocal(out=stat[:, 24:24 + ntile], in_=stat[:, 16:16 + ntile]),
                       reads=[B_stat], writes=[B_stat], partial=True)
                for t in range(ntile if part in (0, 1) else 0):
                    op("dve", lambda t=t: nc.vector.tensor_scalar(out=xn[:, t, :], in0=xc[:, t, :], scalar1=stat[:, 24 + t:25 + t],
                                                                  scalar2=None, op0=ALU.mult),
                       reads=[B_xc, B_stat], writes=[B_xn], partial=True)
                for fc in range(8 if part in (0, 2) else 0):
                    bi = tbanks[fc % len(tbanks)]
                    ptb, B_ptb = bank_bf(bi), banks[bi][1]

                    def tr(fc=fc, ptb=ptb):
                        ins = None
                        for t in range(ntile):
                            ins = nc.tensor.transpose(out=ptb[:, t * 128:(t + 1) * 128], in_=xn[:, t, fc * 128:(fc + 1) * 128],
                                                      identity=identb[:, :])
                        return ins
                    op("pe", tr, reads=[B_xn, B_identb], writes=[B_ptb])
                    eng = "act" if fc % 2 == 0 else "dve"
                    if eng == "act":
                        op("act", lambda fc=fc, ptb=ptb: nc.scalar.activation(out=dst_fn(fc), in_=ptb[:, 0:ntile * 128], func=AF.Copy,
                                                                              scale=vecT[:, gcol + fc:gcol + fc + 1]),
                           reads=[B_ptb, B_vecT], writes=[B_dst], partial=True)
                    else:
                        op("dve", lambda fc=fc, ptb=ptb: nc.vector.tensor_scalar(out=dst_fn(fc), in0=ptb[:, 0:ntile * 128],
                                                                                 scalar1=vecT[:, gcol + fc:gcol + fc + 1], scalar2=None, op0=ALU.mult),
                           reads=[B_ptb, B_vecT], writes=[B_dst], partial=True)

            def load_weight(dst, B_dst, src_fn, nk, ncols, stg_ring=None, max_cols=None):
                for k in range(nk):
                    dma(dst[:, k, 0:ncols], src_fn(k, 0, ncols), writes=[B_dst], partial=True)

            def phase_ffn(which):
                gcol = G_FFN1 if which == 0 else G_FFN2
                wun, wdn_n = ("ffn1_w_up", "ffn1_w_down") if which == 0 else ("ffn2_w_up", "ffn2_w_down")
                with ExitStack() as ph:
                    wup = sbt(ph, "wup", [128, 8, 2 * D_FF], BF16)
                    wdn = sbt(ph, "wdn", [128, 22, D], BF16)
                    B_wup, B_wdn = fw.buf("wup"), fw.buf("wdn")
                    stgs = None
                    xc = sbt(ph, "fxc", [128, 4, D], F32)
                    xc2 = sbt(ph, "fxc2", [128, 4, D], F32)
                    B_xc2 = fw.buf("fxc2")
                    xn = sbt(ph, "fxn", [128, 4, D], BF16)
                    hTc = sbt(ph, "fhT", [128, 8, 512], BF16)
                    uT = [sbt(ph, "fuT%d" % i, [128, 11, 512], BF16) for i in range(2)]
                    sa = [sbt(ph, "fsa%d" % i, [128, 512], F32) for i in range(2)]
                    stat = sbt(ph, "fstat", [128, 32], F32)
                    B_xc, B_xn, B_hTc, B_stat = fw.buf("fxc"), fw.buf("fxn"), fw.buf("fhT"), fw.buf("fstat")
                    B_uT = [fw.buf("fuT0"), fw.buf("fuT1")]
                    B_sa = [fw.buf("fsa0"), fw.buf("fsa1")]
                    load_weight(wup, B_wup, lambda k, c0, c1: wl(wun, slice(k * 128, (k + 1) * 128), slice(c0, c1)), 8, 2 * D_FF, stgs, 1408)
                    load_weight(wdn, B_wdn, lambda k, c0, c1: wl(wdn_n, slice(k * 128, (k + 1) * 128), slice(c0, c1)), 22, D, stgs, 1408)
                    abank = Ring([2, 3])
                    bbank = Ring([4, 5])
                    ybank = Ring([6, 7])
                    sai = 0
                    xcs_ = [xc, xc2]
                    B_xcs_ = [B_xc, B_xc2]

                    def load_x(c):
                        for t_ in range(4):
                            dma(xcs_[c % 2][:, t_, :], xchunk(c)[:, t_, :], reads=[B_x[c]], writes=[B_xcs_[c % 2]], partial=(t_ > 0))

                    def nt(c, part):
                        norm_transpose(xcs_[c % 2], B_xcs_[c % 2], 4, xn, B_xn, stat, B_stat, gcol, lambda fc: hTc[:, fc, :], B_hTc, [0, 1], 512, part=part)
                    load_x(0)
                    nt(0, 0)
                    for c in range(NCH):
                        xc_, B_xc_ = xcs_[c % 2], B_xcs_[c % 2]
                        if c + 1 < NCH:
                            load_x(c + 1)
                        for half in range(2):
                            if half == 1 and c + 1 < NCH:
                                nt(c + 1, 1)
                            for jj in range(11):
                                j = half * 11 + jj
                                ai, bi = abank.next(), bbank.next()
                                pa, B_pa = banks[ai]
                                pb, B_pb = banks[bi]

                                def mm_up(pa=pa, pb=pb, j=j):
                                    ins = None
                                    for k in range(8):
                                        nc.tensor.matmul(pa[:, :], lhsT=wup[:, k, j * 128:(j + 1) * 128], rhs=hTc[:, k, :], start=(k == 0), stop=(k == 7))
                                    for k in range(8):
                                        ins = nc.tensor.matmul(pb[:, :], lhsT=wup[:, k, D_FF + j * 128:D_FF + (j + 1) * 128], rhs=hTc[:, k, :],
                                                               start=(k == 0), stop=(k == 7))
                                    return ins
                                op("pe", mm_up, reads=[B_wup, B_hTc], writes=[B_pa, B_pb])
                                sat, B_sat = sa[sai % 2], B_sa[sai % 2]
                                sai += 1
                                op("act", lambda pa=pa, sat=sat: nc.scalar.activation(out=sat[:, :], in_=pa[:, :], func=AF.Silu),
                                   reads=[B_pa], writes=[B_sat])
                                op("dve", lambda pb=pb, sat=sat, half=half, jj=jj: nc.vector.tensor_tensor(out=uT[half][:, jj, :], in0=sat[:, :], in1=pb[:, :], op=ALU.mult),
                                   reads=[B_sat, B_pb], writes=[B_uT[half]], partial=True)
                            if half == 1 and c + 1 < NCH:
                                nt(c + 1, 2)
                            for t in range(4):
                                for oh in range(2):
                                    yi = ybank.next()
                                    py, B_py = banks[yi]

                                    def mm_dn(py=py, t=t, oh=oh, half=half):
                                        ins = None
                                        for jj in range(11):
                                            ins = nc.tensor.matmul(py[:, :], lhsT=uT[half][:, jj, t * 128:(t + 1) * 128],
                                                                   rhs=wdn[:, half * 11 + jj, oh * 512:(oh + 1) * 512], start=(jj == 0), stop=(jj == 10))
                                        return ins
                                    op("pe", mm_dn, reads=[B_uT[half], B_wdn], writes=[B_py])
                                    op("dve", lambda py=py, t=t, oh=oh: nc.vector.scalar_tensor_tensor(
                                        out=xc_[:, t, oh * 512:(oh + 1) * 512], in0=py[:, :], scalar=0.5, in1=xc_[:, t, oh * 512:(oh + 1) * 512],
                                        op0=ALU.mult, op1=ALU.add), reads=[B_py, B_xc_], writes=[B_xc_], partial=True)
                        for t_ in range(4):
                            dma(xchunk(c)[:, t_, :], xc_[:, t_, :], reads=[B_xc_], writes=[B_x[c]], partial=(t_ > 0))
                    fw.sync_all()


            def phase_hT():
                with ExitStack() as ph:
                    xcs = [sbt(ph, "hxc%d" % i, [128, 4, D], F32) for i in range(2)]
                    B_xcs = [fw.buf("hxc0"), fw.buf("hxc1")]
                    xn = sbt(ph, "hxn", [128, 4, D], BF16)
                    stat = sbt(ph, "hstat", [128, 32], F32)
                    B_xn, B_stat = fw.buf("hxn"), fw.buf("hstat")
                    for c in range(NCH):
                        xc, B_xc = xcs[c % 2], B_xcs[c % 2]
                        dma(xc[:, :, :], xchunk(c), reads=[B_x[c]], writes=[B_xc])
                        norm_transpose(xc, B_xc, 4, xn, B_xn, stat, B_stat, G_MIX,
                                       lambda fc, c=c: hTf[:, fc, c * 512:(c + 1) * 512], B_hTf, [0, 1], 512)
                    if cfg.debug:
                        for fc in range(8):
                            dma(hdbg[fc, :, :], hTf[:, fc, :], reads=[B_hTf])
                    fw.sync_all()

            def alloc_qkv(ph, need_v=True):
                o = {}
                o["wq"] = [sbt(ph, "wqkv%d" % i, [128, 8, 384], BF16) for i in range(2)]
                o["bt"] = [sbt(ph, "bt%d" % i, [128, 384], F32) for i in range(2)]
                o["B_wq"] = [fw.buf("wq0"), fw.buf("wq1")]
                o["B_bt"] = [fw.buf("bt0"), fw.buf("bt1")]
                o["QT"] = sbt(ph, "QT", [128, S], BF16)
                o["KT"] = sbt(ph, "KT", [128, S], BF16)
                o["V"] = sbt(ph, "V", [128, NT, 128], BF16) if need_v else None
                for k in ("QT", "KT", "V"):
                    o["B_" + k] = fw.buf(k)
                NR = 3
                o["NR"] = NR
                o["qkv"] = [sbt(ph, "qkv%d" % i, [128, 384], F32) for i in range(NR)]
                o["sq"] = [sbt(ph, "sqt%d" % i, [128, 256], F32) for i in range(2)]
                o["qn"] = [sbt(ph, "qn%d" % i, [128, 256], BF16) for i in range(NR)]
                o["st"] = [sbt(ph, "qst%d" % i, [128, 16], F32) for i in range(NR)]
                o["B_qkv"] = [fw.buf("qkv%d" % i) for i in range(NR)]
                o["B_sq"] = [fw.buf("sq0"), fw.buf("sq1")]
                o["B_qn"] = [fw.buf("qn%d" % i) for i in range(NR)]
                o["B_st"] = [fw.buf("qst%d" % i) for i in range(NR)]
                return o

            def load_qkv_w(o, slot, cq, ck, cv):
                wq, bt = o["wq"][slot], o["bt"][slot]
                for i, c0 in enumerate((cq, ck, cv)):
                    dma(wq[:, :, i * 128:(i + 1) * 128], wl("w_in", slice(None), slice(c0, c0 + 128)).rearrange("(k p) c -> p k c", p=128),
                        writes=[o["B_wq"][slot]], partial=True)
                    dma(bt[:, i * 128:(i + 1) * 128],
                        wl("b_in", slice(c0, c0 + 128)).rearrange("(o n) -> o n", o=1).partition_broadcast(128),
                        writes=[o["B_bt"][slot]], partial=True)

            def qkv_project(o, slot, gqc, gkc, vdst=None):
                wq, bt, QT, KT, V = o["wq"][slot], o["bt"][slot], o["QT"], o["KT"], o["V"]
                B_wq, B_bt = o["B_wq"][slot], o["B_bt"][slot]
                NR = o["NR"]
                ptr = Ring([2, 3])
                ptbs = {}

                def stM(t):
                    pp, B_pp = banks[t % 2]

                    def mm():
                        ins = None
                        for k in range(8):
                            ins = nc.tensor.matmul(pp[:, 0:384], lhsT=hTf[:, k, t * 128:(t + 1) * 128], rhs=wq[:, k, :], start=(k == 0), stop=(k == 7))
                        return ins
                    op("pe", mm, reads=[B_hTf, B_wq], writes=[B_pp])

                def stA(t):
                    pp, B_pp = banks[t % 2]
                    qkv, B_qkv = o["qkv"][t % NR], o["B_qkv"][t % NR]
                    stt_, B_st = o["st"][t % NR], o["B_st"][t % NR]
                    sq, B_sq = o["sq"][t % 2], o["B_sq"][t % 2]
                    op("dve", lambda: nc.vector.tensor_tensor(out=qkv[:, :], in0=pp[:, 0:384], in1=bt[:, :], op=ALU.add), reads=[B_pp, B_bt], writes=[B_qkv])
                    for g_ in range(4):
                        op("act", lambda g_=g_: nc.scalar.activation(out=sq[:, g_ * 64:(g_ + 1) * 64], in_=qkv[:, g_ * 64:(g_ + 1) * 64], func=AF.Square,
                                                                     accum_out=stt_[:, g_:g_ + 1]),
                           reads=[B_qkv], writes=[B_sq, B_st], partial=True)
                    op("dve", lambda: nc.vector.tensor_scalar(out=stt_[:, 4:8], in0=stt_[:, 0:4], scalar1=1.0 / HD, scalar2=EPS, op0=ALU.mult, op1=ALU.add),
                       reads=[B_st], writes=[B_st], partial=True)
                    op("act", lambda: nc.scalar.activation(out=stt_[:, 8:12], in_=stt_[:, 4:8], func=AF.Sqrt), reads=[B_st], writes=[B_st], partial=True)
                    if vdst is None:
                        op("act", lambda: nc.scalar.copy(out=V[:, t, :], in_=qkv[:, 256:384]), reads=[B_qkv], writes=[o["B_V"]], partial=True)
                    else:
                        vt_, B_vt_ = vdst
                        op("act", lambda: nc.scalar.copy(out=vt_[:, t, :, 0:64], in_=qkv[:, 256:384].rearrange("p (e d) -> p e d", e=2)),
                           reads=[B_qkv], writes=[B_vt_], partial=True)

                def stB(t):
                    qkv, B_qkv = o["qkv"][t % NR], o["B_qkv"][t % NR]
                    qn, B_qn = o["qn"][t % NR], o["B_qn"][t % NR]
                    stt_, B_st = o["st"][t % NR], o["B_st"][t % NR]
                    op("dve", lambda: nc.vector.reciprocal(out=stt_[:, 12:16], in_=stt_[:, 8:12]), reads=[B_st], writes=[B_st], partial=True)
                    op("dve", lambda: nc.vector.tensor_tensor(
                        out=qn[:, :].rearrange("p (g d) -> p g d", g=4), in0=qkv[:, 0:256].rearrange("p (g d) -> p g d", g=4),
                        in1=stt_[:, 12:16].rearrange("p (g o) -> p g o", o=1).to_broadcast([128, 4, HD]), op=ALU.mult),
                       reads=[B_qkv, B_st], writes=[B_qn])

                def stT(t):
                    qn, B_qn = o["qn"][t % NR], o["B_qn"][t % NR]
                    if t % 4 == 0:
                        ti = ptr.next()
                        ptbs[t // 4] = (bank_bf(ti), banks[ti][1])
                    ptb, B_ptb = ptbs[t // 4]

                    def tr():
                        nc.tensor.transpose(out=ptb[:, (t % 4) * 128:(t % 4 + 1) * 128], in_=qn[:, 0:128], identity=identb[:, :])
                        return nc.tensor.transpose(out=ptb[:, 512 + (t % 4) * 128:512 + (t % 4 + 1) * 128], in_=qn[:, 128:256], identity=identb[:, :])
                    op("pe", tr, reads=[B_qn, B_identb], writes=[B_ptb], partial=True)
                    if t % 4 == 3:
                        t0 = t - 3
                        op("act", lambda: nc.scalar.activation(out=QT[:, t0 * 128:(t0 + 4) * 128], in_=ptb[:, 0:512], func=AF.Copy, scale=small[:, gqc:gqc + 1]),
                           reads=[B_ptb, B_small], writes=[o["B_QT"]], partial=True)
                        op("dve", lambda: nc.vector.tensor_scalar(out=KT[:, t0 * 128:(t0 + 4) * 128], in0=ptb[:, 512:1024],
                                                                  scalar1=small[:, gkc:gkc + 1], scalar2=None, op0=ALU.mult),
                           reads=[B_ptb, B_small], writes=[o["B_KT"]], partial=True)
                for s_ in range(NT + 3):
                    if s_ < NT:
                        stM(s_)
                    if 0 <= s_ - 1 < NT:
                        stA(s_ - 1)
                    if 0 <= s_ - 2 < NT:
                        stB(s_ - 2)
                    if 0 <= s_ - 3 < NT:
                        stT(s_ - 3)

            def phase_A():
                with ExitStack() as ph:
                    o = alloc_qkv(ph)
                    QT, KT, V = o["QT"], o["KT"], o["V"]
                    ib = sbt(ph, "ibias", [128, 4, 512], I32)
                    Bneg = sbt(ph, "Bneg", [128, 512], F32)
                    Bdiag = sbt(ph, "Bdiag", [128, 4, 512], F32)
                    B_ib, B_Bneg, B_Bdiag = fw.buf("ib"), fw.buf("Bneg"), fw.buf("Bdiag")
                    op("pool", lambda: nc.gpsimd.iota(ib[:, 0, :], pattern=[[-1, 512]], base=0, channel_multiplier=1), writes=[B_ib])
                    op("dve", lambda: nc.vector.tensor_copy(out=Bneg[:, :], in_=ib[:, 0, :]), reads=[B_ib], writes=[B_Bneg])
                    op("pool", lambda: nc.gpsimd.iota(ib[:, :, :], pattern=[[-128, 4], [1, 512]], base=0, channel_multiplier=-1), reads=[], writes=[B_ib])
                    op("dve", lambda: nc.vector.tensor_copy(out=Bdiag[:, :, :], in_=ib[:, :, :]), reads=[B_ib], writes=[B_Bdiag])
                    for tt_ in range(4):
                        op("dve", lambda tt_=tt_: nc.vector.scalar_tensor_tensor(out=Bdiag[:, tt_, :], in0=Bdiag[:, tt_, :], scalar=-1.0, in1=Bdiag[:, tt_, :], op0=ALU.mult, op1=ALU.min),
                           reads=[B_Bdiag], writes=[B_Bdiag], partial=True)
                    QA = KA = None
                    B_QA, B_KA = fw.buf("QAaug"), fw.buf("KAaug")
                    tts = [sbt(ph, "att%d" % i, [128, 2, 512], F32) for i in range(2)]
                    B_tts = [fw.buf("att0"), fw.buf("att1")]
                    NPT = 4
                    PTs = [sbt(ph, "aPT%d" % i, [128, 2, 512], BF16) for i in range(NPT)]
                    B_PTs = [fw.buf("aPT%d" % i) for i in range(NPT)]
                    f = [sbt(ph, "af%d" % i, [128, 512], F32) for i in range(4)]
                    B_f = [fw.buf("af%d" % i) for i in range(4)]
                    sqb = sbt(ph, "asqb", [128, 512], BF16)
                    B_sqb = fw.buf("asqb")
                    yaTs = [sbt(ph, "ayaT%d" % i, [128, 512], BF16) for i in range(2)]
                    B_yaTs = [fw.buf("ayaT0"), fw.buf("ayaT1")]
                    ycount = 0
                    if cfg.a_heads:
                        h0_ = cfg.a_heads[0]
                        load_qkv_w(o, 0, OFF_AQ + h0_ * 128, OFF_AK + h0_ * 128, OFF_AV + h0_ * 128)
                    for hi_, h in enumerate(cfg.a_heads):
                        if hi_ + 1 < len(cfg.a_heads):
                            hn_ = cfg.a_heads[hi_ + 1]
                            load_qkv_w(o, (hi_ + 1) % 2, OFF_AQ + hn_ * 128, OFF_AK + hn_ * 128, OFF_AV + hn_ * 128)
                        qkv_project(o, hi_ % 2, 0, 1)
                        m = SLOPES_A[h]
                        SKIP_T = 120.0
                        kept = {}
                        for qc in range(NCH):
                            kl = []
                            for kt in range(NT):
                                q0_, k0_ = qc * 512, kt * 128
                                dmin = max(q0_ - (k0_ + 127), k0_ - (q0_ + 511), 0)
                                if m * dmin <= SKIP_T:
                                    kl.append(kt)
                            kept[qc] = kl
                        steps = [(qc, kt) for qc in range(NCH) for kt in kept[qc]]
                        spair = Ring([0, 1])
                        pend = {}

                        USE_AUG = False

                        def side_of(qc, kt):
                            q0, k0 = qc * 512, kt * 128
                            if k0 + 128 <= q0:
                                return 0
                            if k0 >= q0 + 512:
                                return 1
                            return -1

                        def emit_S(idx):
                            qc, kt = steps[idx]
                            di = spair.next()
                            pend[idx] = di
                            sd = side_of(qc, kt) if USE_AUG else -1

                            def mmS(di=di, qc=qc, kt=kt):
                                ins = None
                                for g in range(2):
                                    ins = nc.tensor.matmul(dbanks[di][:, g * 512:(g + 1) * 512], lhsT=KT[g * 64:(g + 1) * 64, kt * 128:(kt + 1) * 128],
                                                           rhs=QT[g * 64:(g + 1) * 64, qc * 512:(qc + 1) * 512], start=True, stop=(sd < 0))
                                    if sd >= 0:
                                        ins = nc.tensor.matmul(dbanks[di][:, g * 512:(g + 1) * 512], lhsT=KA[:, h, sd, :], rhs=QA[:, h, :], start=False, stop=True)
                                return ins
                            op("pe", mmS, reads=[o["B_QT"], o["B_KT"], B_QA, B_KA], writes=[banks[2 * di][1], banks[2 * di + 1][1]])
                        emit_S(0)
                        backs = []
                        LAG = 1
                        ycount_box = [ycount]
                        for idx, (qc, kt) in enumerate(steps):
                            if idx + 1 < len(steps):
                                emit_S(idx + 1)
                            di = pend.pop(idx)
                            q0, k0 = qc * 512, kt * 128
                            tt, B_tt = tts[idx % 2], B_tts[idx % 2]
                            PT, B_PT = PTs[idx % NPT], B_PTs[idx % NPT]
                            sd = side_of(qc, kt)
                            if sd < 0:
                                tab, B_tab = Bdiag[:, (k0 - q0) // 128, :], B_Bdiag
                                op("dve", lambda tab=tab, di=di, tt=tt: nc.vector.scalar_tensor_tensor(
                                    out=tt[:, :, :], in0=tab.rearrange("p (o n) -> p o n", o=1).to_broadcast([128, 2, 512]), scalar=float(m),
                                    in1=dbanks[di][:, :].rearrange("p (g n) -> p g n", g=2), op0=ALU.mult, op1=ALU.add),
                                   reads=[B_tab, banks[2 * di][1], banks[2 * di + 1][1]], writes=[B_tt])
                                op("act", lambda tt=tt, PT=PT: nc.scalar.activation(out=PT[:, :, :], in_=tt[:, :, :], func=AF.Exp),
                                   reads=[B_tt], writes=[B_PT])
                            elif not USE_AUG:
                                cst = -m * (q0 - k0) if sd == 0 else -m * (k0 - q0)
                                sc = m if sd == 0 else -m
                                op("dve", lambda di=di, tt=tt, sc=sc: nc.vector.scalar_tensor_tensor(
                                    out=tt[:, :, :], in0=Bneg[:, :].rearrange("p (o n) -> p o n", o=1).to_broadcast([128, 2, 512]), scalar=float(sc),
                                    in1=dbanks[di][:, :].rearrange("p (g n) -> p g n", g=2), op0=ALU.mult, op1=ALU.add),
                                   reads=[B_Bneg, banks[2 * di][1], banks[2 * di + 1][1]], writes=[B_tt])
                                op("act", lambda tt=tt, PT=PT, cst=cst: nc.scalar.activation(out=PT[:, :, :], in_=tt[:, :, :], func=AF.Exp, bias=float(cst)),
                                   reads=[B_tt], writes=[B_PT])
                            else:
                                cst = -m * (q0 - k0) if sd == 0 else -m * (k0 - q0)
                                op("act", lambda di=di, PT=PT, cst=cst: nc.scalar.activation(
                                    out=PT[:, :, :], in_=dbanks[di][:, :].rearrange("p (g n) -> p g n", g=2), func=AF.Exp, bias=float(cst)),
                                   reads=[banks[2 * di][1], banks[2 * di + 1][1]], writes=[B_PT])

                            def back(idx=idx, qc=qc, kt=kt, q0=q0, PT=PT, B_PT=B_PT):
                                def mmPV(kt=kt, PT=PT):
                                    ins = None
                                    for g in range(2):
                                        nc.tensor.matmul(banks[4 + 2 * g][0][:, :], lhsT=V[:, kt, :], rhs=PT[:, g, :], start=(kt == kept[qc][0]), stop=(kt == kept[qc][-1]))
                                        ins = nc.tensor.matmul(banks[5 + 2 * g][0][:, :], lhsT=onesb[:, :], rhs=PT[:, g, :], start=(kt == kept[qc][0]), stop=(kt == kept[qc][-1]))
                                    return ins
                                op("pe", mmPV, reads=[o["B_V"], B_PT, B_onesb], writes=[banks[4][1], banks[5][1], banks[6][1], banks[7][1]])
                                if kt == kept[qc][-1]:
                                    O0, D0, O1, D1 = banks[4][0], banks[5][0], banks[6][0], banks[7][0]
                                    B_O0, B_D0, B_O1, B_D1 = banks[4][1], banks[5][1], banks[6][1], banks[7][1]
                                    op("act", lambda: nc.scalar.activation(out=f[0][:, :], in_=D0[:, :], func=AF.Ln), reads=[B_D0], writes=[B_f[0]])
                                    op("act", lambda: nc.scalar.activation(out=f[1][:, :], in_=D1[:, :], func=AF.Ln), reads=[B_D1], writes=[B_f[1]])
                                    op("act", lambda: nc.scalar.activation(out=f[0][:, :], in_=f[0][:, :], func=AF.Exp, scale=-1.0), reads=[B_f[0]], writes=[B_f[0]])
                                    op("act", lambda: nc.scalar.activation(out=f[1][:, :], in_=f[1][:, :], func=AF.Exp, scale=-1.0), reads=[B_f[1]], writes=[B_f[1]])
                                    op("dve", lambda: nc.vector.tensor_tensor(out=f[2][:, :], in0=O0[:, :], in1=f[0][:, :], op=ALU.mult), reads=[B_O0, B_f[0]], writes=[B_f[2]])
                                    op("dve", lambda: nc.vector.tensor_tensor(out=f[3][:, :], in0=O1[:, :], in1=f[1][:, :], op=ALU.mult), reads=[B_O1, B_f[1]], writes=[B_f[3]])
                                    op("dve", lambda: nc.vector.scalar_tensor_tensor(out=f[2][:, :], in0=f[3][:, :], scalar=small[:, 6:7], in1=f[2][:, :],
                                                                                     op0=ALU.mult, op1=ALU.add), reads=[B_f[3], B_f[2], B_small], writes=[B_f[2]])
                                    op("act", lambda: nc.scalar.activation(out=sqb[:, :], in_=f[2][:, :], func=AF.Square), reads=[B_f[2]], writes=[B_sqb])
                                    op("pe", lambda: nc.tensor.matmul(D0[:, :], lhsT=onesb[:, :], rhs=sqb[:, :], start=True, stop=True),
                                       reads=[B_sqb, B_onesb], writes=[B_D0])
                                    op("act", lambda: nc.scalar.activation(out=f[0][:, :], in_=D0[:, :], func=AF.Ln, scale=1.0 / 128, bias=EPS),
                                       reads=[B_D0], writes=[B_f[0]])
                                    op("act", lambda: nc.scalar.activation(out=f[0][:, :], in_=f[0][:, :], func=AF.Exp, scale=-0.5), reads=[B_f[0]], writes=[B_f[0]])
                                    yaT, B_yaT = yaTs[ycount_box[0] % 2], B_yaTs[ycount_box[0] % 2]
                                    ycount_box[0] += 1
                                    op("dve", lambda yaT=yaT: nc.vector.scalar_tensor_tensor(out=yaT[:, :], in0=f[2][:, :], scalar=small[:, 4:5], in1=f[0][:, :],
                                                                                             op0=ALU.mult, op1=ALU.mult), reads=[B_f[2], B_f[0], B_small], writes=[B_yaT])
                                    dma(ysc[h, :, q0:q0 + 512], yaT[:, :], reads=[B_yaT], writes=[B_ysc], partial=True)
                            backs.append(back)
                            if len(backs) > LAG:
                                backs.pop(0)()
                        while backs:
                            backs.pop(0)()
                        ycount = ycount_box[0]
                    fw.sync_all()

            def phase_B():
                with ExitStack() as ph:
                    o = alloc_qkv(ph, need_v=False)
                    QT, KT, V = o["QT"], o["KT"], o["V"]
                    Vc = {d_: sbt(ph, "Vb%d" % d_, [128, NT, 2, 128], BF16) for d_ in (1, 4, 16)}
                    B_Vc = {d_: fw.buf("Vb%d" % d_) for d_ in (1, 4, 16)}
                    for d_ in (1, 4, 16):
                        op("pool", lambda d_=d_: nc.gpsimd.memset(Vc[d_][:, :, :, 64:128], 1.0), writes=[B_Vc[d_]], partial=True)
                    Bband = sbt(ph, "Bband", [128, 384], F32)
                    ibb = sbt(ph, "ibband", [128, 384], I32)
                    btmp = ibb[:, :].bitcast(F32)
                    B_ibb, B_Bband, B_btmp = fw.buf("ibb"), fw.buf("Bband"), fw.buf("btmp")
                    op("pool", lambda: nc.gpsimd.iota(ibb[:, :], pattern=[[1, 384]], base=-128, channel_multiplier=-1), writes=[B_ibb])
                    op("dve", lambda: nc.vector.tensor_copy(out=Bband[:, :], in_=ibb[:, :]), reads=[B_ibb], writes=[B_Bband])
                    op("dve", lambda: nc.vector.scalar_tensor_tensor(out=Bband[:, :], in0=Bband[:, :], scalar=-1.0, in1=Bband[:, :], op0=ALU.mult, op1=ALU.max),
                       reads=[B_Bband], writes=[B_Bband])
                    op("dve", lambda: nc.vector.tensor_scalar(out=btmp[:, :], in0=Bband[:, :], scalar1=64.0, scalar2=-NEG_BIG, op0=ALU.is_gt, op1=ALU.mult),
                       reads=[B_Bband], writes=[B_btmp])
                    op("dve", lambda: nc.vector.tensor_tensor(out=Bband[:, :], in0=Bband[:, :], in1=btmp[:, :], op=ALU.add), reads=[B_Bband, B_btmp], writes=[B_Bband])
                    op("dve", lambda: nc.vector.tensor_scalar(out=Bband[:, :], in0=Bband[:, :], scalar1=-1.0, scalar2=None, op0=ALU.mult),
                       reads=[B_Bband], writes=[B_Bband])
                    tts = [sbt(ph, "btt%d" % i, [128, 384], F32) for i in range(3)]
                    B_tts = [fw.buf("btt%d" % i) for i in range(3)]
                    PTs = [sbt(ph, "bPT%d" % i, [128, 384], BF16) for i in range(8)]
                    B_PTs = [fw.buf("bPT%d" % i) for i in range(8)]
                    acc = sbt(ph, "accB", [128, S], F32)
                    rden = sbt(ph, "rdenB", [64, 2048], F32)
                    ybT = sbt(ph, "ybT", [64, S], BF16)
                    B_acc, B_rden, B_ybT = fw.buf("accB"), fw.buf("rdenB"), fw.buf("ybT")
                    if cfg.b_pairs:
                        p0_ = cfg.b_pairs[0]
                        load_qkv_w(o, 0, OFF_BQ + p0_ * 128, OFF_BK + p0_ * 128, OFF_BV + p0_ * 128)
                    for pi_, hp in enumerate(cfg.b_pairs):
                        if pi_ + 1 < len(cfg.b_pairs):
                            pn_ = cfg.b_pairs[pi_ + 1]
                            load_qkv_w(o, (pi_ + 1) % 2, OFF_BQ + pn_ * 128, OFF_BK + pn_ * 128, OFF_BV + pn_ * 128)
                        qkv_project(o, pi_ % 2, 2, 3, vdst=(Vc[1], B_Vc[1]))
                        wq, bt = o["wq"][pi_ % 2], o["bt"][pi_ % 2]
                        B_wq_, B_bt_ = o["B_wq"][pi_ % 2], o["B_bt"][pi_ % 2]
                        vbr = Ring([0, 1])
                        for dil in (4, 16):
                            U = S // dil
                            nut = U // 128
                            tiles = [(r, ut) for r in range(dil) for ut in range(nut)]
                            for g0 in range(0, len(tiles), 4):
                                grp = tiles[g0:g0 + 4]
                                bi = vbr.next()
                                pv, B_pv = banks[bi]

                                def mmv(grp=grp, pv=pv, dil=dil):
                                    ins = None
                                    for j, (r, ut) in enumerate(grp):
                                        c0 = r + dil * ut * 128
                                        for k in range(8):
                                            ins = nc.tensor.matmul(pv[:, j * 128:(j + 1) * 128], lhsT=hTf[:, k, c0:c0 + dil * 127 + 1:dil],
                                                                   rhs=wq[:, k, 256:384], start=(k == 0), stop=(k == 7))
                                    return ins
                                op("pe", mmv, reads=[B_hTf, B_wq_], writes=[B_pv])
                                ng = len(grp)
                                op("dve", lambda pv=pv, g0=g0, ng=ng, dil=dil: nc.vector.tensor_tensor(
                                    out=Vc[dil][:, g0:g0 + ng, :, 0:64], in0=pv[:, 0:ng * 128].rearrange("p (j e c) -> p j e c", j=ng, e=2),
                                    in1=bt[:, 256:384].rearrange("p (o e c) -> p o e c", o=1, e=2).to_broadcast([128, ng, 2, 64]), op=ALU.add),
                                   reads=[B_pv, B_bt_], writes=[B_Vc[dil]], partial=True)
                        for e in range(2):
                            m = SLOPES_B[hp * 2 + e]
                            sring = Ring([0, 1, 4, 5, 6, 7])
                            LOOK = 3
                            steps = []
                            for (_win, dil) in PATTERNS:
                                nkt = (S // dil) // 128
                                for r in range(dil):
                                    for kt in range(nkt):
                                        steps.append((dil, r, kt, nkt))

                            def ccols(dil, r, u0, n):
                                c0 = r + dil * u0
                                return slice(c0, c0 + dil * (n - 1) + 1, dil)
                            sbank = {}
                            nxt = [0]

                            def emit_S(i):
                                dil, r, kt, nkt = steps[i]
                                qa, qb = max(kt - 1, 0), min(kt + 1, nkt - 1)
                                nq = (qb - qa + 1) * 128
                                si = sring.next()
                                sbank[i] = si
                                pS, B_pS = banks[si]
                                op("pe", lambda: nc.tensor.matmul(
                                    pS[:, 0:nq], lhsT=KT[e * 64:(e + 1) * 64, ccols(dil, r, kt * 128, 128)], rhs=QT[e * 64:(e + 1) * 64, ccols(dil, r, qa * 128, nq)],
                                    start=True, stop=True), reads=[o["B_QT"], o["B_KT"]], writes=[B_pS])
                            hist = {}
                            segctr = {}
                            nsegs = [0]
                            bq = []
                            LAGB = 2
                            for i, (dil, r, kt, nkt) in enumerate(steps):
                                while nxt[0] <= min(i + LOOK, len(steps) - 1):
                                    emit_S(nxt[0])
                                    nxt[0] += 1
                                qa, qb = max(kt - 1, 0), min(kt + 1, nkt - 1)
                                nq = (qb - qa + 1) * 128
                                pS, B_pS = banks[sbank.pop(i)]
                                tt, B_tt = tts[i % 3], B_tts[i % 3]
                                PT, B_PT = PTs[i % 8], B_PTs[i % 8]
                                boff = (qa - (kt - 1)) * 128
                                op("dve", lambda: nc.vector.scalar_tensor_tensor(
                                    out=tt[:, 0:nq], in0=Bband[:, boff:boff + nq], scalar=float(m * dil), in1=pS[:, 0:nq], op0=ALU.mult, op1=ALU.add),
                                   reads=[B_Bband, B_pS], writes=[B_tt])
                                op("act", lambda: nc.scalar.activation(out=PT[:, 0:nq], in_=tt[:, 0:nq], func=AF.Exp), reads=[B_tt], writes=[B_PT])
                                hist[(dil, r, kt)] = (PT, B_PT, qa)
                                qts = []
                                if kt >= 1:
                                    qts.append(kt - 1)
                                if kt == nkt - 1:
                                    qts.append(kt)
                                for qt in qts:
                                    bq.append((dil, r, nkt, qt))
                                while len(bq) > 0 and (len(bq) > LAGB * 2 or i == len(steps) - 1):
                                    dil, r, nkt, qt = bq.pop(0)
                                    seg = qt // 4
                                    if (dil, r, seg) not in segctr:
                                        segctr[(dil, r, seg)] = nsegs[0]
                                        nsegs[0] += 1
                                    par = segctr[(dil, r, seg)] % 2
                                    bN, B_bN = banks[2 + par]
                                    col = (qt % 4) * 128
                                    kts = list(range(max(qt - 1, 0), min(qt + 1, nkt - 1) + 1))

                                    def mmPV():
                                        ins = None
                                        for i_, k_ in enumerate(kts):
                                            PTk, _b, qak = hist[(dil, r, k_)]
                                            rhs = PTk[:, (qt - qak) * 128:(qt - qak + 1) * 128]
                                            ins = nc.tensor.matmul(bN[:, col:col + 128], lhsT=Vc[dil][:, r * nkt + k_, e, :], rhs=rhs,
                                                                   start=(i_ == 0), stop=(i_ == len(kts) - 1))
                                        return ins
                                    op("pe", mmPV, reads=[B_Vc[dil]] + [hist[(dil, r, k_)][1] for k_ in kts], writes=[B_bN], partial=True)
                                    if qt % 4 == 3 or qt == nkt - 1:
                                        ncols = (qt - seg * 4 + 1) * 128
                                        dst = ccols(dil, r, seg * 512, ncols)
                                        if dil == 1:
                                            op("act", lambda: nc.scalar.copy(out=acc[:, dst], in_=bN[:, 0:ncols]), reads=[B_bN], writes=[B_acc], partial=True)
                                        else:
                                            op("dve", lambda: nc.vector.tensor_tensor(out=acc[:, dst], in0=bN[:, 0:ncols], in1=acc[:, dst], op=ALU.add),
                                               reads=[B_bN, B_acc], writes=[B_acc], partial=True)
                            for c0 in range(0, S, 2048):
                                op("act", lambda c0=c0: nc.scalar.activation(out=acc[64:128, c0:c0 + 2048], in_=acc[64:128, c0:c0 + 2048], func=AF.Ln),
                                   reads=[B_acc], writes=[B_acc], partial=True)
                                op("act", lambda c0=c0: nc.scalar.activation(out=acc[64:128, c0:c0 + 2048], in_=acc[64:128, c0:c0 + 2048], func=AF.Exp, scale=-1.0),
                                   reads=[B_acc], writes=[B_acc], partial=True)
                            for c0 in range(0, S, 2048):
                                dma(rden[:, :], acc[64:128, c0:c0 + 2048], reads=[B_acc], writes=[B_rden])
                                op("dve", lambda c0=c0: nc.vector.tensor_tensor(out=ybT[:, c0:c0 + 2048], in0=acc[0:64, c0:c0 + 2048], in1=rden[:, :], op=ALU.mult),
                                   reads=[B_acc, B_rden], writes=[B_ybT], partial=True)
                            dma(ysc[8 + hp, e * 64:(e + 1) * 64, :], ybT[:, :], reads=[B_ybT], writes=[B_ysc], partial=True)
                    fw.sync_all()

            def phase_C():
                with ExitStack() as ph:
                    zT = sbt(ph, "zT", [128, 6, S + 30], BF16)
                    B_zT = fw.buf("zT")
                    op("pool", lambda: nc.gpsimd.memset(zT[:, :, 0:15], 0.0), writes=[B_zT], partial=True)
                    op("pool", lambda: nc.gpsimd.memset(zT[:, :, S + 15:S + 30], 0.0), writes=[B_zT], partial=True)
                    wcv = sbt(ph, "wcv", [128, 192], F32)
                    diagW = sbt(ph, "diagW", [128, 186, 128], BF16)
                    B_wcv, B_diagW = fw.buf("wcv"), fw.buf("diagW")
                    with ExitStack() as ph1:
                        stgc = sbt(ph1, "stgc", [96, 2, 128], F32)
                        B_stgc = fw.buf("stgc")
                        src = wl("c_dw_w").rearrange("j (ct p) -> (j ct) p", p=128)
                        dma(stgc[0:96, 0, :], src[0:96, :], writes=[B_stgc], partial=True)
                        dma(stgc[0:90, 1, :], src[96:186, :], writes=[B_stgc], partial=True)
                        pt, B_pt = banks[0]
                        op("pe", lambda: nc.tensor.transpose(out=pt[:, 0:96], in_=stgc[0:96, 0, :], identity=identf[0:96, 0:96]),
                           reads=[B_stgc, B_identf], writes=[B_pt], partial=True)
                        op("pe", lambda: nc.tensor.transpose(out=pt[:, 96:186], in_=stgc[0:90, 1, :], identity=identf[0:90, 0:90]),
                           reads=[B_stgc, B_identf], writes=[B_pt], partial=True)
                        op("act", lambda: nc.scalar.copy(out=wcv[:, 0:186], in_=pt[:, 0:186]), reads=[B_pt], writes=[B_wcv])
                        for idx in range(186):
                            eng = "dve" if idx % 2 == 0 else "pool"
                            e_ = nc.vector if eng == "dve" else nc.gpsimd
                            op(eng, lambda idx=idx, e_=e_: e_.tensor_scalar(out=diagW[:, idx, :], in0=identb[:, :], scalar1=wcv[:, idx:idx + 1], scalar2=None, op0=ALU.mult),
                               reads=[B_identb, B_wcv], writes=[B_diagW], partial=True)
                        wc = sbt(ph1, "wc", [128, 8, 1536], BF16)
                        B_wc = fw.buf("wc")
                        stgs = None
                        load_weight(wc, B_wc, lambda k, c0, c1: wl("w_in", slice(k * 128, (k + 1) * 128), slice(OFF_CU + c0, OFF_CU + c1)), 8, 1536, stgs, 1536)
                        sgs = [sbt(ph1, "csg%d" % i, [128, 512], F32) for i in range(2)]
                        B_sgs = [fw.buf("csg0"), fw.buf("csg1")]
                        ar, gr = Ring([0, 1]), Ring([2, 3])
                        n = 0
                        for c in range(NCH):
                            for ct in range(6):
                                pa, B_pa = banks[ar.next()]
                                pg, B_pg = banks[gr.next()]

                                def mm(pa=pa, pg=pg, ct=ct, c=c):
                                    ins = None
                                    for k in range(8):
                                        nc.tensor.matmul(pa[:, :], lhsT=wc[:, k, ct * 128:(ct + 1) * 128], rhs=hTf[:, k, c * 512:(c + 1) * 512], start=(k == 0), stop=(k == 7))
                                    for k in range(8):
                                        ins = nc.tensor.matmul(pg[:, :], lhsT=wc[:, k, 768 + ct * 128:768 + (ct + 1) * 128], rhs=hTf[:, k, c * 512:(c + 1) * 512],
                                                               start=(k == 0), stop=(k == 7))
                                    return ins
                                op("pe", mm, reads=[B_wc, B_hTf], writes=[B_pa, B_pg])
                                sg, B_sg = sgs[n % 2], B_sgs[n % 2]
                                n += 1
                                op("act", lambda pg=pg, sg=sg, ct=ct: nc.scalar.activation(out=sg[:, :], in_=pg[:, :], func=AF.Sigmoid,
                                                                                         bias=vecT[:, V_BC + 6 + ct:V_BC + 7 + ct]),
                                   reads=[B_pg, B_vecT], writes=[B_sg])
                                op("dve", lambda pa=pa, sg=sg, ct=ct, c=c: nc.vector.scalar_tensor_tensor(
                                    out=zT[:, ct, 15 + c * 512:15 + (c + 1) * 512], in0=pa[:, :], scalar=vecT[:, V_BC + ct:V_BC + ct + 1], in1=sg[:, :],
                                    op0=ALU.add, op1=ALU.mult), reads=[B_pa, B_sg, B_vecT], writes=[B_zT], partial=True)
                        fw.sync_all()
                    sqs = [sbt(ph, "csq%d" % i, [128, 512], BF16) for i in range(2)]
                    B_sqs = [fw.buf("csq0"), fw.buf("csq1")]
                    rstd = sbt(ph, "crstd", [128, 512], F32)
                    B_rstd = fw.buf("crstd")
                    tmps = [sbt(ph, "ctmp%d" % i, [128, 512], F32) for i in range(2)]
                    B_tmps = [fw.buf("ctmp0"), fw.buf("ctmp1")]
                    ycs = [sbt(ph, "cyc%d" % i, [128, 512], BF16) for i in range(3)]
                    B_ycs = [fw.buf("cyc%d" % i) for i in range(3)]
                    n = 0
                    for c in range(NCH):
                        for ct in range(6):
                            pc, B_pc = banks[ct]

                            def mmc(pc=pc, ct=ct, c=c):
                                ins = None
                                for j in range(CONV_W):
                                    ins = nc.tensor.matmul(pc[:, :], lhsT=diagW[:, j * 6 + ct, :], rhs=zT[:, ct, c * 512 + j:c * 512 + j + 512],
                                                           start=(j == 0), stop=(j == CONV_W - 1))
                                return ins
                            op("pe", mmc, reads=[B_diagW, B_zT], writes=[B_pc])
                            sq, B_sq = sqs[ct % 2], B_sqs[ct % 2]
                            op("act", lambda pc=pc, sq=sq, ct=ct: nc.scalar.activation(out=sq[:, :], in_=pc[:, :], func=AF.Square,
                                                                                     bias=vecT[:, V_DWB + ct:V_DWB + ct + 1]),
                               reads=[B_pc, B_vecT], writes=[B_sq])
                            pr, B_pr = banks[6]
                            op("pe", lambda pr=pr, sq=sq, ct=ct: nc.tensor.matmul(pr[:, :], lhsT=onesb[:, :], rhs=sq[:, :], start=(ct == 0), stop=(ct == 5)),
                               reads=[B_sq, B_onesb], writes=[B_pr], partial=(ct != 0))
                        pr, B_pr = banks[6]
                        op("act", lambda pr=pr: nc.scalar.activation(out=rstd[:, :], in_=pr[:, :], func=AF.Ln, scale=1.0 / C_CH, bias=EPS), reads=[B_pr], writes=[B_rstd])
                        op("act", lambda: nc.scalar.activation(out=rstd[:, :], in_=rstd[:, :], func=AF.Exp, scale=-0.5), reads=[B_rstd], writes=[B_rstd])
                        for ct in range(6):
                            pc, B_pc = banks[ct]
                            tmp, B_tmp = tmps[ct % 2], B_tmps[ct % 2]
                            yc, B_yc = ycs[n % 3], B_ycs[n % 3]
                            n += 1
                            op("dve", lambda pc=pc, tmp=tmp, ct=ct: nc.vector.scalar_tensor_tensor(out=tmp[:, :], in0=pc[:, :], scalar=vecT[:, V_DWB + ct:V_DWB + ct + 1],
                                                                                                 in1=rstd[:, :], op0=ALU.add, op1=ALU.mult),
                               reads=[B_pc, B_rstd, B_vecT], writes=[B_tmp])
                            op("act", lambda tmp=tmp, yc=yc, ct=ct: nc.scalar.activation(out=yc[:, :], in_=tmp[:, :], func=AF.Silu, scale=vecT[:, V_CN + ct:V_CN + ct + 1]),
                               reads=[B_tmp, B_vecT], writes=[B_yc])
                            dma(ysc[14 + ct, :, c * 512:(c + 1) * 512], yc[:, :], reads=[B_yc], writes=[B_ysc], partial=True)
                    fw.sync_all()

            def phase_merge():
                CW = 256
                with ExitStack() as ph:
                    wg = sbt(ph, "wg", [128, 8, 3072], BF16)
                    woa = sbt(ph, "woa", [128, 8, D], BF16)
                    wob = sbt(ph, "wob", [128, 6, D], BF16)
                    woc = sbt(ph, "woc", [128, 6, D], BF16)
                    wo = sbt(ph, "wo", [128, 8, D], BF16)
                    B_wg, B_woa, B_wob, B_woc, B_wo = (fw.buf(n_) for n_ in ("wg", "woa", "wob", "woc", "wo"))
                    stgs = None
                    load_weight(woa, B_woa, lambda k, c0, c1: wl("w_out_a", slice(k * 128, (k + 1) * 128), slice(c0, c1)), 8, D, stgs, 1024)
                    load_weight(wob, B_wob, lambda k, c0, c1: wl("w_out_b", slice(k * 128, (k + 1) * 128), slice(c0, c1)), 6, D, stgs, 1024)
                    load_weight(woc, B_woc, lambda k, c0, c1: wl("w_out_c", slice(k * 128, (k + 1) * 128), slice(c0, c1)), 6, D, stgs, 1024)
                    load_weight(wg, B_wg, lambda k, c0, c1: wl("w_in", slice(k * 128, (k + 1) * 128), slice(OFF_G + c0, OFF_G + c1)), 8, 3072, stgs, 1024)
                    load_weight(wo, B_wo, lambda k, c0, c1: wl("w_out", slice(k * 128, (k + 1) * 128), slice(c0, c1)), 8, D, stgs, 1024)
                    yT = sbt(ph, "myT", [128, 20, CW], BF16)
                    xc = sbt(ph, "mxc", [128, CW // 128, D], F32)
                    mT = sbt(ph, "mmT", [128, 8, CW], BF16)
                    B_yT, B_xc, B_mT = fw.buf("myT"), fw.buf("mxc"), fw.buf("mmT")
                    gs = [sbt(ph, "mg%d" % i, [128, 3, CW], F32) for i in range(2)]
                    B_gs = [fw.buf("mg0"), fw.buf("mg1")]
                    m1s = [sbt(ph, "mm1%d" % i, [128, 3, CW], F32) for i in range(1)] * 2
                    B_m1s = [fw.buf("mm10")] * 2
                    zr = Ring([0, 3])
                    xr = Ring([6, 7])
                    nn = 0
                    ntl = CW // 128

                    def slot(base, i):
                        return banks[base + i // 2][0][:, (i % 2) * CW:(i % 2 + 1) * CW]
                    for c in range(S // CW):
                        c0 = c * CW
                        dma(yT[:, :, :], ysc[:, :, c0:c0 + CW].rearrange("c p n -> p c n"), reads=[B_ysc], writes=[B_yT])
                        dma(xc[:, :, :], xchunk(c, CW), reads=[B_x[c0 // 512]], writes=[B_xc])
                        for m in range(8):
                            base = zr.next()
                            B_db = [banks[base][1], banks[base + 1][1], banks[base + 2][1]]

                            def mmz(base=base, m=m, c0=c0):
                                ins = None
                                for br, (wt, nk, ko) in enumerate(((woa, 8, 0), (wob, 6, 8), (woc, 6, 14))):
                                    for k in range(nk):
                                        nc.tensor.matmul(slot(base, br), lhsT=wt[:, k, m * 128:(m + 1) * 128], rhs=yT[:, ko + k, :], start=(k == 0), stop=(k == nk - 1))
                                for br in range(3):
                                    for k in range(8):
                                        ins = nc.tensor.matmul(slot(base, 3 + br), lhsT=wg[:, k, br * D + m * 128:br * D + (m + 1) * 128], rhs=hTf[:, k, c0:c0 + CW],
                                                               start=(k == 0), stop=(k == 7))
                                return ins
                            op("pe", mmz, reads=[B_woa, B_wob, B_woc, B_wg, B_yT, B_hTf], writes=B_db)

                            def zslice(db, br, base=base):
                                return slot(base, br)

                            def gslice(db, br, base=base):
                                return slot(base, 3 + br)
                            db = None
                            g, B_g = gs[nn % 2], B_gs[nn % 2]
                            m1, B_m1 = m1s[nn % 2], B_m1s[nn % 2]
                            nn += 1
                            for br in range(3):
                                op("act", lambda db=db, g=g, br=br, m=m: nc.scalar.activation(out=g[:, br, :], in_=gslice(db, br), func=AF.Sigmoid,
                                                                                             bias=vecT[:, V_BG + br * 8 + m:V_BG + br * 8 + m + 1]),
                                   reads=B_db + [B_vecT], writes=[B_g], partial=True)
                            for br in range(3):
                                op("dve", lambda db=db, g=g, m1=m1, br=br: nc.vector.tensor_tensor(out=m1[:, br, :], in0=g[:, br, :], in1=zslice(db, br), op=ALU.mult),
                                   reads=B_db + [B_g], writes=[B_m1], partial=True)
                            op("dve", lambda m1=m1: nc.vector.tensor_tensor(out=m1[:, 0, :], in0=m1[:, 0, :], in1=m1[:, 1, :], op=ALU.add), reads=[B_m1], writes=[B_m1], partial=True)
                            op("dve", lambda m1=m1, m=m: nc.vector.tensor_tensor(out=mT[:, m, :], in0=m1[:, 0, :], in1=m1[:, 2, :], op=ALU.add),
                               reads=[B_m1], writes=[B_mT], partial=True)
                        for t in range(ntl):
                            for oh in range(2):
                                px, B_px = banks[xr.next()]

                                def mmx(px=px, t=t, oh=oh):
                                    ins = None
                                    for k in range(8):
                                        ins = nc.tensor.matmul(px[:, :], lhsT=mT[:, k, t * 128:(t + 1) * 128], rhs=wo[:, k, oh * 512:(oh + 1) * 512], start=(k == 0), stop=(k == 7))
                                    return ins
                                op("pe", mmx, reads=[B_mT, B_wo], writes=[B_px])
                                op("dve", lambda px=px, t=t, oh=oh: nc.vector.tensor_tensor(out=xc[:, t, oh * 512:(oh + 1) * 512], in0=px[:, :], in1=xc[:, t, oh * 512:(oh + 1) * 512], op=ALU.add),
                                   reads=[B_px, B_xc], writes=[B_xc], partial=True)
                        dma(xchunk(c, CW), xc[:, :, :], reads=[B_xc], writes=[B_x[c0 // 512]])
                    fw.sync_all()


            if "ffn1" in cfg.phases:
                phase_ffn(0)
            hT_stack = ExitStack()
            hTf = sbt(hT_stack, "hTf", [128, 8, S], BF16)
            B_hTf = fw.buf("hTf")
            if "hT" in cfg.phases:
                phase_hT()
            if "A" in cfg.phases:
                phase_A()
            if "B" in cfg.phases:
                phase_B()
            if "C" in cfg.phases:
                phase_C()
            if "merge" in cfg.phases:
                phase_merge()
            hT_stack.close()
            if "ffn2" in cfg.phases:
                phase_ffn(1)

        with nc.Fori(0, L) as l:
            nc.sync.dma_start(out=wcur[:, :], in_=wblob[bass.ds(l, 1), 0:PIECE_ROWS, :].rearrange("o r c -> (o r) c")).then_inc(s_wc, 16)
            nc.sync.wait_ge(s_wc, 16)
            with ExitStack() as cv:
                HALF = PIECE_ROWS // 2
                RPP = HALF // 128
                NEL = RPP * BLOB_COLS
                st32 = [sbt(cv, "cv32_%d" % i, [128, NEL], F32) for i in range(2)]
                st16 = [sbt(cv, "cv16_%d" % i, [128, NEL], BF16) for i in range(2)]
                B32 = [fw.buf("cv32_0"), fw.buf("cv32_1")]
                B16 = [fw.buf("cv16_0"), fw.buf("cv16_1")]
                with nc.Fori(0, NPIECE) as pi:
                    for hf in range(2):
                        r0 = pi * PIECE_ROWS + hf * HALF
                        dma(st32[hf][:, :], wblob[bass.ds(l, 1), bass.ds(r0, HALF), :].rearrange("o (p r) c -> p (o r c)", p=128), writes=[B32[hf]])
                    for hf in range(2):
                        NCK = 8
                        CK = NEL // NCK
                        for ck in range(NCK):
                            ek = ("act", "dve", "pool", "act", "dve", "act", "dve", "pool")[ck]
                            if ek == "act":
                                op("act", lambda: nc.scalar.copy(out=st16[hf][:, ck * CK:(ck + 1) * CK], in_=st32[hf][:, ck * CK:(ck + 1) * CK]),
                                   reads=[B32[hf]], writes=[B16[hf]], partial=True)
                            elif ek == "dve":
                                op("dve", lambda: nc.vector.tensor_copy(out=st16[hf][:, ck * CK:(ck + 1) * CK], in_=st32[hf][:, ck * CK:(ck + 1) * CK]),
                                   reads=[B32[hf]], writes=[B16[hf]], partial=True)
                            else:
                                op("pool", lambda: nc.gpsimd.tensor_copy(out=st16[hf][:, ck * CK:(ck + 1) * CK], in_=st32[hf][:, ck * CK:(ck + 1) * CK]),
                                   reads=[B32[hf]], writes=[B16[hf]], partial=True)
                        r0 = pi * PIECE_ROWS + hf * HALF
                        dma(wcur16[bass.ds(r0, HALF), :].rearrange("(p r) c -> p (r c)", p=128), st16[hf][:, :], reads=[B16[hf]])
                    fw.hard_barrier()
            for s_ in range(NSEQ):
                body(l, s_)
                if s_ < NSEQ - 1:
                    fw.sync_all()
                else:
                    fw.hard_barrier()
    return nc, fw


def kernel(**inputs):
    cfg = Cfg()
    return run_kernel(cfg, inputs)


def run_kernel(cfg, inputs, n_cores=8, trace=False):
    nc, fw = build_program(cfg)
    x = np.ascontiguousarray(inputs["x"], dtype=np.float32)
    lay, brows = blob_layout()
    blob = np.zeros((cfg.L, brows * BLOB_COLS), dtype=np.float32)
    for name, shp in W_SHAPES:
        off = lay[name][0]
        n = int(np.prod(shp))
        if name == "lam0":
            for l in range(cfg.L):
                blob[l, off] = 0.8 - 0.6 * math.exp(-0.3 * l)
                blob[l, off + 1] = 1.0 - (0.8 - 0.6 * math.exp(-0.3 * l))
        elif name == "aq_aug":
            j = np.arange(512)
            t = np.zeros((4, A_HEADS, 512), dtype=np.float32)
            for h in range(A_HEADS):
                t[0, h] = -SLOPES_A[h] * (j % 256)
                t[1, h] = -SLOPES_A[h] * 256.0 * (j // 256)
                t[2, h] = 1.0
            blob[:, off:off + n] = t.reshape(1, n)
        elif name == "ak_aug":
            i = np.arange(128)
            t = np.zeros((4, A_HEADS, 2, 128), dtype=np.float32)
            for h in range(A_HEADS):
                t[0, h, 0] = 1.0
                t[1, h, 0] = 1.0
                t[2, h, 0] = SLOPES_A[h] * i
                t[0, h, 1] = -1.0
                t[1, h, 1] = -1.0
                t[2, h, 1] = -SLOPES_A[h] * i
            blob[:, off:off + n] = t.reshape(1, n)
        else:
            blob[:, off:off + n] = np.asarray(inputs[name], dtype=np.float32).reshape(cfg.L, n)
    blob = blob.reshape(cfg.L, brows, BLOB_COLS)
    in_maps = []
    for c in range(n_cores):
        in_maps.append({"x": x[c * cfg.NSEQ:(c + 1) * cfg.NSEQ], "wblob": blob})
    res = run_bass_kernel_spmd(nc, in_maps, core_ids=list(range(n_cores)), **({"trace": True} if trace else {}))
    if trace:
        print("EXEC_NS", res.exec_time_ns, "n_instr", fw.n_instr)
    outs = np.concatenate([r["out"] for r in res.results], axis=0)
    if cfg.debug:
        return outs, res.results
    return outs
```
